# Optimizing a Trainium2 kernel written in Bass

```python
import jax, jax.numpy as jnp
from jax import lax
import numpy as np

D_MODEL = 1024
BATCH = 8
SEQ = 4096
DEPTH = 1
DEC_BATCH = 16
DEC_SEQ = 64
PAST_LEN = 2048

CHUNK = 64
MIX_WIDTH = D_MODEL
HG_WIDTH = MIX_WIDTH // 2
RG_WIDTH = MIX_WIDTH - HG_WIDTH
HG_HEAD_DIM = 128
HG_HEADS = HG_WIDTH // HG_HEAD_DIM
RG_BLOCKS = 8
RG_BLOCK_DIM = RG_WIDTH // RG_BLOCKS
CONV_WIDTH = 4
RG_C = 8.0
D_FF = 4 * D_MODEL
N_MOD = 6
EPS = 1e-6
IN_WIDTH = 4 * HG_WIDTH + 2 * RG_WIDTH

kernel_name = "hymba_hgrn2_rglru_streaming_step"


def _rms(x):
    xf = x.astype(jnp.float32)
    return (xf * lax.rsqrt(jnp.mean(xf * xf, axis=-1, keepdims=True) + EPS)).astype(x.dtype)


def _hgrn2_chunk(S0, q, logf, k, v):
    L = q.shape[2]
    b = jnp.cumsum(logf, axis=2)
    mask = jnp.tril(jnp.ones((L, L), dtype=bool))
    diff = b[:, :, :, None, :] - b[:, :, None, :, :]
    decay = jnp.exp(jnp.where(mask[:, :, None], diff, -jnp.inf))
    scores = jnp.einsum('bhtd,bhsd,bhtsd->bhts', q, k, decay)
    o = (jnp.einsum('bhts,bhsv->bhtv', scores, v)
         + jnp.einsum('bhtd,bhdv->bhtv', q * jnp.exp(b), S0))
    b_last = b[:, :, -1:, :]
    S = (jnp.exp(b_last[:, :, 0, :])[..., None] * S0
         + jnp.einsum('bhsd,bhsv->bhdv', k * jnp.exp(b_last - b), v))
    return S, o


def _hgrn2_sequence(S0, q, logf, k, v):
    Bn, H, L, DK = q.shape
    if L <= CHUNK:
        return _hgrn2_chunk(S0, q, logf, k, v)
    n = L // CHUNK

    def to_chunks(t):
        return jnp.moveaxis(t.reshape(Bn, H, n, CHUNK, t.shape[-1]), 2, 0)

    def step(S, inp):
        return _hgrn2_chunk(S, *inp)

    S, o = lax.scan(step, S0, (to_chunks(q), to_chunks(logf), to_chunks(k), to_chunks(v)))
    o = jnp.moveaxis(o, 0, 2).reshape(Bn, H, L, v.shape[-1])
    return S, o


def _lin_combine(left, right):
    a1, b1 = left
    a2, b2 = right
    return a1 * a2, a2 * b1 + b2


def _layer(x, c, S0, h0, conv_buf, lb, w_ada, b_ada, w_in, hg_gain, conv_w, conv_b,
           rg_wa, rg_ba, rg_wx, rg_bx, rg_lam, w_out, w_up, w_down):
    f32 = jnp.float32
    Bn, L, _ = x.shape
    mod = jax.nn.silu(c) @ w_ada + b_ada
    sh1, sc1, g1, sh2, sc2, g2 = jnp.split(mod[:, None, :], N_MOD, axis=-1)

    hn = _rms(x) * (1 + sc1) + sh1
    proj = hn @ w_in
    q, f, iv, og, xr, gr = jnp.split(
        proj, [HG_WIDTH, 2 * HG_WIDTH, 3 * HG_WIDTH, 4 * HG_WIDTH, 4 * HG_WIDTH + RG_WIDTH], axis=-1)

    def heads(t):
        return t.reshape(Bn, L, HG_HEADS, HG_HEAD_DIM).transpose(0, 2, 1, 3).astype(f32)

    fgate = lb + (1.0 - lb) * jax.nn.sigmoid(f.astype(f32))
    S_new, o = _hgrn2_sequence(S0.astype(f32), heads(jax.nn.silu(q)), heads(jnp.log(fgate)),
                               heads(1.0 - fgate), heads(iv))
    o = o.transpose(0, 2, 1, 3)
    o = o * lax.rsqrt(jnp.mean(o * o, axis=-1, keepdims=True) + EPS)
    o_hg = o.reshape(Bn, L, HG_WIDTH).astype(x.dtype) * hg_gain * jax.nn.silu(og)

    xpad = jnp.concatenate([conv_buf.astype(xr.dtype), xr], axis=1)
    xc = conv_b + xpad[:, 0:L] * conv_w[0]
    for j in range(1, CONV_WIDTH):
        xc = xc + xpad[:, j:j + L] * conv_w[j]
    new_buf = xpad[:, -(CONV_WIDTH - 1):]
    xb = xc.reshape(Bn, L, RG_BLOCKS, RG_BLOCK_DIM)
    r = jax.nn.sigmoid(jnp.einsum('blnc,ncd->blnd', xb, rg_wa).reshape(Bn, L, RG_WIDTH) + rg_ba)
    ig = jax.nn.sigmoid(jnp.einsum('blnc,ncd->blnd', xb, rg_wx).reshape(Bn, L, RG_WIDTH) + rg_bx)
    log_a = -RG_C * r.astype(f32) * jax.nn.softplus(-rg_lam.astype(f32))
    a = jnp.exp(log_a)
    u = jnp.sqrt(-jnp.expm1(2.0 * log_a)) * (ig * xc).astype(f32)
    A, Bc = lax.associative_scan(_lin_combine, (a, u), axis=1)
    hseq = A * h0.astype(f32)[:, None, :] + Bc
    h_new = hseq[:, -1]
    o_rg = hseq.astype(x.dtype) * jax.nn.gelu(gr)

    y = jnp.concatenate([o_hg, o_rg], axis=-1) @ w_out
    x = x + g1 * y

    hn2 = _rms(x) * (1 + sc2) + sh2
    x = x + g2 * (jnp.square(jax.nn.relu(hn2 @ w_up)) @ w_down)
    return x, S_new.astype(S0.dtype), h_new.astype(h0.dtype), new_buf.astype(conv_buf.dtype)


def _trunk(x, c, S0s, h0s, bufs, hg_lb_logits, w_ada, b_ada, w_in, hg_norm_gain, conv_w, conv_b,
           rg_wa, rg_ba, rg_wx, rg_bx, rg_lambda, w_out, w_up, w_down, final_gain):
    lbs = jnp.cumsum(jax.nn.softmax(hg_lb_logits.astype(jnp.float32), axis=0), axis=0)
    Ss, hs, cbs = [], [], []
    for l in range(DEPTH):
        x, S, h, cb = _layer(x, c, S0s[l], h0s[l], bufs[l], lbs[l], w_ada[l], b_ada[l], w_in[l],
                             hg_norm_gain[l], conv_w[l], conv_b[l], rg_wa[l], rg_ba[l], rg_wx[l],
                             rg_bx[l], rg_lambda[l], w_out[l], w_up[l], w_down[l])
        Ss.append(S)
        hs.append(h)
        cbs.append(cb)
    y = _rms(x) * final_gain
    return y, jnp.stack(Ss), jnp.stack(hs), jnp.stack(cbs)


def setup_inputs(seed: int = 0) -> dict:
    key = jax.random.key(seed)
    ks = jax.random.split(key, 32)
    nrm = lambda k, shape, s: jax.random.normal(k, shape, jnp.float32) * s
    a_c = jax.random.uniform(ks[0], (DEPTH, RG_WIDTH), jnp.float32, 0.9, 0.999)
    a0 = a_c ** (1.0 / RG_C)
    rg_lambda = jnp.log(a0) - jnp.log1p(-a0)
    return {
        "x_prompt": nrm(ks[1], (BATCH, SEQ, D_MODEL), 1.0),
        "x_sample": nrm(ks[2], (DEC_BATCH, DEC_SEQ, D_MODEL), 1.0),
        "c_prompt": nrm(ks[3], (BATCH, D_MODEL), 1.0),
        "c_sample": nrm(ks[4], (DEC_BATCH, D_MODEL), 1.0),
        "state_hgrn": nrm(ks[5], (DEPTH, DEC_BATCH, HG_HEADS, HG_HEAD_DIM, HG_HEAD_DIM), 0.3),
        "state_rglru": nrm(ks[6], (DEPTH, DEC_BATCH, RG_WIDTH), 0.5),
        "cache_conv": nrm(ks[7], (DEPTH, DEC_BATCH, CONV_WIDTH - 1, RG_WIDTH), 1.0),
        "hg_lb_logits": nrm(ks[8], (DEPTH + 1, HG_WIDTH), 0.5),
        "w_ada": nrm(ks[9], (DEPTH, D_MODEL, N_MOD * D_MODEL), D_MODEL ** -0.5),
        "b_ada": nrm(ks[10], (DEPTH, N_MOD * D_MODEL), 0.02),
        "w_in": nrm(ks[11], (DEPTH, D_MODEL, IN_WIDTH), D_MODEL ** -0.5),
        "hg_norm_gain": 1.0 + nrm(ks[12], (DEPTH, HG_WIDTH), 0.02),
        "conv_w": nrm(ks[13], (DEPTH, CONV_WIDTH, RG_WIDTH), CONV_WIDTH ** -0.5),
        "conv_b": nrm(ks[14], (DEPTH, RG_WIDTH), 0.02),
        "rg_wa": nrm(ks[15], (DEPTH, RG_BLOCKS, RG_BLOCK_DIM, RG_BLOCK_DIM), RG_BLOCK_DIM ** -0.5),
        "rg_ba": nrm(ks[16], (DEPTH, RG_WIDTH), 0.02),
        "rg_wx": nrm(ks[17], (DEPTH, RG_BLOCKS, RG_BLOCK_DIM, RG_BLOCK_DIM), RG_BLOCK_DIM ** -0.5),
        "rg_bx": nrm(ks[18], (DEPTH, RG_WIDTH), 0.02),
        "rg_lambda": rg_lambda,
        "w_out": nrm(ks[19], (DEPTH, MIX_WIDTH, D_MODEL), MIX_WIDTH ** -0.5),
        "w_up": nrm(ks[20], (DEPTH, D_MODEL, D_FF), D_MODEL ** -0.5),
        "w_down": nrm(ks[21], (DEPTH, D_FF, D_MODEL), D_FF ** -0.5),
        "final_gain": 1.0 + nrm(ks[22], (D_MODEL,), 0.02),
    }


def reference(x_prompt, x_sample, c_prompt, c_sample, state_hgrn, state_rglru, cache_conv,
              hg_lb_logits, w_ada, b_ada, w_in, hg_norm_gain, conv_w, conv_b,
              rg_wa, rg_ba, rg_wx, rg_bx, rg_lambda, w_out, w_up, w_down, final_gain):
    Bp = x_prompt.shape[0]
    S0p = jnp.zeros((DEPTH, Bp, HG_HEADS, HG_HEAD_DIM, HG_HEAD_DIM), state_hgrn.dtype)
    h0p = jnp.zeros((DEPTH, Bp, RG_WIDTH), state_rglru.dtype)
    cbp = jnp.zeros((DEPTH, Bp, CONV_WIDTH - 1, RG_WIDTH), cache_conv.dtype)
    y_prompt, S_p, h_p, cb_p = _trunk(x_prompt, c_prompt, S0p, h0p, cbp, hg_lb_logits, w_ada, b_ada,
                                      w_in, hg_norm_gain, conv_w, conv_b, rg_wa, rg_ba, rg_wx, rg_bx,
                                      rg_lambda, w_out, w_up, w_down, final_gain)
    y_sample, S_s, h_s, cb_s = _trunk(x_sample, c_sample, state_hgrn, state_rglru, cache_conv,
                                      hg_lb_logits, w_ada, b_ada, w_in, hg_norm_gain, conv_w, conv_b,
                                      rg_wa, rg_ba, rg_wx, rg_bx, rg_lambda, w_out, w_up, w_down,
                                      final_gain)
    return (y_prompt, y_sample, S_p, h_p, cb_p, S_s, h_s, cb_s)
```

```python
import numpy as np
from contextlib import ExitStack
import concourse.bass as bass
import concourse.mybir as mybir
from concourse.bass_utils import run_bass_kernel_spmd

F32 = mybir.dt.float32
BF16 = mybir.dt.bfloat16
ALU = mybir.AluOpType
AF = mybir.ActivationFunctionType

ENGS = ("pe", "act", "dve", "pool", "sp")


class Res:
    __slots__ = ("name", "w", "r")

    def __init__(self, name=""):
        self.name = name
        self.w = None
        self.r = []


class _Op:
    __slots__ = ("eng", "fn", "deps", "signal", "sigval", "dsem", "dval", "cost", "tset", "idx", "order_deps",
                 "npred", "succ", "dr", "fin", "start", "dcost", "desc", "bind", "nbytes", "tail", "psum")

    def __init__(self, eng, fn):
        self.eng = eng
        self.fn = fn
        self.deps = set()
        self.signal = False
        self.sigval = 0
        self.dsem = None
        self.dval = 0
        self.cost = 0.3
        self.tset = None
        self.idx = 0
        self.order_deps = []
        self.dcost = 0.0


import os as _os0
ATTACH_WAIT = _os0.environ.get("K_ATTACH", "1") == "1"
SEM_LAT = float(_os0.environ.get("K_SEMLAT", "0.15"))
TSWITCH = float(_os0.environ.get("K_TSW", "2.0"))


class Sched:
    def __init__(self, nc):
        self.nc = nc
        self.ops = {e: [] for e in ENGS}
        self.all_ops = []
        self.dma_tot = {}
        self.dma_last = {}
        self.dma_sems = {}
        self.stage = ""
        self.filler = None

    def _collect(self, op, reads, writes, after=()):
        eng = op.eng
        skip_self = eng in ("pe", "sp")

        def add(t):
            if t is None:
                return
            if skip_self and t[0] == "op" and t[1].eng == eng:
                op.order_deps.append(t[1])
                return
            op.deps.add(t)

        for r in reads:
            add(r.w)
        for w in after:
            add(w.w)
            for t in w.r:
                add(t)
        for w in writes:
            add(w.w)
            for t in w.r:
                add(t)

    def op(self, eng, fn, reads=(), writes=(), cost=0.3, tset=None, after=()):
        o = _Op(eng, fn)
        o.cost = cost
        o.tset = tset
        self._collect(o, reads, writes, after)
        o.idx = len(self.all_ops)
        o.desc = self.stage
        o.psum = (eng != "pe") and any(w.name.startswith("PB") for w in writes)
        self.all_ops.append(o)
        self.ops[eng].append(o)
        tok = ("op", o)
        for r in reads:
            r.r.append(tok)
        for w in writes:
            w.w = tok
            w.r = []
        return tok

    def dma(self, eng, fn, sem, reads=(), writes=(), after=(), cost=3.0, nbytes=0.0, after_tok=()):
        o = _Op(eng, fn)
        o.cost = 1.06 if eng == "pool" else 0.06
        o.dcost = cost
        o.nbytes = nbytes
        self._collect(o, reads, writes, after)
        for t in after_tok:
            o.deps.add(t)
        o.idx = len(self.all_ops)
        o.desc = self.stage + ":dma:" + sem
        self.all_ops.append(o)
        self.ops[eng].append(o)
        tot = self.dma_tot[sem] + 16
        self.dma_tot[sem] = tot
        prev = self.dma_last.get(sem)
        if prev is not None:
            assert prev.eng == eng
            o.order_deps.append(prev)
        self.dma_last[sem] = o
        o.dsem = sem
        o.dval = tot
        tok = ("dma", o, sem, tot)
        for r in reads:
            r.r.append(tok)
        for w in writes:
            w.w = tok
            w.r = []
        return tok

    def dsem(self, name):
        if name not in self.dma_tot:
            self.dma_tot[name] = 0
        return name

    def schedule(self):
        import heapq
        ops = self.all_ops
        for o in ops:
            o.succ = []
            o.fin = None
            o.start = None
        for o in ops:
            preds = set(t[1] for t in o.deps) | set(o.order_deps)
            o.npred = len(preds)
            for p in preds:
                p.succ.append(o)
        PRIO = _os0.environ.get("K_PRIO", "1") == "1"
        for o in reversed(ops):
            t = 0.0
            for sc in o.succ:
                v = sc.tail + SEM_LAT
                if v > t:
                    t = v
            o.tail = t + o.cost + (o.dcost if o.dsem is not None else 0.0)
        fut = {e: [] for e in ENGS}
        avail = {e: [] for e in ENGS}

        def data_ready(o):
            dr = 0.0
            for t in o.deps:
                p = t[1]
                f = p.fin + SEM_LAT
                if f > dr:
                    dr = f
            for p in o.order_deps:
                if p.start > dr:
                    dr = p.start
            return dr

        PBONUS = float(_os0.environ.get("K_PBONUS", "2.0"))

        def prio(o):
            if not PRIO:
                return o.idx
            if PBONUS and getattr(o, "psum", False):
                return -o.tail - PBONUS
            return -o.tail

        for o in ops:
            if o.npred == 0:
                o.dr = 0.0
                heapq.heappush(fut[o.eng], (0.0, o.idx, o))
        free = {e: 0.0 for e in ENGS}
        dma_free = [0.0]
        DMA_BW = 230e3
        cur_set = [None]
        new_order = {e: [] for e in ENGS}
        n_done = 0
        total = len(ops)
        EPS = float(_os0.environ.get("K_EPS", "0.02"))
        while n_done < total:
            best_e = None
            best_t = None
            for e in ENGS:
                if avail[e]:
                    t_e = free[e]
                    if fut[e] and fut[e][0][0] < t_e:
                        pass
                elif fut[e]:
                    t_e = max(free[e], fut[e][0][0])
                else:
                    continue
                if best_t is None or t_e < best_t:
                    best_t = t_e
                    best_e = e
            e = best_e
            t_e = best_t
            f = fut[e]
            while f and f[0][0] <= t_e + EPS:
                dr, idx, o = heapq.heappop(f)
                heapq.heappush(avail[e], (prio(o), idx, o))
            a = avail[e]
            if e == "act" and cur_set[0] is not None:
                cands = heapq.nsmallest(8, a)
                pick = cands[0]
                if pick[2].tset is not None and pick[2].tset != cur_set[0]:
                    for c in cands[1:]:
                        if (c[2].tset is None or c[2].tset == cur_set[0]) and c[0] - pick[0] < float(_os0.environ.get("K_SWTH", "4.0")):
                            pick = c
                            break
                if pick is a[0]:
                    heapq.heappop(a)
                else:
                    a.remove(pick)
                    heapq.heapify(a)
                o = pick[2]
            else:
                o = heapq.heappop(a)[2]
            est = max(free[e], o.dr)
            if e == "act" and o.tset is not None:
                if cur_set[0] is not None and o.tset != cur_set[0]:
                    est += TSWITCH
                cur_set[0] = o.tset
            o.start = est
            o.bind = None
            free[e] = est + o.cost
            if o.dsem is not None:
                xs_ = max(est + o.cost + 0.8, dma_free[0])
                dma_free[0] = xs_ + o.nbytes / DMA_BW
                o.fin = dma_free[0] + 1.2
            else:
                o.fin = est + o.cost
            new_order[e].append(o)
            n_done += 1
            for sc in o.succ:
                sc.npred -= 1
                if sc.npred == 0:
                    sc.dr = data_ready(sc)
                    heapq.heappush(fut[sc.eng], (sc.dr, sc.idx, sc))
        if self.filler is not None:
            fn, ftok, fcost, t_lo, t_hi, gmin = self.filler
            pe = []
            prev_end = 0.0
            nfill = 0
            for o in new_order["pe"]:
                gap = o.start - prev_end
                if t_lo < o.start < t_hi and gap >= gmin:
                    n = int((gap - 0.35) / fcost)
                    for _ in range(max(0, n)):
                        f = _Op("pe", fn)
                        f.cost = fcost
                        f.deps = set([ftok])
                        f.desc = "filler"
                        f.start = prev_end
                        f.fin = prev_end + fcost
                        pe.append(f)
                        nfill += 1
                pe.append(o)
                prev_end = o.start + o.cost
            new_order["pe"] = pe
            self.nfill = nfill
        self.ops = new_order
        self.est_time = max(o.fin for o in ops)
        self.est_busy = {e: sum(o.cost for o in new_order[e]) for e in ENGS}

    def emit(self, stack, final_tokens=()):
        nc = self.nc
        esem = {e: stack.enter_context(nc.semaphore("es_" + e)) for e in ENGS}
        for name in self.dma_tot:
            self.dma_sems[name] = stack.enter_context(nc.semaphore("ds_" + name))
        for e in ENGS:
            for o in self.ops[e]:
                for t in o.deps:
                    if t[0] == "op":
                        t[1].signal = True
        for t in final_tokens:
            if t[0] == "op":
                t[1].signal = True
        for e in ENGS:
            c = 0
            for o in self.ops[e]:
                if o.signal:
                    c += 1
                    o.sigval = c

        def tokkey(t):
            if t[0] == "op":
                return ("e", t[1].eng), t[1].sigval
            return ("d", t[2]), t[3]

        PRUNE = _os0.environ.get("K_PRUNE", "1") == "1"
        know = {}
        if PRUNE:
            allops = []
            for e in ENGS:
                allops.extend(self.ops[e])
            allops.sort(key=lambda o: o.start)
            last_on = {}
            for o in allops:
                k = dict(last_on.get(o.eng, {}))
                for t in o.deps:
                    key, val = tokkey(t)
                    if val > k.get(key, 0):
                        k[key] = val
                    kd = know.get(id(t[1]))
                    if kd:
                        for kk_, vv_ in kd.items():
                            if vv_ > k.get(kk_, 0):
                                k[kk_] = vv_
                last_on[o.eng] = k
                k2 = k
                if o.dsem is not None:
                    k2 = dict(k)
                    k2[("d", o.dsem)] = max(k2.get(("d", o.dsem), 0), o.dval)
                elif o.signal:
                    k2 = dict(k)
                    k2[("e", o.eng)] = max(k2.get(("e", o.eng), 0), o.sigval)
                know[id(o)] = k2

        def run_engine(e, engobj, extra_final=None):
            waited = {}
            for o in self.ops[e]:
                need = {}
                needop = {}
                for t in o.deps:
                    key, val = tokkey(t)
                    if val > need.get(key, 0):
                        need[key] = val
                        needop[key] = t[1]
                todo = []
                items = sorted(need.items(), key=lambda kv: -needop[kv[0]].start) if PRUNE else list(need.items())
                for key, val in items:
                    if waited.get(key, 0) >= val:
                        continue
                    waited[key] = val
                    if PRUNE:
                        kd = know.get(id(needop[key]))
                        if kd:
                            for kk_, vv_ in kd.items():
                                if vv_ > waited.get(kk_, 0):
                                    waited[kk_] = vv_
                    s = esem[key[1]] if key[0] == "e" else self.dma_sems[key[1]]
                    todo.append((s, val))
                attach = None
                if todo and o.dsem is None and ATTACH_WAIT and e != "pe":
                    attach = todo.pop()
                for s, val in todo:
                    engobj.wait_ge(s, val)
                inst = o.fn(engobj)
                if attach is not None:
                    inst._wait_ge(attach[0], attach[1])
                if o.dsem is not None:
                    inst.then_inc(self.dma_sems[o.dsem], 16)
                elif o.signal:
                    inst.then_inc(esem[e], 1)
            if extra_final:
                need = {}
                for t in extra_final:
                    key, val = tokkey(t)
                    if val > need.get(key, 0):
                        need[key] = val
                for key, val in need.items():
                    s = esem[key[1]] if key[0] == "e" else self.dma_sems[key[1]]
                    engobj.wait_ge(s, val)

        with nc.Block() as block:
            @block.tensor
            def _(eng):
                run_engine("pe", eng)

            @block.scalar
            def _(eng):
                run_engine("act", eng)

            @block.vector
            def _(eng):
                run_engine("dve", eng)

            @block.gpsimd
            def _(eng):
                run_engine("pool", eng)

            @block.sync
            def _(eng):
                run_engine("sp", eng, extra_final=final_tokens)


D = 1024
KC = 8
SEQ = 4096
NTOK = SEQ + 128
GN = 256
NPG = SEQ // GN
EPS = 1e-6
NPP = 36
import os as _os
NXT = int(_os.environ.get("K_NXT", "3"))
DBL = set(x for x in _os.environ.get("K_DBL", "").split(",") if x)
EXPLORE = _os.environ.get("K_EXPLORE", "") == "1"


def build_nc(n_prompt_groups=NPG, with_sample=True):
    nc = bass.Bass("TRN2", target_bir_lowering=False)
    S = Sched(nc)

    def din(name, shape):
        return nc.dram_tensor(name, list(shape), F32, kind="ExternalInput").ap()

    def dout(name, shape):
        return nc.dram_tensor(name, list(shape), F32, kind="ExternalOutput").ap()

    xs = din("xs", [NTOK, D])
    cT_d = din("cT", [128, KC, 3])
    s_hg_d = din("s_hg", [2, 4, 128, 128])
    h0_d = din("h0", [128, 2, 4])
    cc0_d = din("cc0", [128, 2, 4, 3])
    lbl_d = din("lbl", [128, 2, 4])
    w_ada_d = din("w_ada", [D, 6 * D])
    b_ada_d = din("b_ada", [6 * D])
    b48_d = din("b48", [128, 48])
    w_in_d = din("w_in", [D, 3 * D])
    pp_d = din("pp", [128, NPP])
    rg_wa_d = din("rg_wa", [8, 64, 64])
    rg_wx_d = din("rg_wx", [8, 64, 64])
    w_out_d = din("w_out", [D, D])
    w_up_d = din("w_up", [D, 4 * D])
    w_down_d = din("w_down", [4 * D, D])
    fgain_d = din("fgain", [D])

    y_d = dout("y", [NTOK, D])
    S_out_d = dout("S_out", [3, 4, 128, 128])
    h_out_d = dout("h_out", [128, 3, 4])
    cb_out_d = dout("cb_out", [128, 3, 4, 3])
    x1_d = nc.dram_tensor("x1_scratch", [NTOK, D], F32).ap()

    groups = []
    for g in range(n_prompt_groups):
        groups.append(dict(tok0=g * GN, N=GN, segs=[(0, 0, GN)], first=(g == 0), last=(g == n_prompt_groups - 1)))
    if with_sample:
        groups.append(dict(tok0=SEQ, N=128, segs=[(1, 0, 64), (2, 64, 64)], first=True, last=True))
    NGRP = len(groups)

    final_tokens = []
    with ExitStack() as st:
        def sbt(name, shape, dt):
            return st.enter_context(nc.sbuf_tensor("sb_" + name, list(shape), dt))

        def pst(name, shape, dt):
            return st.enter_context(nc.psum_tensor("ps_" + name, list(shape), dt))

        arenaA = sbt("arenaA", [128, 32768], BF16)
        ARB_E = 46592
        arenaB = sbt("arenaB", [128, ARB_E], BF16)
        resA = []
        resB = []
        offB = [0]

        offB2 = [0]

        def allocB(nelem_bf16, name):
            if EXPLORE and "_b" in name:
                o = offB2[0]
                offB2[0] = o + nelem_bf16
                return arenaB[:, o:o + nelem_bf16]
            o = offB[0]
            offB[0] = o + nelem_bf16
            assert offB[0] <= ARB_E, (name, offB[0])
            return arenaB[:, o:o + nelem_bf16]

        def resB_new(name):
            r = Res(name)
            resB.append(r)
            return r

        wi = arenaA[:, 0:KC * 3072].rearrange("p (k n) -> p k n", k=KC)
        wo = arenaA[:, KC * 3072:KC * 4096].rearrange("p (k n) -> p k n", k=KC)
        wd = arenaA[:, 0:32 * 1024].rearrange("p (f n) -> p f n", f=32)
        r_wi = [Res("wi%d" % i) for i in range(6)]
        r_wo = [Res("wo%d" % i) for i in range(2)]
        resA.extend(r_wi + r_wo)

        wu = arenaB[:, 0:KC * 4096].rearrange("p (k n) -> p k n", k=KC)
        hT = arenaB[:, 32768:32768 + 32 * GN].rearrange("p (f n) -> p f n", f=32)
        RL = [arenaB[:, 40960 + i * 512:40960 + (i + 1) * 512] for i in range(3)]
        tmp2 = [arenaB[:, 42496 + i * 1024:42496 + (i + 1) * 1024].bitcast(F32) for i in range(2)]

        def f32v(n, name):
            return allocB(2 * n, name).bitcast(F32)

        def _h4f(nm):
            return lambda tag: (f32v(4 * GN, nm + tag).rearrange("p (h n) -> p h n", h=4), [resB_new("%s%s%d" % (nm, tag, h)) for h in range(4)])

        def _h4b(nm):
            return lambda tag: (allocB(4 * GN, nm + tag).rearrange("p (h n) -> p h n", h=4), [resB_new("%s%s%d" % (nm, tag, h)) for h in range(4)])

        CTOR = {}
        for nm in ("TQ", "TF", "LF", "BB", "RA", "RB", "RC", "RD", "RE", "TO"):
            CTOR[nm] = _h4f(nm)
        for nm in ("QT", "KT", "OSQ", "XCB"):
            CTOR[nm] = _h4b(nm)
        CTOR["vTM"] = lambda tag: (allocB(2 * 512, "vTM" + tag).rearrange("p (b n) -> p b n", b=2), [resB_new("vTM%s%d" % (tag, b)) for b in range(2)])
        CTOR["kTT"] = lambda tag: (allocB(4 * 2 * 128, "kTT" + tag).rearrange("p (h b n) -> p h b n", h=4, b=2), [resB_new("kTT%s%d" % (tag, h)) for h in range(4)])
        CTOR["mixT"] = lambda tag: (allocB(8 * GN, "mixT" + tag).rearrange("p (k n) -> p k n", k=8), [resB_new("mix%s%d" % (tag, k)) for k in range(8)])
        CTOR["S0m"] = lambda tag: (allocB(4 * 4 * 128, "S0m" + tag).rearrange("p (j h n) -> p j h n", j=4, h=4),
                                   [[resB_new("S0m%s%d_%d" % (tag, j, h)) for h in range(4)] for j in range(4)])
        CTOR["scT"] = lambda tag: ([allocB(512, "scT%s%d" % (tag, i)).rearrange("p (h t) -> p h t", h=4) for i in range(2)],
                                   [resB_new("scT%s%d" % (tag, i)) for i in range(2)])
        FAM = {}
        for nm, ct in CTOR.items():
            a0 = ct("")
            FAM[nm] = [a0, ct("_b") if nm in DBL else a0]
        XBW = 264
        XB = f32v(4 * XBW, "XB").rearrange("p (c n) -> p c n", c=4)
        Sst = f32v(3 * 4 * 128, "Sst").rearrange("p (s h n) -> p s h n", s=3, h=4)
        WAP = 256
        wada = [allocB(KC * WAP, "wada%d" % i).rearrange("p (k n) -> p k n", k=KC) for i in range(2)]
        g1bc = [f32v(1024, "g1bc%d" % i) for i in range(2)]
        tmp1 = f32v(512, "tmp1")

        r_XB = [resB_new("XB%d" % h) for h in range(4)]
        r_S = [[resB_new("S%d_%d" % (s, h)) for h in range(4)] for s in range(3)]
        r_wada = [resB_new("wada%d" % i) for i in range(2)]
        r_g1bc = [resB_new("g1bc%d" % i) for i in range(2)]
        r_tmp1 = resB_new("tmp1")

        XT = [sbt("xt%d" % i, [128, 2, D], F32) for i in range(min(NXT, 2 if EXPLORE else NXT))]
        while len(XT) < NXT:
            XT.append(XT[0])
        r_XT = [Res("xt%d" % i) for i in range(NXT)]
        xn = sbt("xn", [128, 2, D], BF16)
        r_xn = [Res("xn0"), Res("xn1")]
        hnT = sbt("hnT", [128, KC, GN], BF16)
        r_hnT = [Res("hnT%d" % k) for k in range(KC)]
        g2bc = [sbt("g2bc%d" % i, [128, D], F32) for i in range(2)]
        r_g2bc = [Res("g2bc0"), Res("g2bc1")]
        fgb = sbt("fgb", [128, D], F32)
        r_fgb = Res("fgb")
        ident = sbt("ident", [128, 128], BF16)
        mask2 = sbt("mask2", [128, 128], BF16)
        ones_bf = sbt("ones_bf", [128, 128], BF16)
        RM = sbt("RM", [128, 4 * GN], BF16)
        r_const = Res("const")
        wbd = sbt("wbd", [128, 2, 4, 128], BF16)
        r_wbd = Res("wbd")
        pp = sbt("pp", [128, NPP], F32)
        r_pp = Res("pp")
        prm = sbt("prm", [128, 64], F32)
        r_prm = Res("prm")
        cT = sbt("cTs", [128, KC, 3], F32)
        cTt = sbt("cTt", [128, KC, 3], F32)
        scb = sbt("scb", [128, KC, 3], BF16)
        r_cT = Res("cT")
        modFM = sbt("modFM", [128, 4, KC, 3], F32)
        r_mod = [Res("mod%d" % i) for i in range(4)]
        b48 = sbt("b48", [128, 48], F32)
        r_b48 = Res("b48")
        grow = fgb[0:3, :]
        r_grow = r_fgb
        selP = sbt("selP", [3, 128], F32)
        selS = sbt("selS", [3, 128], F32)
        lbt = sbt("lbt", [128, 2, 4], F32)
        r_lbt = Res("lbt")
        stat = sbt("stat", [128, 16], F32)
        r_stat = [Res("stat%d" % i) for i in range(8)]
        nhalf = sbt("nhalf", [128, 1], F32)
        sm = sbt("sm", [128, 3, 16], F32)
        r_sm = Res("sm")
        hc = sbt("hc", [128, 3, 4], F32)
        r_hc = [Res("hc%d" % c) for c in range(4)]
        cbo = sbt("cbo", [128, 3, 4, 3], F32)
        r_cbo = Res("cbo")

        PB = [pst("pb%d" % i, [128, 512], F32) for i in range(7)]
        PB7 = pst("pb7", [128, 1024], BF16)
        r_PB = [Res("PB%d" % i) for i in range(8)]
        OSK = _os.environ.get("K_OSK", "1") == "1"
        A_BANKS = (0, 1, 2) if (_os.environ.get("K_FILL", "1") != "1" or OSK) else (0, 1)
        V_BANKS = (3, 4)
        BANKCFG = _os.environ.get("K_BANKS", "base")
        O_BANKS = (6,) if OSK else (5,)
        S_BANK, K_BANK, T_BANK = 6, 6, 7
        if BANKCFG == "o2":
            O_BANKS = (5, 6)
            S_BANK, K_BANK = 7, 7
            PB.append(PB7[:, :].bitcast(F32))
        rot = {"A": 0, "V": 0, "tmp": 0, "RL": 0}

        def nxt(k, n):
            v = rot[k]
            rot[k] = (v + 1) % n
            return v

        def bankA():
            return A_BANKS[nxt("A", len(A_BANKS))]

        def bankV():
            return V_BANKS[nxt("V", 2)]

        def nel(ap):
            n = 1
            for d in ap.shape[1:]:
                n *= d
            return n

        def mm(out, lhsT, rhs, start, stop, reads, writes, skip=False):
            if _os.environ.get("K_MMWARM", "1") == "1":
                c = max(0.1, nel(rhs) * 0.00042 + 0.003)
            else:
                c = max(0.035, nel(rhs) * 0.00052 + 0.012)
            if rhs.dtype == F32:
                c *= 4
            if skip:
                S.op("pe", lambda e: e.matmul(out, lhsT=lhsT, rhs=rhs, start=start, stop=stop, skip_group_check=True), reads, writes, cost=c)
            else:
                S.op("pe", lambda e: e.matmul(out, lhsT=lhsT, rhs=rhs, start=start, stop=stop), reads, writes, cost=c)

        def tr(out, in_, reads, writes):
            S.op("pe", lambda e: e.transpose(out, in_, ident[:]), list(reads) + [r_const], writes, cost=0.08)

        TSET = {AF.Tanh: "A", AF.Ln: "L"}

        def act(out, in_, func, reads, writes, scale=1.0, bias=0.0, accum=None, after=()):
            c = 0.25 + nel(out) / 1200.0
            ts_ = TSET.get(func)
            if accum is None:
                S.op("act", lambda e: e.activation(out=out, in_=in_, func=func, bias=bias, scale=scale), reads, writes, cost=c, tset=ts_, after=after)
            else:
                S.op("act", lambda e: e.activation(out=out, in_=in_, func=func, bias=bias, scale=scale, accum_out=accum), reads, writes, cost=c + 0.1, tset=ts_)

        def ecost(eng, n, kind):
            if eng == "pool":
                return 0.12 + n * (0.00105 if kind == "ts" else 0.0023)
            return 0.09 + n / 960.0

        def tsc(eng, out, in0, s1, s2, op0, op1, reads, writes):
            c = ecost(eng, nel(out), "ts")
            if s2 is None:
                S.op(eng, lambda e: e.tensor_scalar(out=out, in0=in0, scalar1=s1, scalar2=None, op0=op0), reads, writes, cost=c)
            else:
                S.op(eng, lambda e: e.tensor_scalar(out=out, in0=in0, scalar1=s1, scalar2=s2, op0=op0, op1=op1), reads, writes, cost=c)

        def stt(out, in0, scalar, in1, op0, op1, reads, writes):
            S.op("dve", lambda e: e.scalar_tensor_tensor(out=out, in0=in0, scalar=scalar, in1=in1, op0=op0, op1=op1), reads, writes,
                 cost=ecost("dve", nel(out), "tt"))

        def tt(eng, out, in0, in1, op, reads, writes):
            c = ecost(eng, nel(out), "tt")
            if op == ALU.pow:
                c = 0.3 + nel(out) * 0.166
            S.op(eng, lambda e: e.tensor_tensor(out=out, in0=in0, in1=in1, op=op), reads, writes, cost=c)

        def cp(eng, out, in_, reads, writes):
            S.op(eng, lambda e: e.tensor_copy(out=out, in_=in_), reads, writes, cost=ecost(eng, nel(out), "ts"))

        def mset(eng, ap, val, writes):
            S.op(eng, lambda e: e.memset(ap, val), (), writes, cost=0.1 + nel(ap) * 0.001)

        def scan(out, d0, d1, init, reads, writes):
            S.op("dve", lambda e: e.tensor_tensor_scan(out=out, data0=d0, data1=d1, initial=init, op0=ALU.mult, op1=ALU.add), reads, writes,
                 cost=0.1 + nel(out) / 960.0)

        dcount = [0]

        wq = []
        WQ_WIN = int(_os.environ.get("K_WQ", "5"))

        def dma(eng, out, in_, reads, writes, sem=None, slow=False, after=()):
            if sem is None:
                sem = "d%d" % dcount[0]
                dcount[0] += 1
            S.dsem(sem)
            nbytes = 4.0 * out.shape[0] * nel(out)
            c = 2.0 + nbytes / 150e3
            atok = ()
            big = (eng == "pool" and nbytes >= 500e3)
            if big and len(wq) >= WQ_WIN:
                atok = (wq[-WQ_WIN],)
            if slow:
                tok = S.dma(eng, lambda e: e.dma_start(out=out, in_=in_, allow_slow_non_contiguous=True), sem, reads, writes, after, cost=c,
                            nbytes=nbytes, after_tok=atok)
            else:
                tok = S.dma(eng, lambda e: e.dma_start(out=out, in_=in_), sem, reads, writes, after, cost=c, nbytes=nbytes, after_tok=atok)
            if big:
                wq.append(tok)
            return tok

        dma("sp", pp[:], pp_d, [], [r_pp])
        dma("sp", cT[:], cT_d, [], [r_cT])
        dma("sp", lbt[:], lbl_d, [], [r_lbt])
        dma("sp", b48[:], b48_d, [], [r_b48])
        xsems = ["xt%d" % i for i in range(NXT)]

        def load_x(gi, src):
            G = groups[gi]
            nb = G["N"] // 128
            buf = gi % NXT
            dma("sp", XT[buf][:, 0:nb, :], src[G["tok0"]:G["tok0"] + G["N"], :].rearrange("(b p) d -> p b d", p=128),
                [], [r_XT[buf]], sem=xsems[buf])

        load_x(0, xs)
        w_in_v = w_in_d.rearrange("(k p) n -> p k n", p=128)
        w_ada_v = w_ada_d.rearrange("(k p) n -> p k n", p=128)
        w_out_v = w_out_d.rearrange("(k p) n -> p k n", p=128)
        NPIECE = 6 * D // WAP
        ada_loaded = [0]

        STG = ["TQ", "TF", "LF", "BB", "RA", "RB", "RC", "RD"]

        def ada_buf(pi):
            if pi < 8:
                v, rl = FAM[STG[pi]][0]
                vb = v.rearrange("p h n -> p (h n)").bitcast(BF16).rearrange("p (k n) -> p k n", k=KC)
                return vb, list(rl), "wadas%d" % pi
            return wada[pi % 2], [r_wada[pi % 2]], "wada%d" % (pi % 2)

        def load_ada(pi):
            vb, rl, sem = ada_buf(pi)
            dma("pool", vb[:, :, :], w_ada_v[:, :, pi * WAP:(pi + 1) * WAP], [], rl, sem=sem)

        for pi in range(8):
            load_ada(pi)
        ada_loaded[0] = 8

        mset("pool", ident[:], 1.0, [r_const])
        S.op("pool", lambda e: e.affine_select(out=ident[:], in_=ident[:], pattern=[[-1, 128]], compare_op=ALU.is_equal,
                                               fill=0.0, base=0, channel_multiplier=1), [r_const], [r_const])
        mset("pool", mask2[:], 1.0, [r_const])
        S.op("pool", lambda e: e.affine_select(out=mask2[:], in_=mask2[:], pattern=[[1, 128]], compare_op=ALU.is_ge,
                                               fill=0.0, base=0, channel_multiplier=-1), [r_const], [r_const])
        mset("pool", mask2[0:64, 64:128], 0.0, [r_const])
        mset("pool", ones_bf[:], 1.0, [r_const])
        mset("pool", RM[:], 1.0, [r_const])
        mset("pool", RM[:].rearrange("p (c t) -> p c t", t=64)[:, :, 0:1], 0.0, [r_const])
        mset("pool", nhalf[:], -0.5, [r_const])
        mset("pool", selP[:], 1.0, [r_const])
        S.op("pool", lambda e: e.affine_select(out=selP[:], in_=selP[:], pattern=[[0, 128]], compare_op=ALU.is_ge,
                                               fill=0.0, base=0, channel_multiplier=-1), [r_const], [r_const])
        mset("pool", selS[:], 1.0, [r_const])
        S.op("pool", lambda e: e.affine_select(out=selS[:], in_=selS[:], pattern=[[1, 128]], compare_op=ALU.is_ge,
                                               fill=0.0, base=64, channel_multiplier=-64), [r_const], [r_const])
        S.op("pool", lambda e: e.affine_select(out=selS[:], in_=selS[:], pattern=[[-1, 128]], compare_op=ALU.is_ge,
                                               fill=0.0, base=-1, channel_multiplier=64), [r_const], [r_const])
        mset("pool", hc[:], 0.0, r_hc)
        mset("pool", XB[:], 0.0, r_XB)
        mset("pool", Sst[:, 0, :, :], 0.0, r_S[0])
        mset("pool", wbd[:], 0.0, [r_wbd])
        for gi_, src in enumerate((rg_wa_d, rg_wx_d)):
            v = src.rearrange("(c two) i o -> two i c o", two=2)
            dma("pool", wbd[0:64, gi_, :, 0:64], v[0], [], [r_wbd], sem="ld_wbd")
            dma("pool", wbd[64:128, gi_, :, 64:128], v[1], [], [r_wbd], sem="ld_wbd")
        if with_sample:
            dma("sp", hc[:, 1:3, :], h0_d, [], r_hc)
            dma("sp", Sst[:, 1:3, :, :], s_hg_d.rearrange("s h d v -> d s h v"), [], r_S[1] + r_S[2])
        for t in range(2):
            dma("sp", g1bc[t][:, :], b_ada_d[2 * D:3 * D].partition_broadcast(128), [], [r_g1bc[t]])
            dma("sp", g2bc[t][:, :], b_ada_d[5 * D:6 * D].partition_broadcast(128), [], [r_g2bc[t]])

        P_A, P_B, P_NB, P_M4, P_M8, P_HBA, P_HBX, P_GC, P_T = 0, 4, 8, 12, 16, 20, 24, 28, 32
        PP_GAIN, PP_CW, PP_CB, PP_BA, PP_BX, PP_LAM = 0, 4, 20, 24, 28, 32
        tt("dve", prm[:, P_T:P_T + 4], lbt[:, 0, :], lbt[:, 1, :], ALU.subtract, [r_lbt], [r_prm])
        act(prm[:, P_T:P_T + 4], prm[:, P_T:P_T + 4], AF.Tanh, [r_prm], [r_prm], scale=0.5)
        tsc("dve", prm[:, P_A:P_A + 4], prm[:, P_T:P_T + 4], 0.25, 0.75, ALU.mult, ALU.add, [r_prm], [r_prm])
        tsc("dve", prm[:, P_B:P_B + 4], prm[:, P_T:P_T + 4], -0.25, 0.25, ALU.mult, ALU.add, [r_prm], [r_prm])
        tsc("dve", prm[:, P_NB:P_NB + 4], prm[:, P_T:P_T + 4], 0.25, -0.25, ALU.mult, ALU.add, [r_prm], [r_prm])
        act(prm[:, P_T + 4:P_T + 8], pp[:, PP_LAM:PP_LAM + 4], AF.Exp, [r_pp, r_prm], [r_prm], scale=-1.0)
        act(prm[:, P_T + 4:P_T + 8], prm[:, P_T + 4:P_T + 8], AF.Ln, [r_prm], [r_prm], bias=1.0)
        tsc("dve", prm[:, P_M4:P_M4 + 4], prm[:, P_T + 4:P_T + 8], -4.0, None, ALU.mult, None, [r_prm], [r_prm])
        tsc("dve", prm[:, P_M8:P_M8 + 4], prm[:, P_T + 4:P_T + 8], -8.0, None, ALU.mult, None, [r_prm], [r_prm])
        tsc("dve", prm[:, P_HBA:P_HBA + 4], pp[:, PP_BA:PP_BA + 4], 0.5, None, ALU.mult, None, [r_pp, r_prm], [r_prm])
        tsc("dve", prm[:, P_HBX:P_HBX + 4], pp[:, PP_BX:PP_BX + 4], 0.5, None, ALU.mult, None, [r_pp, r_prm], [r_prm])
        tsc("dve", prm[:, P_GC:P_GC + 4], pp[:, PP_GAIN:PP_GAIN + 4], 0.5, None, ALU.mult, None, [r_pp, r_prm], [r_prm])
        act(cTt[:], cT[:], AF.Tanh, [r_cT], [r_cT], scale=0.5)
        stt(cTt[:].rearrange("p k s -> p (k s)"), cTt[:].rearrange("p k s -> p (k s)"), 1.0,
            cT[:].rearrange("p k s -> p (k s)"), ALU.add, ALU.mult, [r_cT], [r_cT])
        tsc("dve", scb[:], cTt[:], 0.5, None, ALU.mult, None, [r_cT], [r_cT])

        def ada_piece(pi):
            wbuf, r_wb, _ = ada_buf(pi)
            col0 = pi * WAP
            which = col0 // D
            cin = col0 % D
            if which in (0, 1, 3, 4):
                mi = {0: 0, 1: 1, 3: 2, 4: 3}[which]
                a = bankA()
                nfc = WAP // 128
                for fc in range(nfc):
                    o = PB[a][:, fc * 3:fc * 3 + 3]
                    for k in range(KC):
                        mm(o, wbuf[:, k, fc * 128:(fc + 1) * 128], scb[:, k, :], k == 0, k == KC - 1,
                           r_wb + [r_cT], [r_PB[a]])
                fc0 = cin // 128
                dst = modFM[:, mi, fc0:fc0 + nfc, :]
                src = PB[a][:, 0:nfc * 3].rearrange("p (f s) -> p f s", s=3)
                ci = col0 // 128
                bsrc = b48[:, ci:ci + nfc].unsqueeze(2).to_broadcast([128, nfc, 3])
                tt("dve", dst, src, bsrc, ALU.add, [r_b48], [r_mod[mi], r_PB[a]])
                if which in (1, 4):
                    tsc("dve", dst, dst, 1.0, 32.0, ALU.add, ALU.mult, [r_mod[mi]], [r_mod[mi]])
            else:
                v = bankV()
                o = PB[v][0:3, 0:WAP]
                for k in range(KC):
                    mm(o, scb[:, k, :], wbuf[:, k, :], k == 0, k == KC - 1, r_wb + [r_cT], [r_PB[v]])
                act(grow[:, cin:cin + WAP], o, AF.Copy, [], [r_grow, r_PB[v]])
            if pi >= 8 and ada_loaded[0] < NPIECE:
                load_ada(ada_loaded[0])
                ada_loaded[0] += 1

        def bcast_g(dst, r_dst):
            for t, sel in ((0, selP), (1, selS)):
                for half in range(2):
                    v = bankV()
                    hs = slice(half * 512, (half + 1) * 512)
                    mm(PB[v][:, :], sel[:, :], grow[:, hs], True, True, [r_grow, r_const], [r_PB[v]])
                    tt("dve", dst[t][:, hs], PB[v][:, :], dst[t][:, hs], ALU.add, [], [r_dst[t], r_PB[v]])

        PPW = D // WAP
        for pi in range(0, 2 * PPW):
            ada_piece(pi)

        for i in (1, 0, 2):
            dma("pool", wi[:, :, i * 1024:(i + 1) * 1024], w_in_v[:, :, i * 1024:(i + 1) * 1024], [], [r_wi[2 * i], r_wi[2 * i + 1]],
                sem="wi%d" % i)
        dma("pool", wo[:, :, :], w_out_v[:, :, :], [], [r_wo[0], r_wo[1]], sem="wo0")

        load_ada(8)
        load_ada(9)
        ada_loaded[0] = 10

        PT = PB7
        r_T = r_PB[T_BANK]

        hnT2 = arenaB[:, 44544:44544 + KC * GN].rearrange("p (k n) -> p k n", k=KC)
        r_hnT2 = [Res("hnT2_%d" % k) for k in range(KC)]

        def norm_and_transpose(gi, xt, r_xt, mi_sh, mi_sc, stat0, hnT=hnT, r_hnT=r_hnT, after=()):
            G = groups[gi]
            N = G["N"]
            nb = N // 128
            for b in range(nb):
                ss = stat[:, stat0 + b:stat0 + b + 1]
                rs = stat[:, stat0 + 2 + b:stat0 + 3 + b]
                rst = r_stat[stat0 // 2 + b]
                act(xn[:, b, :], xt[:, b, :], AF.Square, [r_xt], [r_xn[b], rst], accum=ss)
                tsc("pool", rs, ss, 1024.0 * EPS, 0.0, ALU.add, ALU.add, [], [rst])
                tt("pool", rs, rs, nhalf[:], ALU.pow, [r_const], [rst])
                tsc("dve", xn[:, b, :], xt[:, b, :], rs, None, ALU.mult, None, [r_xt, rst], [r_xn[b]])
            for c4 in range(2):
                for cc in range(4):
                    c = c4 * 4 + cc
                    for b in range(nb):
                        tr(PT[:, cc * 256 + b * 128:cc * 256 + (b + 1) * 128], xn[:, b, c * 128:(c + 1) * 128], [r_xn[b]], [r_T])
                for cc in range(4):
                    c = c4 * 4 + cc
                    for (s, c0, L) in G["segs"]:
                        act(hnT[:, c, c0:c0 + L], PT[:, cc * 256 + c0:cc * 256 + c0 + L], AF.Identity,
                            [r_mod[mi_sh], r_mod[mi_sc]], [r_hnT[c], r_T],
                            scale=modFM[:, mi_sc, c, s:s + 1], bias=modFM[:, mi_sh, c, s:s + 1], after=after)

        def proj_pair(cols, N, w, r_w_of_col, hnT=hnT, r_hnT=r_hnT):
            a = bankA()
            for i, col0 in enumerate(cols):
                for k in range(KC):
                    mm(PB[a][:, i * 256:i * 256 + N], w[:, k, col0:col0 + 128], hnT[:, k, 0:N], k == 0, k == KC - 1,
                       [r_w_of_col(col0), r_hnT[k]], [r_PB[a]])
            return a

        def phase1(gi):
            par = gi % 2
            TQ, r_TQ = FAM["TQ"][par]
            TF, r_TF = FAM["TF"][par]
            LF, r_LF = FAM["LF"][par]
            BB, r_BB = FAM["BB"][par]
            TO, r_TO = FAM["TO"][par]
            RA, r_RA = FAM["RA"][par]
            RB, r_RB = FAM["RB"][par]
            RC, r_RC = FAM["RC"][par]
            RD, r_RD = FAM["RD"][par]
            RE, r_RE = FAM["RE"][par]
            QT, r_QT = FAM["QT"][par]
            KT, r_KT = FAM["KT"][par]
            OSQ, r_OSQ = FAM["OSQ"][par]
            XCB, r_XCB = FAM["XCB"][par]
            vTM, r_vTM = FAM["vTM"][par]
            kTT, r_kTT = FAM["kTT"][par]
            mixT, r_mix = FAM["mixT"][par]
            S0m, r_S0m = FAM["S0m"][par]
            scT, r_scT = FAM["scT"][par]
            G = groups[gi]
            N = G["N"]
            nb = N // 128
            nch = N // 64
            segs = G["segs"]
            buf = gi % NXT
            xt = XT[buf]
            r_xt = r_XT[buf]
            chunk_seq = []
            for (s, c0, L) in segs:
                chunk_seq += [s] * (L // 64)
            if gi + 1 < NGRP:
                load_x(gi + 1, xs)
            if G["tok0"] == SEQ:
                for si, (s, c0, L) in enumerate(segs):
                    base = 0 if si == 0 else 128
                    dma("sp", XB[:, :, base:base + 3], cc0_d[:, s - 1, :, :], [], r_XB, sem="ld_cc")

            S.stage = "g%d:A" % gi
            norm_and_transpose(gi, xt, r_xt, 0, 1, 0)
            rwi = lambda col: r_wi[col // 512]

            S.stage = "g%d:%s" % (gi, "iv")
            for b in range(nb):
                v = bankV()
                for k in range(KC):
                    mm(PB[v][:, :], hnT[:, k, b * 128:(b + 1) * 128], wi[:, k, 1024:1536], k == 0, k == KC - 1,
                       [r_wi[2], r_hnT[k]], [r_PB[v]])
                act(vTM[:, b, :], PB[v][:, :], AF.Copy, [], [r_vTM[b], r_PB[v]])
            S.stage = "g%d:%s" % (gi, "fq")
            FULL = (N == GN)

            def pairv(buf, hp):
                return buf[:, 2 * hp:2 * hp + 2, :].rearrange("p i n -> p (i n)")

            for hp in range(2):
                a = proj_pair([512 + (2 * hp + i) * 128 for i in range(2)], N, wi, rwi)
                if FULL:
                    act(pairv(TF, hp), PB[a][:, :], AF.Tanh, [], r_TF[2 * hp:2 * hp + 2] + [r_PB[a]], scale=0.5)
                    continue
                for i in range(2):
                    h = 2 * hp + i
                    act(TF[:, h, 0:N], PB[a][:, i * 256:i * 256 + N], AF.Tanh, [], [r_TF[h], r_PB[a]], scale=0.5)
            for hp in range(2):
                a = proj_pair([(2 * hp + i) * 128 for i in range(2)], N, wi, rwi)
                if FULL:
                    rr = r_TQ[2 * hp:2 * hp + 2] + [r_PB[a]]
                    act(pairv(TQ, hp), PB[a][:, :], AF.Tanh, [], rr, scale=0.5)
                    stt(pairv(TQ, hp), pairv(TQ, hp), 1.0, PB[a][:, :], ALU.add, ALU.mult, [], rr)
                    continue
                for i in range(2):
                    h = 2 * hp + i
                    ps = PB[a][:, i * 256:i * 256 + N]
                    act(TQ[:, h, 0:N], ps, AF.Tanh, [], [r_TQ[h], r_PB[a]], scale=0.5)
                    stt(TQ[:, h, 0:N], TQ[:, h, 0:N], 1.0, ps, ALU.add, ALU.mult, [], [r_TQ[h], r_PB[a]])
            S.stage = "g%d:%s" % (gi, "xr")
            for cp_ in range(2):
                a = proj_pair([2048 + (2 * cp_ + i) * 128 for i in range(2)], N, wi, rwi)
                if FULL:
                    act(XB[:, 2 * cp_:2 * cp_ + 2, 3:3 + N], PB[a][:, :].rearrange("p (i n) -> p i n", i=2), AF.Copy, [],
                        r_XB[2 * cp_:2 * cp_ + 2] + [r_PB[a]])
                for i in range(2):
                    if FULL:
                        break
                    c = 2 * cp_ + i
                    for si, (s, c0, L) in enumerate(segs):
                        base = 0 if si == 0 else 128
                        act(XB[:, c, base + 3:base + 3 + L], PB[a][:, i * 256 + c0:i * 256 + c0 + L], AF.Copy, [], [r_XB[c], r_PB[a]])
                for i in range(2):
                    c = 2 * cp_ + i
                    for si, (s, c0, L) in enumerate(segs):
                        base = 0 if si == 0 else 128
                        xc = RA[:, c, c0:c0 + L]
                        tsc("dve", xc, XB[:, c, base:base + L], pp[:, PP_CW + c * 4:PP_CW + c * 4 + 1], pp[:, PP_CB + c:PP_CB + c + 1],
                            ALU.mult, ALU.add, [r_XB[c], r_pp], [r_RA[c]])
                        for j in range(1, 4):
                            stt(xc, XB[:, c, base + j:base + j + L], pp[:, PP_CW + c * 4 + j:PP_CW + c * 4 + j + 1], xc,
                                ALU.mult, ALU.add, [r_XB[c], r_pp], [r_RA[c]])
                        if G["last"]:
                            cp("pool", cbo[:, s, c, :], XB[:, c, base + L:base + L + 3], [r_XB[c]], [r_cbo])
                        else:
                            cp("pool", XB[:, c, 0:3], XB[:, c, L:L + 3], [], [r_XB[c]])
                    cp("pool", XCB[:, c, 0:N], RA[:, c, 0:N], [r_RA[c]], [r_XCB[c]])
            S.stage = "g%d:%s" % (gi, "gates")
            for c in range(4):
                a = bankA()
                mm(PB[a][:, 0:N], wbd[:, 0, c, :], XCB[:, c, 0:N], True, True, [r_wbd, r_XCB[c]], [r_PB[a]])
                mm(PB[a][:, 256:256 + N], wbd[:, 1, c, :], XCB[:, c, 0:N], True, True, [r_wbd, r_XCB[c]], [r_PB[a]])
                act(RB[:, c, 0:N], PB[a][:, 0:N], AF.Tanh, [r_prm], [r_RB[c], r_PB[a]], scale=0.5, bias=prm[:, P_HBA + c:P_HBA + c + 1])
                act(RC[:, c, 0:N], PB[a][:, 256:256 + N], AF.Tanh, [r_prm], [r_RC[c], r_PB[a]], scale=0.5, bias=prm[:, P_HBX + c:P_HBX + c + 1])
                act(RD[:, c, 0:N], RB[:, c, 0:N], AF.Exp, [r_RB[c], r_prm], [r_RD[c]],
                    scale=prm[:, P_M4 + c:P_M4 + c + 1], bias=prm[:, P_M4 + c:P_M4 + c + 1])
                act(RB[:, c, 0:N], RB[:, c, 0:N], AF.Exp, [r_prm], [r_RB[c]],
                    scale=prm[:, P_M8 + c:P_M8 + c + 1], bias=prm[:, P_M8 + c:P_M8 + c + 1])
                stt(RC[:, c, 0:N], RC[:, c, 0:N], 1.0, RA[:, c, 0:N], ALU.add, ALU.mult, [r_RA[c]], [r_RC[c]])
            S.stage = "g%d:%s" % (gi, "gr")
            for cp_ in range(2):
                a = proj_pair([2560 + (2 * cp_ + i) * 128 for i in range(2)], N, wi, rwi)
                if FULL:
                    act(pairv(RE, cp_), PB[a][:, :], AF.Copy, [], r_RE[2 * cp_:2 * cp_ + 2] + [r_PB[a]])
                    act(pairv(RA, cp_), PB[a][:, :], AF.Square, [], r_RA[2 * cp_:2 * cp_ + 2] + [r_PB[a]])
                    if cp_ == 1:
                        raf = RA[:].rearrange("p c n -> p (c n)")
                        ref = RE[:].rearrange("p c n -> p (c n)")
                        stt(raf, raf, 0.044715, ref, ALU.mult, ALU.mult, r_RE, r_RA)
                        tt("pool", raf, raf, ref, ALU.add, r_RE, r_RA)
                        act(raf, raf, AF.Tanh, [], r_RA, scale=0.7978845608)
                        stt(ref, raf, 1.0, ref, ALU.add, ALU.mult, r_RA, r_RE)
                    continue
                for i in range(2):
                    c = 2 * cp_ + i
                    ps = PB[a][:, i * 256:i * 256 + N]
                    act(RE[:, c, 0:N], ps, AF.Copy, [], [r_RE[c], r_PB[a]])
                    act(RA[:, c, 0:N], ps, AF.Square, [], [r_RA[c], r_PB[a]])
                for i in range(2):
                    c = 2 * cp_ + i
                    stt(RA[:, c, 0:N], RA[:, c, 0:N], 0.044715, RE[:, c, 0:N], ALU.mult, ALU.mult, [r_RE[c]], [r_RA[c]])
                    tt("dve", RA[:, c, 0:N], RA[:, c, 0:N], RE[:, c, 0:N], ALU.add, [r_RE[c]], [r_RA[c]])
                    act(RA[:, c, 0:N], RA[:, c, 0:N], AF.Tanh, [], [r_RA[c]], scale=0.7978845608)
                    stt(RE[:, c, 0:N], RA[:, c, 0:N], 1.0, RE[:, c, 0:N], ALU.add, ALU.mult, [r_RA[c]], [r_RE[c]])
            S.stage = "g%d:%s" % (gi, "og")
            for hp in range(2):
                a = proj_pair([1536 + (2 * hp + i) * 128 for i in range(2)], N, wi, rwi)
                if FULL:
                    rr = r_TO[2 * hp:2 * hp + 2] + [r_PB[a]]
                    act(pairv(TO, hp), PB[a][:, :], AF.Tanh, [], rr, scale=0.5)
                    stt(pairv(TO, hp), pairv(TO, hp), 1.0, PB[a][:, :], ALU.add, ALU.mult, [], rr)
                    continue
                for i in range(2):
                    h = 2 * hp + i
                    ps = PB[a][:, i * 256:i * 256 + N]
                    act(TO[:, h, 0:N], ps, AF.Tanh, [], [r_TO[h], r_PB[a]], scale=0.5)
                    stt(TO[:, h, 0:N], TO[:, h, 0:N], 1.0, ps, ALU.add, ALU.mult, [], [r_TO[h], r_PB[a]])

            S.stage = "g%d:%s" % (gi, "ln")
            for h in range(4):
                act(LF[:, h, 0:N], TF[:, h, 0:N], AF.Ln, [r_TF[h], r_prm], [r_LF[h]],
                    scale=prm[:, P_B + h:P_B + h + 1], bias=prm[:, P_A + h:P_A + h + 1])
                tsc("dve", TF[:, h, 0:N], TF[:, h, 0:N], prm[:, P_NB + h:P_NB + h + 1], prm[:, P_B + h:P_B + h + 1],
                    ALU.mult, ALU.add, [r_prm], [r_TF[h]])
            if N == GN:
                scan(BB[:].rearrange("p h n -> p (h n)"), RM[:], LF[:].rearrange("p h n -> p (h n)"), 0.0,
                     r_LF + [r_const], r_BB)
            else:
                for h in range(4):
                    scan(BB[:, h, 0:N], RM[:, 0:N], LF[:, h, 0:N], 0.0, [r_LF[h], r_const], [r_BB[h]])
            bv = BB[:, :, 0:N].rearrange("p h (c t) -> p h c t", t=64)
            lv = LF[:, :, 0:N].rearrange("p h (c t) -> p h c t", t=64)
            smv = sm[:].rearrange("p a (h c) -> p a h c", h=4)
            if N == GN:
                bw = BB[:].rearrange("p h (c t) -> p (h c) t", t=64)
                lw = LF[:].rearrange("p h (c t) -> p (h c) t", t=64)
                tt("dve", lw, bw, bw[:, :, 31:32].to_broadcast([128, 4 * nch, 64]), ALU.subtract, r_BB, r_LF)
                act(sm[:, 0, :], bw[:, :, 31], AF.Exp, r_BB, [r_sm])
                act(sm[:, 1, :], bw[:, :, 63], AF.Exp, r_BB, [r_sm])
                act(sm[:, 2, :], lw[:, :, 63], AF.Exp, r_LF, [r_sm])
                bf = BB[:].rearrange("p h n -> p (h n)")
                lf = LF[:].rearrange("p h n -> p (h n)")
                act(bf, lf, AF.Exp, r_LF, r_BB)
                act(lf, lf, AF.Exp, [], r_LF, scale=-1.0)
                tt("dve", QT[:].rearrange("p h n -> p (h n)"), TQ[:].rearrange("p h n -> p (h n)"), bf, ALU.mult, r_TQ + r_BB, r_QT)
                tt("dve", KT[:].rearrange("p h n -> p (h n)"), TF[:].rearrange("p h n -> p (h n)"), lf, ALU.mult, r_TF + r_LF, r_KT)
            else:
                for h in range(4):
                    tt("dve", lv[:, h], bv[:, h], bv[:, h, :, 31:32].to_broadcast([128, nch, 64]), ALU.subtract,
                       [r_BB[h]], [r_LF[h]])
                for h in range(4):
                    act(smv[:, 0, h, 0:nch], bv[:, h, :, 31], AF.Exp, [r_BB[h]], [r_sm])
                    act(smv[:, 1, h, 0:nch], bv[:, h, :, 63], AF.Exp, [r_BB[h]], [r_sm])
                    act(smv[:, 2, h, 0:nch], lv[:, h, :, 63], AF.Exp, [r_LF[h]], [r_sm])
                for h in range(4):
                    act(BB[:, h, 0:N], LF[:, h, 0:N], AF.Exp, [r_LF[h]], [r_BB[h]])
                    act(LF[:, h, 0:N], LF[:, h, 0:N], AF.Exp, [], [r_LF[h]], scale=-1.0)
                    tt("dve", QT[:, h, 0:N], TQ[:, h, 0:N], BB[:, h, 0:N], ALU.mult, [r_TQ[h], r_BB[h]], [r_QT[h]])
                    tt("dve", KT[:, h, 0:N], TF[:, h, 0:N], LF[:, h, 0:N], ALU.mult, [r_TF[h], r_LF[h]], [r_KT[h]])
            S.stage = "g%d:%s" % (gi, "rgln")
            if FULL:
                rbf = RB[:].rearrange("p c n -> p (c n)")
                rcf = RC[:].rearrange("p c n -> p (c n)")
                act(rbf, rbf, AF.Ln, [], r_RB, scale=-1.0, bias=1.0)
                act(rbf, rbf, AF.Exp, [], r_RB, scale=0.5)
                stt(rcf, rbf, 0.5, rcf, ALU.mult, ALU.mult, r_RB, r_RC)
            for c in range(4):
                if not FULL:
                    act(RB[:, c, 0:N], RB[:, c, 0:N], AF.Ln, [], [r_RB[c]], scale=-1.0, bias=1.0)
                    act(RB[:, c, 0:N], RB[:, c, 0:N], AF.Exp, [], [r_RB[c]], scale=0.5)
                    stt(RC[:, c, 0:N], RB[:, c, 0:N], 0.5, RC[:, c, 0:N], ALU.mult, ALU.mult, [r_RB[c]], [r_RC[c]])
                for (s, c0, L) in segs:
                    scan(RB[:, c, c0:c0 + L], RD[:, c, c0:c0 + L], RC[:, c, c0:c0 + L], hc[:, s, c:c + 1],
                         [r_RD[c], r_RC[c], r_hc[c]], [r_RB[c]])
                    cp("pool", hc[:, s, c:c + 1], RB[:, c, c0 + L - 1:c0 + L], [r_RB[c]], [r_hc[c]])
                if not FULL:
                    stt(mixT[:, 4 + c, 0:N], RB[:, c, 0:N], 0.5, RE[:, c, 0:N], ALU.mult, ALU.mult, [r_RB[c], r_RE[c]], [r_mix[4 + c]])
            if FULL:
                stt(mixT[:, 4:8, :].rearrange("p c n -> p (c n)"), RB[:].rearrange("p c n -> p (c n)"), 0.5,
                    RE[:].rearrange("p c n -> p (c n)"), ALU.mult, ALU.mult, r_RB + r_RE, r_mix[4:8])

            S.stage = "g%d:%s" % (gi, "kT")
            for h in range(4):
                for b in range(nb):
                    tr(PT[:, (h * nb + b) * 128:(h * nb + b + 1) * 128], KT[:, h, b * 128:(b + 1) * 128], [r_KT[h]], [r_T])
            act(kTT[:, :, 0:nb, :], PT[:, 0:4 * nb * 128].rearrange("p (h b n) -> p h b n", h=4, b=nb), AF.Copy, [], r_kTT + [r_T])
            S.stage = "g%d:%s" % (gi, "state")
            rK = r_PB[K_BANK]
            for j in range(nch):
                s = chunk_seq[j]
                b = j // 2
                p0 = (j % 2) * 64
                for h in range(4):
                    mm(PB[K_BANK][:, h * 128:(h + 1) * 128], kTT[p0:p0 + 64, h, b, :], vTM[p0:p0 + 64, b, h * 128:(h + 1) * 128],
                       True, True, [r_kTT[h], r_vTM[b]], [rK])
                for h in range(4):
                    tsc("pool", S0m[:, j, h, :], Sst[:, s, h, :], smv[:, 0, h, j:j + 1], 0.0, ALU.mult, ALU.add,
                        [r_S[s][h], r_sm], [r_S0m[j][h]])
                    tsc("pool", Sst[:, s, h, :], Sst[:, s, h, :], smv[:, 1, h, j:j + 1], 0.0, ALU.mult, ALU.add,
                        [r_sm], [r_S[s][h]])
                    stt(Sst[:, s, h, :], PB[K_BANK][:, h * 128:(h + 1) * 128], smv[:, 2, h, j:j + 1], Sst[:, s, h, :],
                        ALU.mult, ALU.add, [r_sm], [r_S[s][h], rK])
            S.stage = "g%d:%s" % (gi, "scores")
            rS = r_PB[S_BANK]
            for b in range(nb):
                cs = slice(b * 128, (b + 1) * 128)
                for h in range(4):
                    mm(PB[S_BANK][:, h * 128:(h + 1) * 128], KT[:, h, cs], QT[:, h, cs], True, True, [r_KT[h], r_QT[h]], [rS])
                tt("dve", scT[b][:, :, :], PB[S_BANK][:, :].rearrange("p (h t) -> p h t", h=4),
                   mask2[:].unsqueeze(1).to_broadcast([128, 4, 128]), ALU.mult, [r_const], [r_scT[b], rS])
            S.stage = "g%d:%s" % (gi, "o")
            for hp in range(2):
                O_BANK = O_BANKS[hp % len(O_BANKS)]
                rO = r_PB[O_BANK]
                for i in range(2):
                    h = 2 * hp + i
                    for b in range(nb):
                        cs = slice(i * 256 + b * 128, i * 256 + (b + 1) * 128)
                        mm(PB[O_BANK][:, cs], vTM[:, b, h * 128:(h + 1) * 128], scT[b][:, h, :], True, True,
                           [r_vTM[b], r_scT[b]], [rO])
                        for jj in range(2):
                            j = b * 2 + jj
                            mm(PB[O_BANK][:, i * 256 + j * 64:i * 256 + (j + 1) * 64], S0m[:, j, h, :], QT[:, h, j * 64:(j + 1) * 64],
                               False, True, [r_S0m[j][h], r_QT[h]], [rO], skip=True)
                if FULL:
                    act(pairv(BB, hp), PB[O_BANK][:, :], AF.Copy, [], r_BB[2 * hp:2 * hp + 2] + [rO])
                    act(pairv(OSQ, hp), PB[O_BANK][:, :], AF.Square, [], r_OSQ[2 * hp:2 * hp + 2] + [rO])
                for i in range(2):
                    if FULL:
                        break
                    h = 2 * hp + i
                    ps = PB[O_BANK][:, i * 256:i * 256 + N]
                    act(BB[:, h, 0:N], ps, AF.Copy, [], [r_BB[h], rO])
                    act(OSQ[:, h, 0:N], ps, AF.Square, [], [r_OSQ[h], rO])
                a = bankA()
                for i in range(2):
                    h = 2 * hp + i
                    mm(PB[a][:, i * 256:i * 256 + N], ones_bf[:], OSQ[:, h, 0:N], True, True, [r_const, r_OSQ[h]], [r_PB[a]])
                if FULL:
                    rl = r_LF[2 * hp:2 * hp + 2]
                    act(pairv(LF, hp), PB[a][:, :], AF.Ln, [], rl + [r_PB[a]], scale=1.0 / 128.0, bias=4.0 * EPS)
                    act(pairv(LF, hp), pairv(LF, hp), AF.Exp, [], rl, scale=-0.5)
                    tt("dve", pairv(BB, hp), pairv(BB, hp), pairv(LF, hp), ALU.mult, rl, r_BB[2 * hp:2 * hp + 2])
                for i in range(2):
                    h = 2 * hp + i
                    if not FULL:
                        act(LF[:, h, 0:N], PB[a][:, i * 256:i * 256 + N], AF.Ln, [], [r_LF[h], r_PB[a]], scale=1.0 / 128.0, bias=4.0 * EPS)
                        act(LF[:, h, 0:N], LF[:, h, 0:N], AF.Exp, [], [r_LF[h]], scale=-0.5)
                        tt("dve", BB[:, h, 0:N], BB[:, h, 0:N], LF[:, h, 0:N], ALU.mult, [r_LF[h]], [r_BB[h]])
                    stt(mixT[:, h, 0:N], BB[:, h, 0:N], prm[:, P_GC + h:P_GC + h + 1], TO[:, h, 0:N], ALU.mult, ALU.mult,
                        [r_BB[h], r_prm, r_TO[h]], [r_mix[h]])
            if G["last"]:
                for s in sorted(set(chunk_seq)):
                    final_tokens.append(dma("sp", S_out_d[s].rearrange("h d v -> d h v"), Sst[:, s, :, :], r_S[s], [], sem="st_S%d" % s))

            S.stage = "g%d:%s" % (gi, "outproj")
            gb = g1bc[0] if G["tok0"] < SEQ else g1bc[1]
            r_gb = r_g1bc[0] if G["tok0"] < SEQ else r_g1bc[1]
            for b in range(nb):
                for half in range(2):
                    v = bankV()
                    hs = slice(half * 512, (half + 1) * 512)
                    for k in range(KC):
                        mm(PB[v][:, :], mixT[:, k, b * 128:(b + 1) * 128], wo[:, k, hs], k == 0, k == KC - 1,
                           [r_mix[k], r_wo[half]], [r_PB[v]])
                    tt("dve", tmp1[:, :], PB[v][:, :], gb[:, hs], ALU.mult, [r_gb], [r_tmp1, r_PB[v]])
                    tt(_os.environ.get("K_RESENG", "pool"), xt[:, b, hs], xt[:, b, hs], tmp1[:, :], ALU.add, [r_tmp1], [r_xt])
            dma("sp", x1_d[G["tok0"]:G["tok0"] + N, :].rearrange("(b p) d -> p b d", p=128), xt[:, 0:nb, :],
                [r_xt], [r_x1d[gi]], sem="x1s%d" % buf)

        r_x1d = [Res("x1d%d" % g) for g in range(NGRP)]

        for pi in range(2 * PPW, 3 * PPW):
            ada_piece(pi)
        bcast_g(g1bc, r_g1bc)

        for gi in range(NGRP):
            phase1(gi)
            if gi == 0:
                for pi in range(3 * PPW, 6 * PPW):
                    ada_piece(pi)
                bcast_g(g2bc, r_g2bc)
                dma("sp", fgb[:], fgain_d.partition_broadcast(128), [], [r_fgb])
                tsc("dve", fgb[:], fgb[:], 32.0, None, ALU.mult, None, [r_fgb], [r_fgb])

        final_tokens.append(dma("sp", h_out_d, hc[:], r_hc, [], sem="st_h"))
        final_tokens.append(dma("sp", cb_out_d, cbo[:], [r_cbo], [], sem="st_cb"))

        r_wu = [Res("wu%d" % i) for i in range(8)]
        r_wd = [Res("wd%d" % i) for i in range(8)]
        r_hT = [Res("hT%d" % f) for f in range(32)]
        r_RL = [Res("RL%d" % i) for i in range(3)]
        r_tmp2 = [Res("tmp2_%d" % i) for i in range(2)]
        w_up_v = w_up_d.rearrange("(k p) n -> p k n", p=128)
        w_down_v = w_down_d.rearrange("(f p) n -> p f n", p=128)
        for i in range(4):
            dma("pool", wu[:, :, i * 1024:(i + 1) * 1024], w_up_v[:, :, i * 1024:(i + 1) * 1024], [], [r_wu[2 * i], r_wu[2 * i + 1]],
                sem="wu%d" % i, after=resB)
        for i in range(8):
            dma("pool", wd[:, i * 4:(i + 1) * 4, :], w_down_v[:, i * 4:(i + 1) * 4, :], [], [r_wd[i]], sem="wd%d" % i,
                after=(r_wi if i < 6 else r_wo))

        def load_x1(gi):
            G = groups[gi]
            nb = G["N"] // 128
            buf = (NGRP + gi) % NXT
            dma("sp", XT[buf][:, 0:nb, :], x1_d[G["tok0"]:G["tok0"] + G["N"], :].rearrange("(b p) d -> p b d", p=128),
                [r_x1d[gi]], [r_XT[buf]], sem=xsems[buf])

        load_x1(0)

        def phase2(gi):
            G = groups[gi]
            N = G["N"]
            nb = N // 128
            buf = (NGRP + gi) % NXT
            xt = XT[buf]
            r_xt = r_XT[buf]
            if gi + 1 < NGRP:
                load_x1(gi + 1)
            hb, r_hb = (hnT, r_hnT) if gi % 2 == 0 else (hnT2, r_hnT2)
            norm_and_transpose(gi, xt, r_xt, 2, 3, 4, hb, r_hb, after=(resB if gi == 1 else ()))
            rwu = lambda col: r_wu[col // 512]
            for fp in range(16):
                a = proj_pair([(2 * fp + i) * 128 for i in range(2)], N, wu, rwu, hb, r_hb)
                rl = nxt("RL", 3)
                if N == GN:
                    act(RL[rl][:, :], PB[a][:, :], AF.Relu, [], [r_RL[rl], r_PB[a]])
                    tt("pool", hT[:, 2 * fp:2 * fp + 2, :], RL[rl][:, :].rearrange("p (f n) -> p f n", f=2),
                       RL[rl][:, :].rearrange("p (f n) -> p f n", f=2), ALU.mult, [r_RL[rl]], [r_hT[2 * fp], r_hT[2 * fp + 1]])
                else:
                    for i in range(2):
                        act(RL[rl][:, i * 256:i * 256 + N], PB[a][:, i * 256:i * 256 + N], AF.Relu, [], [r_RL[rl], r_PB[a]])
                        tt("pool", hT[:, 2 * fp + i, 0:N], RL[rl][:, i * 256:i * 256 + N], RL[rl][:, i * 256:i * 256 + N],
                           ALU.mult, [r_RL[rl]], [r_hT[2 * fp + i]])
            gb = g2bc[0] if G["tok0"] < SEQ else g2bc[1]
            r_gb = r_g2bc[0] if G["tok0"] < SEQ else r_g2bc[1]
            for b in range(nb):
                for half in range(2):
                    v = bankV()
                    hs = slice(half * 512, (half + 1) * 512)
                    for fc in range(32):
                        mm(PB[v][:, :], hT[:, fc, b * 128:(b + 1) * 128], wd[:, fc, hs], fc == 0, fc == 31,
                           [r_hT[fc], r_wd[fc // 4]], [r_PB[v]])
                    t = nxt("tmp", 2)
                    tt("dve", tmp2[t][:, :], PB[v][:, :], gb[:, hs], ALU.mult, [r_gb], [r_tmp2[t], r_PB[v]])
                    tt("pool", xt[:, b, hs], xt[:, b, hs], tmp2[t][:, :], ALU.add, [r_tmp2[t]], [r_xt])
                ss = stat[:, 8 + b:9 + b]
                rs = stat[:, 10 + b:11 + b]
                jk = nxt("tmp", 2)
                act(tmp2[jk][:, :].bitcast(BF16), xt[:, b, :], AF.Square, [r_xt], [r_tmp2[jk], r_stat[4 + b]], accum=ss)
                tsc("pool", rs, ss, 1024.0 * EPS, 0.0, ALU.add, ALU.add, [], [r_stat[4 + b]])
                tt("pool", rs, rs, nhalf[:], ALU.pow, [r_const], [r_stat[4 + b]])
                stt(xt[:, b, :], xt[:, b, :], rs, fgb[:], ALU.mult, ALU.mult, [r_stat[4 + b], r_fgb], [r_xt])
            tok = dma("sp", y_d[G["tok0"]:G["tok0"] + N, :].rearrange("(b p) d -> p b d", p=128), xt[:, 0:nb, :],
                      [r_xt], [], sem="ys%d" % buf)
            final_tokens.append(tok)

        for gi in range(NGRP):
            phase2(gi)

        if _os.environ.get("K_FILL", "1") == "1":
            t_end_p1 = 0.0
            fsrc = RM[:, 0:256]
            fdst = PB[5 if OSK else 2][:, 0:256]
            S.filler = (lambda e: e.matmul(fdst, lhsT=ident[:], rhs=fsrc, start=True, stop=True), r_const.w, 0.22,
                        float(_os.environ.get("K_FILL_LO", "150")), float(_os.environ.get("K_FILL_HI", "1250")),
                        float(_os.environ.get("K_FILL_GMIN", "1.0")))
        S.schedule()
        build_nc.last_sched = S
        S.emit(st, final_tokens=final_tokens)
    return nc


_NC_CACHE = {}


def _get_nc():
    if "nc" not in _NC_CACHE:
        _NC_CACHE["nc"] = build_nc()
    return _NC_CACHE["nc"]


def kernel(x_prompt, x_sample, c_prompt, c_sample, state_hgrn, state_rglru, cache_conv,
           hg_lb_logits, w_ada, b_ada, w_in, hg_norm_gain, conv_w, conv_b,
           rg_wa, rg_ba, rg_wx, rg_bx, rg_lambda, w_out, w_up, w_down, final_gain):
    f = lambda a: np.ascontiguousarray(np.asarray(a, dtype=np.float32))
    x_prompt, x_sample, c_prompt, c_sample = f(x_prompt), f(x_sample), f(c_prompt), f(c_sample)
    state_hgrn, state_rglru, cache_conv = f(state_hgrn), f(state_rglru), f(cache_conv)
    n = 8

    def chan(v, nchunk):
        return np.asarray(v, np.float32).reshape(nchunk, 128).T

    pp = np.zeros((128, NPP), np.float32)
    pp[:, 0:4] = chan(hg_norm_gain[0], 4)
    cw = np.asarray(conv_w[0], np.float32)
    for c in range(4):
        for j in range(4):
            pp[:, 4 + c * 4 + j] = cw[j, c * 128:(c + 1) * 128]
    pp[:, 20:24] = chan(conv_b[0], 4)
    pp[:, 24:28] = chan(rg_ba[0], 4)
    pp[:, 28:32] = chan(rg_bx[0], 4)
    pp[:, 32:36] = chan(rg_lambda[0], 4)
    lbl = f(np.asarray(hg_lb_logits, np.float32).reshape(2, 4, 128).transpose(2, 0, 1))
    shared = {
        "lbl": lbl, "w_ada": f(w_ada[0]), "b_ada": f(np.asarray(b_ada[0]).reshape(-1)), "b48": f(np.asarray(b_ada[0], np.float32).reshape(48, 128).T), "w_in": f(w_in[0]),
        "pp": pp, "rg_wa": f(rg_wa[0]), "rg_wx": f(rg_wx[0]), "w_out": f(w_out[0]), "w_up": f(w_up[0]),
        "w_down": f(w_down[0]), "fgain": f(final_gain),
    }
    in_maps = []
    for i in range(n):
        xs_i = np.concatenate([x_prompt[i], x_sample[2 * i], x_sample[2 * i + 1]], axis=0)
        cs = np.stack([c_prompt[i], c_sample[2 * i], c_sample[2 * i + 1]], axis=0)
        cT = f(cs.reshape(3, KC, 128).transpose(2, 1, 0))
        s_hg = f(state_hgrn[0, 2 * i:2 * i + 2])
        h0 = f(state_rglru[0, 2 * i:2 * i + 2].reshape(2, 4, 128).transpose(2, 0, 1))
        cc0 = f(cache_conv[0, 2 * i:2 * i + 2].reshape(2, 3, 4, 128).transpose(3, 0, 2, 1))
        m = dict(shared)
        m.update({"xs": f(xs_i), "cT": cT, "s_hg": s_hg, "h0": h0, "cc0": cc0})
        in_maps.append(m)
    nc = _get_nc()
    res = run_bass_kernel_spmd(nc, in_maps, core_ids=list(range(n)))
    R = res.results
    y_prompt = np.stack([R[i]["y"][0:SEQ] for i in range(n)], axis=0)
    y_sample = np.stack([R[i]["y"][SEQ + 64 * j:SEQ + 64 * (j + 1)] for i in range(n) for j in range(2)], axis=0)
    S_p = np.stack([R[i]["S_out"][0] for i in range(n)], axis=0)[None]
    S_s = np.stack([R[i]["S_out"][1 + j] for i in range(n) for j in range(2)], axis=0)[None]

    def hvec(a):
        return a.T.reshape(512)

    h_p = np.stack([hvec(R[i]["h_out"][:, 0, :]) for i in range(n)], axis=0)[None]
    h_s = np.stack([hvec(R[i]["h_out"][:, 1 + j, :]) for i in range(n) for j in range(2)], axis=0)[None]

    def cbm(a):
        return a.transpose(2, 1, 0).reshape(3, 512)

    cb_p = np.stack([cbm(R[i]["cb_out"][:, 0]) for i in range(n)], axis=0)[None]
    cb_s = np.stack([cbm(R[i]["cb_out"][:, 1 + j]) for i in range(n) for j in range(2)], axis=0)[None]
    outs = (y_prompt, y_sample, S_p, h_p, cb_p, S_s, h_s, cb_s)
    return tuple(np.ascontiguousarray(o, dtype=np.float32) for o in outs)
```

```python
import numpy as np
from contextlib import ExitStack
import concourse.bass as bass
import concourse.mybir as mybir
from concourse.bass_utils import run_bass_kernel_spmd

F32 = mybir.dt.float32
BF16 = mybir.dt.bfloat16
ALU = mybir.AluOpType
AF = mybir.ActivationFunctionType

ENGS = ("pe", "act", "dve", "pool", "sp")


class Res:
    __slots__ = ("name", "w", "r")

    def __init__(self, name=""):
        self.name = name
        self.w = None
        self.r = []


class _Op:
    __slots__ = ("eng", "fn", "deps", "signal", "sigval", "dsem", "dval", "cost", "tset", "idx", "order_deps",
                 "npred", "succ", "dr", "fin", "start", "dcost", "desc", "bind", "nbytes", "tail")

    def __init__(self, eng, fn):
        self.eng = eng
        self.fn = fn
        self.deps = set()
        self.signal = False
        self.sigval = 0
        self.dsem = None
        self.dval = 0
        self.cost = 0.3
        self.tset = None
        self.idx = 0
        self.order_deps = []
        self.dcost = 0.0


import os as _os0
ATTACH_WAIT = _os0.environ.get("K_ATTACH", "1") == "1"
SEM_LAT = float(_os0.environ.get("K_SEMLAT", "0.10"))
TSWITCH = float(_os0.environ.get("K_TSW", "2.0"))


class Sched:
    def __init__(self, nc):
        self.nc = nc
        self.ops = {e: [] for e in ENGS}
        self.all_ops = []
        self.dma_tot = {}
        self.dma_last = {}
        self.dma_sems = {}
        self.stage = ""
        self.filler = None

    def _collect(self, op, reads, writes, after=()):
        eng = op.eng
        skip_self = eng in ("pe", "sp")

        def add(t):
            if t is None:
                return
            if skip_self and t[0] == "op" and t[1].eng == eng:
                op.order_deps.append(t[1])
                return
            op.deps.add(t)

        for r in reads:
            add(r.w)
        for w in after:
            add(w.w)
            for t in w.r:
                add(t)
        for w in writes:
            add(w.w)
            for t in w.r:
                add(t)

    def op(self, eng, fn, reads=(), writes=(), cost=0.3, tset=None, after=()):
        o = _Op(eng, fn)
        o.cost = cost
        o.tset = tset
        self._collect(o, reads, writes, after)
        o.idx = len(self.all_ops)
        o.desc = self.stage
        self.all_ops.append(o)
        self.ops[eng].append(o)
        tok = ("op", o)
        for r in reads:
            r.r.append(tok)
        for w in writes:
            w.w = tok
            w.r = []
        return tok

    def dma(self, eng, fn, sem, reads=(), writes=(), after=(), cost=3.0, nbytes=0.0, after_tok=()):
        o = _Op(eng, fn)
        o.cost = 1.06 if eng == "pool" else 0.06
        o.dcost = cost
        o.nbytes = nbytes
        self._collect(o, reads, writes, after)
        for t in after_tok:
            o.deps.add(t)
        o.idx = len(self.all_ops)
        o.desc = self.stage + ":dma:" + sem
        self.all_ops.append(o)
        self.ops[eng].append(o)
        tot = self.dma_tot[sem] + 16
        self.dma_tot[sem] = tot
        prev = self.dma_last.get(sem)
        if prev is not None:
            assert prev.eng == eng
            o.order_deps.append(prev)
        self.dma_last[sem] = o
        o.dsem = sem
        o.dval = tot
        tok = ("dma", o, sem, tot)
        for r in reads:
            r.r.append(tok)
        for w in writes:
            w.w = tok
            w.r = []
        return tok

    def dsem(self, name):
        if name not in self.dma_tot:
            self.dma_tot[name] = 0
        return name

    def schedule(self):
        import heapq
        ops = self.all_ops
        for o in ops:
            o.succ = []
            o.fin = None
            o.start = None
        for o in ops:
            preds = set(t[1] for t in o.deps) | set(o.order_deps)
            o.npred = len(preds)
            for p in preds:
                p.succ.append(o)
        PRIO = _os0.environ.get("K_PRIO", "1") == "1"
        for o in reversed(ops):
            t = 0.0
            for sc in o.succ:
                v = sc.tail + SEM_LAT
                if v > t:
                    t = v
            o.tail = t + o.cost + (o.dcost if o.dsem is not None else 0.0)
        fut = {e: [] for e in ENGS}
        avail = {e: [] for e in ENGS}

        def data_ready(o):
            dr = 0.0
            for t in o.deps:
                p = t[1]
                f = p.fin + SEM_LAT
                if f > dr:
                    dr = f
            for p in o.order_deps:
                if p.start > dr:
                    dr = p.start
            return dr

        def prio(o):
            return -o.tail if PRIO else o.idx

        for o in ops:
            if o.npred == 0:
                o.dr = 0.0
                heapq.heappush(fut[o.eng], (0.0, o.idx, o))
        free = {e: 0.0 for e in ENGS}
        dma_free = [0.0]
        DMA_BW = 230e3
        cur_set = [None]
        new_order = {e: [] for e in ENGS}
        n_done = 0
        total = len(ops)
        EPS = float(_os0.environ.get("K_EPS", "0.02"))
        while n_done < total:
            best_e = None
            best_t = None
            for e in ENGS:
                if avail[e]:
                    t_e = free[e]
                    if fut[e] and fut[e][0][0] < t_e:
                        pass
                elif fut[e]:
                    t_e = max(free[e], fut[e][0][0])
                else:
                    continue
                if best_t is None or t_e < best_t:
                    best_t = t_e
                    best_e = e
            e = best_e
            t_e = best_t
            f = fut[e]
            while f and f[0][0] <= t_e + EPS:
                dr, idx, o = heapq.heappop(f)
                heapq.heappush(avail[e], (prio(o), idx, o))
            a = avail[e]
            if e == "act" and cur_set[0] is not None:
                cands = heapq.nsmallest(8, a)
                pick = cands[0]
                if pick[2].tset is not None and pick[2].tset != cur_set[0]:
                    for c in cands[1:]:
                        if (c[2].tset is None or c[2].tset == cur_set[0]) and c[0] - pick[0] < float(_os0.environ.get("K_SWTH", "10.0")):
                            pick = c
                            break
                if pick is a[0]:
                    heapq.heappop(a)
                else:
                    a.remove(pick)
                    heapq.heapify(a)
                o = pick[2]
            else:
                o = heapq.heappop(a)[2]
            est = max(free[e], o.dr)
            if e == "act" and o.tset is not None:
                if cur_set[0] is not None and o.tset != cur_set[0]:
                    est += TSWITCH
                cur_set[0] = o.tset
            o.start = est
            o.bind = None
            free[e] = est + o.cost
            if o.dsem is not None:
                xs_ = max(est + o.cost + 0.8, dma_free[0])
                dma_free[0] = xs_ + o.nbytes / DMA_BW
                o.fin = dma_free[0] + 1.2
            else:
                o.fin = est + o.cost
            new_order[e].append(o)
            n_done += 1
            for sc in o.succ:
                sc.npred -= 1
                if sc.npred == 0:
                    sc.dr = data_ready(sc)
                    heapq.heappush(fut[sc.eng], (sc.dr, sc.idx, sc))
        if self.filler is not None:
            fn, ftok, fcost, t_lo, t_hi, gmin = self.filler
            pe = []
            prev_end = 0.0
            nfill = 0
            for o in new_order["pe"]:
                gap = o.start - prev_end
                if t_lo < o.start < t_hi and gap >= gmin:
                    n = int((gap - 0.35) / fcost)
                    for _ in range(max(0, n)):
                        f = _Op("pe", fn)
                        f.cost = fcost
                        f.deps = set([ftok])
                        f.desc = "filler"
                        f.start = prev_end
                        f.fin = prev_end + fcost
                        pe.append(f)
                        nfill += 1
                pe.append(o)
                prev_end = o.start + o.cost
            new_order["pe"] = pe
            self.nfill = nfill
        self.ops = new_order
        self.est_time = max(o.fin for o in ops)
        self.est_busy = {e: sum(o.cost for o in new_order[e]) for e in ENGS}

    def emit(self, stack, final_tokens=()):
        nc = self.nc
        esem = {e: stack.enter_context(nc.semaphore("es_" + e)) for e in ENGS}
        for name in self.dma_tot:
            self.dma_sems[name] = stack.enter_context(nc.semaphore("ds_" + name))
        for e in ENGS:
            for o in self.ops[e]:
                for t in o.deps:
                    if t[0] == "op":
                        t[1].signal = True
        for t in final_tokens:
            if t[0] == "op":
                t[1].signal = True
        for e in ENGS:
            c = 0
            for o in self.ops[e]:
                if o.signal:
                    c += 1
                    o.sigval = c

        def tokkey(t):
            if t[0] == "op":
                return ("e", t[1].eng), t[1].sigval
            return ("d", t[2]), t[3]

        PRUNE = _os0.environ.get("K_PRUNE", "1") == "1"
        know = {}
        if PRUNE:
            allops = []
            for e in ENGS:
                allops.extend(self.ops[e])
            allops.sort(key=lambda o: o.start)
            last_on = {}
            for o in allops:
                k = dict(last_on.get(o.eng, {}))
                for t in o.deps:
                    key, val = tokkey(t)
                    if val > k.get(key, 0):
                        k[key] = val
                    kd = know.get(id(t[1]))
                    if kd:
                        for kk_, vv_ in kd.items():
                            if vv_ > k.get(kk_, 0):
                                k[kk_] = vv_
                last_on[o.eng] = k
                k2 = k
                if o.dsem is not None:
                    k2 = dict(k)
                    k2[("d", o.dsem)] = max(k2.get(("d", o.dsem), 0), o.dval)
                elif o.signal:
                    k2 = dict(k)
                    k2[("e", o.eng)] = max(k2.get(("e", o.eng), 0), o.sigval)
                know[id(o)] = k2

        def run_engine(e, engobj, extra_final=None):
            waited = {}
            for o in self.ops[e]:
                need = {}
                needop = {}
                for t in o.deps:
                    key, val = tokkey(t)
                    if val > need.get(key, 0):
                        need[key] = val
                        needop[key] = t[1]
                todo = []
                items = sorted(need.items(), key=lambda kv: -needop[kv[0]].start) if PRUNE else list(need.items())
                for key, val in items:
                    if waited.get(key, 0) >= val:
                        continue
                    waited[key] = val
                    if PRUNE:
                        kd = know.get(id(needop[key]))
                        if kd:
                            for kk_, vv_ in kd.items():
                                if vv_ > waited.get(kk_, 0):
                                    waited[kk_] = vv_
                    s = esem[key[1]] if key[0] == "e" else self.dma_sems[key[1]]
                    todo.append((s, val))
                attach = None
                if todo and o.dsem is None and ATTACH_WAIT and e != "pe":
                    attach = todo.pop()
                for s, val in todo:
                    engobj.wait_ge(s, val)
                inst = o.fn(engobj)
                if attach is not None:
                    inst._wait_ge(attach[0], attach[1])
                if o.dsem is not None:
                    inst.then_inc(self.dma_sems[o.dsem], 16)
                elif o.signal:
                    inst.then_inc(esem[e], 1)
            if extra_final:
                need = {}
                for t in extra_final:
                    key, val = tokkey(t)
                    if val > need.get(key, 0):
                        need[key] = val
                for key, val in need.items():
                    s = esem[key[1]] if key[0] == "e" else self.dma_sems[key[1]]
                    engobj.wait_ge(s, val)

        with nc.Block() as block:
            @block.tensor
            def _(eng):
                run_engine("pe", eng)

            @block.scalar
            def _(eng):
                run_engine("act", eng)

            @block.vector
            def _(eng):
                run_engine("dve", eng)

            @block.gpsimd
            def _(eng):
                run_engine("pool", eng)

            @block.sync
            def _(eng):
                run_engine("sp", eng, extra_final=final_tokens)


D = 1024
KC = 8
SEQ = 4096
NTOK = SEQ + 128
GN = 256
NPG = SEQ // GN
EPS = 1e-6
NPP = 36
import os as _os
NXT = int(_os.environ.get("K_NXT", "3"))
DBL = set(x for x in _os.environ.get("K_DBL", "").split(",") if x)
EXPLORE = _os.environ.get("K_EXPLORE", "") == "1"


def build_nc(n_prompt_groups=NPG, with_sample=True):
    nc = bass.Bass("TRN2", target_bir_lowering=False)
    S = Sched(nc)

    def din(name, shape):
        return nc.dram_tensor(name, list(shape), F32, kind="ExternalInput").ap()

    def dout(name, shape):
        return nc.dram_tensor(name, list(shape), F32, kind="ExternalOutput").ap()

    xs = din("xs", [NTOK, D])
    cT_d = din("cT", [128, KC, 3])
    s_hg_d = din("s_hg", [2, 4, 128, 128])
    h0_d = din("h0", [128, 2, 4])
    cc0_d = din("cc0", [128, 2, 4, 3])
    lbl_d = din("lbl", [128, 2, 4])
    w_ada_d = din("w_ada", [D, 6 * D])
    b_ada_d = din("b_ada", [6 * D])
    b48_d = din("b48", [128, 48])
    w_in_d = din("w_in", [D, 3 * D])
    pp_d = din("pp", [128, NPP])
    rg_wa_d = din("rg_wa", [8, 64, 64])
    rg_wx_d = din("rg_wx", [8, 64, 64])
    w_out_d = din("w_out", [D, D])
    w_up_d = din("w_up", [D, 4 * D])
    w_down_d = din("w_down", [4 * D, D])
    fgain_d = din("fgain", [D])

    y_d = dout("y", [NTOK, D])
    S_out_d = dout("S_out", [3, 4, 128, 128])
    h_out_d = dout("h_out", [128, 3, 4])
    cb_out_d = dout("cb_out", [128, 3, 4, 3])
    x1_d = nc.dram_tensor("x1_scratch", [NTOK, D], F32).ap()

    groups = []
    for g in range(n_prompt_groups):
        groups.append(dict(tok0=g * GN, N=GN, segs=[(0, 0, GN)], first=(g == 0), last=(g == n_prompt_groups - 1)))
    if with_sample:
        groups.append(dict(tok0=SEQ, N=128, segs=[(1, 0, 64), (2, 64, 64)], first=True, last=True))
    NGRP = len(groups)

    final_tokens = []
    with ExitStack() as st:
        def sbt(name, shape, dt):
            return st.enter_context(nc.sbuf_tensor("sb_" + name, list(shape), dt))

        def pst(name, shape, dt):
            return st.enter_context(nc.psum_tensor("ps_" + name, list(shape), dt))

        arenaA = sbt("arenaA", [128, 32768], BF16)
        ARB_E = 46592
        arenaB = sbt("arenaB", [128, ARB_E], BF16)
        resA = []
        resB = []
        offB = [0]

        offB2 = [0]

        def allocB(nelem_bf16, name):
            if EXPLORE and "_b" in name:
                o = offB2[0]
                offB2[0] = o + nelem_bf16
                return arenaB[:, o:o + nelem_bf16]
            o = offB[0]
            offB[0] = o + nelem_bf16
            assert offB[0] <= ARB_E, (name, offB[0])
            return arenaB[:, o:o + nelem_bf16]

        def resB_new(name):
            r = Res(name)
            resB.append(r)
            return r

        wi = arenaA[:, 0:KC * 3072].rearrange("p (k n) -> p k n", k=KC)
        wo = arenaA[:, KC * 3072:KC * 4096].rearrange("p (k n) -> p k n", k=KC)
        wd = arenaA[:, 0:32 * 1024].rearrange("p (f n) -> p f n", f=32)
        r_wi = [Res("wi%d" % i) for i in range(6)]
        r_wo = [Res("wo%d" % i) for i in range(2)]
        resA.extend(r_wi + r_wo)

        wu = arenaB[:, 0:KC * 4096].rearrange("p (k n) -> p k n", k=KC)
        hT = arenaB[:, 32768:32768 + 32 * GN].rearrange("p (f n) -> p f n", f=32)
        RL = [arenaB[:, 40960 + i * 512:40960 + (i + 1) * 512] for i in range(3)]
        tmp2 = [arenaB[:, 42496 + i * 1024:42496 + (i + 1) * 1024].bitcast(F32) for i in range(2)]

        def f32v(n, name):
            return allocB(2 * n, name).bitcast(F32)

        def _h4f(nm):
            return lambda tag: (f32v(4 * GN, nm + tag).rearrange("p (h n) -> p h n", h=4), [resB_new("%s%s%d" % (nm, tag, h)) for h in range(4)])

        def _h4b(nm):
            return lambda tag: (allocB(4 * GN, nm + tag).rearrange("p (h n) -> p h n", h=4), [resB_new("%s%s%d" % (nm, tag, h)) for h in range(4)])

        CTOR = {}
        for nm in ("TQ", "TF", "LF", "BB", "RA", "RB", "RC", "RD", "RE", "TO"):
            CTOR[nm] = _h4f(nm)
        for nm in ("QT", "KT", "OSQ", "XCB"):
            CTOR[nm] = _h4b(nm)
        CTOR["vTM"] = lambda tag: (allocB(2 * 512, "vTM" + tag).rearrange("p (b n) -> p b n", b=2), [resB_new("vTM%s%d" % (tag, b)) for b in range(2)])
        CTOR["kTT"] = lambda tag: (allocB(4 * 2 * 128, "kTT" + tag).rearrange("p (h b n) -> p h b n", h=4, b=2), [resB_new("kTT%s%d" % (tag, h)) for h in range(4)])
        CTOR["mixT"] = lambda tag: (allocB(8 * GN, "mixT" + tag).rearrange("p (k n) -> p k n", k=8), [resB_new("mix%s%d" % (tag, k)) for k in range(8)])
        CTOR["S0m"] = lambda tag: (allocB(4 * 4 * 128, "S0m" + tag).rearrange("p (j h n) -> p j h n", j=4, h=4),
                                   [[resB_new("S0m%s%d_%d" % (tag, j, h)) for h in range(4)] for j in range(4)])
        CTOR["scT"] = lambda tag: ([allocB(512, "scT%s%d" % (tag, i)).rearrange("p (h t) -> p h t", h=4) for i in range(2)],
                                   [resB_new("scT%s%d" % (tag, i)) for i in range(2)])
        FAM = {}
        for nm, ct in CTOR.items():
            a0 = ct("")
            FAM[nm] = [a0, ct("_b") if nm in DBL else a0]
        XBW = 264
        XB = f32v(4 * XBW, "XB").rearrange("p (c n) -> p c n", c=4)
        Sst = f32v(3 * 4 * 128, "Sst").rearrange("p (s h n) -> p s h n", s=3, h=4)
        WAP = 256
        wada = [allocB(KC * WAP, "wada%d" % i).rearrange("p (k n) -> p k n", k=KC) for i in range(2)]
        g1bc = [f32v(1024, "g1bc%d" % i) for i in range(2)]
        tmp1 = f32v(512, "tmp1")

        r_XB = [resB_new("XB%d" % h) for h in range(4)]
        r_S = [[resB_new("S%d_%d" % (s, h)) for h in range(4)] for s in range(3)]
        r_wada = [resB_new("wada%d" % i) for i in range(2)]
        r_g1bc = [resB_new("g1bc%d" % i) for i in range(2)]
        r_tmp1 = resB_new("tmp1")

        XT = [sbt("xt%d" % i, [128, 2, D], F32) for i in range(min(NXT, 2 if EXPLORE else NXT))]
        while len(XT) < NXT:
            XT.append(XT[0])
        r_XT = [Res("xt%d" % i) for i in range(NXT)]
        xn = sbt("xn", [128, 2, D], BF16)
        r_xn = [Res("xn0"), Res("xn1")]
        hnT = sbt("hnT", [128, KC, GN], BF16)
        r_hnT = [Res("hnT%d" % k) for k in range(KC)]
        g2bc = [sbt("g2bc%d" % i, [128, D], F32) for i in range(2)]
        r_g2bc = [Res("g2bc0"), Res("g2bc1")]
        fgb = sbt("fgb", [128, D], F32)
        r_fgb = Res("fgb")
        ident = sbt("ident", [128, 128], BF16)
        mask2 = sbt("mask2", [128, 128], BF16)
        ones_bf = sbt("ones_bf", [128, 128], BF16)
        RM = sbt("RM", [128, 4 * GN], BF16)
        r_const = Res("const")
        wbd = sbt("wbd", [128, 2, 4, 128], BF16)
        r_wbd = Res("wbd")
        pp = sbt("pp", [128, NPP], F32)
        r_pp = Res("pp")
        prm = sbt("prm", [128, 64], F32)
        r_prm = Res("prm")
        cT = sbt("cTs", [128, KC, 3], F32)
        cTt = sbt("cTt", [128, KC, 3], F32)
        scb = sbt("scb", [128, KC, 3], BF16)
        r_cT = Res("cT")
        modFM = sbt("modFM", [128, 4, KC, 3], F32)
        r_mod = [Res("mod%d" % i) for i in range(4)]
        b48 = sbt("b48", [128, 48], F32)
        r_b48 = Res("b48")
        grow = fgb[0:3, :]
        r_grow = r_fgb
        selP = sbt("selP", [3, 128], F32)
        selS = sbt("selS", [3, 128], F32)
        lbt = sbt("lbt", [128, 2, 4], F32)
        r_lbt = Res("lbt")
        stat = sbt("stat", [128, 16], F32)
        r_stat = [Res("stat%d" % i) for i in range(8)]
        nhalf = sbt("nhalf", [128, 1], F32)
        sm = sbt("sm", [128, 3, 16], F32)
        r_sm = Res("sm")
        hc = sbt("hc", [128, 3, 4], F32)
        r_hc = [Res("hc%d" % c) for c in range(4)]
        cbo = sbt("cbo", [128, 3, 4, 3], F32)
        r_cbo = Res("cbo")

        PB = [pst("pb%d" % i, [128, 512], F32) for i in range(7)]
        PB7 = pst("pb7", [128, 1024], BF16)
        r_PB = [Res("PB%d" % i) for i in range(8)]
        OSK = _os.environ.get("K_OSK", "1") == "1"
        A_BANKS = (0, 1, 2) if (_os.environ.get("K_FILL", "1") != "1" or OSK) else (0, 1)
        V_BANKS = (3, 4)
        BANKCFG = _os.environ.get("K_BANKS", "base")
        O_BANKS = (6,) if OSK else (5,)
        S_BANK, K_BANK, T_BANK = 6, 6, 7
        if BANKCFG == "o2":
            O_BANKS = (5, 6)
            S_BANK, K_BANK = 7, 7
            PB.append(PB7[:, :].bitcast(F32))
        rot = {"A": 0, "V": 0, "tmp": 0, "RL": 0}

        def nxt(k, n):
            v = rot[k]
            rot[k] = (v + 1) % n
            return v

        def bankA():
            return A_BANKS[nxt("A", len(A_BANKS))]

        def bankV():
            return V_BANKS[nxt("V", 2)]

        def nel(ap):
            n = 1
            for d in ap.shape[1:]:
                n *= d
            return n

        def mm(out, lhsT, rhs, start, stop, reads, writes, skip=False):
            if _os.environ.get("K_MMWARM", "1") == "1":
                c = max(0.1, nel(rhs) * 0.00042 + 0.003)
            else:
                c = max(0.035, nel(rhs) * 0.00052 + 0.012)
            if rhs.dtype == F32:
                c *= 4
            if skip:
                S.op("pe", lambda e: e.matmul(out, lhsT=lhsT, rhs=rhs, start=start, stop=stop, skip_group_check=True), reads, writes, cost=c)
            else:
                S.op("pe", lambda e: e.matmul(out, lhsT=lhsT, rhs=rhs, start=start, stop=stop), reads, writes, cost=c)

        def tr(out, in_, reads, writes):
            S.op("pe", lambda e: e.transpose(out, in_, ident[:]), list(reads) + [r_const], writes, cost=0.08)

        TSET = {AF.Tanh: "A", AF.Ln: "L"}

        def act(out, in_, func, reads, writes, scale=1.0, bias=0.0, accum=None, after=()):
            c = 0.25 + nel(out) / 1200.0
            ts_ = TSET.get(func)
            if accum is None:
                S.op("act", lambda e: e.activation(out=out, in_=in_, func=func, bias=bias, scale=scale), reads, writes, cost=c, tset=ts_, after=after)
            else:
                S.op("act", lambda e: e.activation(out=out, in_=in_, func=func, bias=bias, scale=scale, accum_out=accum), reads, writes, cost=c + 0.1, tset=ts_)

        def ecost(eng, n, kind):
            if eng == "pool":
                return 0.12 + n * (0.00105 if kind == "ts" else 0.0023)
            return 0.09 + n / 960.0

        def tsc(eng, out, in0, s1, s2, op0, op1, reads, writes):
            c = ecost(eng, nel(out), "ts")
            if s2 is None:
                S.op(eng, lambda e: e.tensor_scalar(out=out, in0=in0, scalar1=s1, scalar2=None, op0=op0), reads, writes, cost=c)
            else:
                S.op(eng, lambda e: e.tensor_scalar(out=out, in0=in0, scalar1=s1, scalar2=s2, op0=op0, op1=op1), reads, writes, cost=c)

        def stt(out, in0, scalar, in1, op0, op1, reads, writes):
            S.op("dve", lambda e: e.scalar_tensor_tensor(out=out, in0=in0, scalar=scalar, in1=in1, op0=op0, op1=op1), reads, writes,
                 cost=ecost("dve", nel(out), "tt"))

        def tt(eng, out, in0, in1, op, reads, writes):
            c = ecost(eng, nel(out), "tt")
            if op == ALU.pow:
                c = 0.3 + nel(out) * 0.166
            S.op(eng, lambda e: e.tensor_tensor(out=out, in0=in0, in1=in1, op=op), reads, writes, cost=c)

        def cp(eng, out, in_, reads, writes):
            S.op(eng, lambda e: e.tensor_copy(out=out, in_=in_), reads, writes, cost=ecost(eng, nel(out), "ts"))

        def mset(eng, ap, val, writes):
            S.op(eng, lambda e: e.memset(ap, val), (), writes, cost=0.1 + nel(ap) * 0.001)

        def scan(out, d0, d1, init, reads, writes):
            S.op("dve", lambda e: e.tensor_tensor_scan(out=out, data0=d0, data1=d1, initial=init, op0=ALU.mult, op1=ALU.add), reads, writes,
                 cost=0.1 + nel(out) / 960.0)

        dcount = [0]

        wq = []
        WQ_WIN = int(_os.environ.get("K_WQ", "5"))

        def dma(eng, out, in_, reads, writes, sem=None, slow=False, after=()):
            if sem is None:
                sem = "d%d" % dcount[0]
                dcount[0] += 1
            S.dsem(sem)
            nbytes = 4.0 * out.shape[0] * nel(out)
            c = 2.0 + nbytes / 150e3
            atok = ()
            big = (eng == "pool" and nbytes >= 500e3)
            if big and len(wq) >= WQ_WIN:
                atok = (wq[-WQ_WIN],)
            if slow:
                tok = S.dma(eng, lambda e: e.dma_start(out=out, in_=in_, allow_slow_non_contiguous=True), sem, reads, writes, after, cost=c,
                            nbytes=nbytes, after_tok=atok)
            else:
                tok = S.dma(eng, lambda e: e.dma_start(out=out, in_=in_), sem, reads, writes, after, cost=c, nbytes=nbytes, after_tok=atok)
            if big:
                wq.append(tok)
            return tok

        dma("sp", pp[:], pp_d, [], [r_pp])
        dma("sp", cT[:], cT_d, [], [r_cT])
        dma("sp", lbt[:], lbl_d, [], [r_lbt])
        dma("sp", b48[:], b48_d, [], [r_b48])
        xsems = ["xt%d" % i for i in range(NXT)]

        def load_x(gi, src):
            G = groups[gi]
            nb = G["N"] // 128
            buf = gi % NXT
            dma("sp", XT[buf][:, 0:nb, :], src[G["tok0"]:G["tok0"] + G["N"], :].rearrange("(b p) d -> p b d", p=128),
                [], [r_XT[buf]], sem=xsems[buf])

        load_x(0, xs)
        w_in_v = w_in_d.rearrange("(k p) n -> p k n", p=128)
        w_ada_v = w_ada_d.rearrange("(k p) n -> p k n", p=128)
        w_out_v = w_out_d.rearrange("(k p) n -> p k n", p=128)
        NPIECE = 6 * D // WAP
        ada_loaded = [0]

        STG = ["TQ", "TF", "LF", "BB", "RA", "RB", "RC", "RD"]

        def ada_buf(pi):
            if pi < 8:
                v, rl = FAM[STG[pi]][0]
                vb = v.rearrange("p h n -> p (h n)").bitcast(BF16).rearrange("p (k n) -> p k n", k=KC)
                return vb, list(rl), "wadas%d" % pi
            return wada[pi % 2], [r_wada[pi % 2]], "wada%d" % (pi % 2)

        def load_ada(pi):
            vb, rl, sem = ada_buf(pi)
            dma("pool", vb[:, :, :], w_ada_v[:, :, pi * WAP:(pi + 1) * WAP], [], rl, sem=sem)

        for pi in range(8):
            load_ada(pi)
        ada_loaded[0] = 8

        mset("pool", ident[:], 1.0, [r_const])
        S.op("pool", lambda e: e.affine_select(out=ident[:], in_=ident[:], pattern=[[-1, 128]], compare_op=ALU.is_equal,
                                               fill=0.0, base=0, channel_multiplier=1), [r_const], [r_const])
        mset("pool", mask2[:], 1.0, [r_const])
        S.op("pool", lambda e: e.affine_select(out=mask2[:], in_=mask2[:], pattern=[[1, 128]], compare_op=ALU.is_ge,
                                               fill=0.0, base=0, channel_multiplier=-1), [r_const], [r_const])
        mset("pool", mask2[0:64, 64:128], 0.0, [r_const])
        mset("pool", ones_bf[:], 1.0, [r_const])
        mset("pool", RM[:], 1.0, [r_const])
        mset("pool", RM[:].rearrange("p (c t) -> p c t", t=64)[:, :, 0:1], 0.0, [r_const])
        mset("pool", nhalf[:], -0.5, [r_const])
        mset("pool", selP[:], 1.0, [r_const])
        S.op("pool", lambda e: e.affine_select(out=selP[:], in_=selP[:], pattern=[[0, 128]], compare_op=ALU.is_ge,
                                               fill=0.0, base=0, channel_multiplier=-1), [r_const], [r_const])
        mset("pool", selS[:], 1.0, [r_const])
        S.op("pool", lambda e: e.affine_select(out=selS[:], in_=selS[:], pattern=[[1, 128]], compare_op=ALU.is_ge,
                                               fill=0.0, base=64, channel_multiplier=-64), [r_const], [r_const])
        S.op("pool", lambda e: e.affine_select(out=selS[:], in_=selS[:], pattern=[[-1, 128]], compare_op=ALU.is_ge,
                                               fill=0.0, base=-1, channel_multiplier=64), [r_const], [r_const])
        mset("pool", hc[:], 0.0, r_hc)
        mset("pool", XB[:], 0.0, r_XB)
        mset("pool", Sst[:, 0, :, :], 0.0, r_S[0])
        mset("pool", wbd[:], 0.0, [r_wbd])
        for gi_, src in enumerate((rg_wa_d, rg_wx_d)):
            v = src.rearrange("(c two) i o -> two i c o", two=2)
            dma("pool", wbd[0:64, gi_, :, 0:64], v[0], [], [r_wbd], sem="ld_wbd")
            dma("pool", wbd[64:128, gi_, :, 64:128], v[1], [], [r_wbd], sem="ld_wbd")
        if with_sample:
            dma("sp", hc[:, 1:3, :], h0_d, [], r_hc)
            dma("sp", Sst[:, 1:3, :, :], s_hg_d.rearrange("s h d v -> d s h v"), [], r_S[1] + r_S[2])
        for t in range(2):
            dma("sp", g1bc[t][:, :], b_ada_d[2 * D:3 * D].partition_broadcast(128), [], [r_g1bc[t]])
            dma("sp", g2bc[t][:, :], b_ada_d[5 * D:6 * D].partition_broadcast(128), [], [r_g2bc[t]])

        P_A, P_B, P_NB, P_M4, P_M8, P_HBA, P_HBX, P_GC, P_T = 0, 4, 8, 12, 16, 20, 24, 28, 32
        PP_GAIN, PP_CW, PP_CB, PP_BA, PP_BX, PP_LAM = 0, 4, 20, 24, 28, 32
        tt("dve", prm[:, P_T:P_T + 4], lbt[:, 0, :], lbt[:, 1, :], ALU.subtract, [r_lbt], [r_prm])
        act(prm[:, P_T:P_T + 4], prm[:, P_T:P_T + 4], AF.Tanh, [r_prm], [r_prm], scale=0.5)
        tsc("dve", prm[:, P_A:P_A + 4], prm[:, P_T:P_T + 4], 0.25, 0.75, ALU.mult, ALU.add, [r_prm], [r_prm])
        tsc("dve", prm[:, P_B:P_B + 4], prm[:, P_T:P_T + 4], -0.25, 0.25, ALU.mult, ALU.add, [r_prm], [r_prm])
        tsc("dve", prm[:, P_NB:P_NB + 4], prm[:, P_T:P_T + 4], 0.25, -0.25, ALU.mult, ALU.add, [r_prm], [r_prm])
        act(prm[:, P_T + 4:P_T + 8], pp[:, PP_LAM:PP_LAM + 4], AF.Exp, [r_pp, r_prm], [r_prm], scale=-1.0)
        act(prm[:, P_T + 4:P_T + 8], prm[:, P_T + 4:P_T + 8], AF.Ln, [r_prm], [r_prm], bias=1.0)
        tsc("dve", prm[:, P_M4:P_M4 + 4], prm[:, P_T + 4:P_T + 8], -4.0, None, ALU.mult, None, [r_prm], [r_prm])
        tsc("dve", prm[:, P_M8:P_M8 + 4], prm[:, P_T + 4:P_T + 8], -8.0, None, ALU.mult, None, [r_prm], [r_prm])
        tsc("dve", prm[:, P_HBA:P_HBA + 4], pp[:, PP_BA:PP_BA + 4], 0.5, None, ALU.mult, None, [r_pp, r_prm], [r_prm])
        tsc("dve", prm[:, P_HBX:P_HBX + 4], pp[:, PP_BX:PP_BX + 4], 0.5, None, ALU.mult, None, [r_pp, r_prm], [r_prm])
        tsc("dve", prm[:, P_GC:P_GC + 4], pp[:, PP_GAIN:PP_GAIN + 4], 0.5, None, ALU.mult, None, [r_pp, r_prm], [r_prm])
        act(cTt[:], cT[:], AF.Tanh, [r_cT], [r_cT], scale=0.5)
        stt(cTt[:].rearrange("p k s -> p (k s)"), cTt[:].rearrange("p k s -> p (k s)"), 1.0,
            cT[:].rearrange("p k s -> p (k s)"), ALU.add, ALU.mult, [r_cT], [r_cT])
        tsc("dve", scb[:], cTt[:], 0.5, None, ALU.mult, None, [r_cT], [r_cT])

        def ada_piece(pi):
            wbuf, r_wb, _ = ada_buf(pi)
            col0 = pi * WAP
            which = col0 // D
            cin = col0 % D
            if which in (0, 1, 3, 4):
                mi = {0: 0, 1: 1, 3: 2, 4: 3}[which]
                a = bankA()
                nfc = WAP // 128
                for fc in range(nfc):
                    o = PB[a][:, fc * 3:fc * 3 + 3]
                    for k in range(KC):
                        mm(o, wbuf[:, k, fc * 128:(fc + 1) * 128], scb[:, k, :], k == 0, k == KC - 1,
                           r_wb + [r_cT], [r_PB[a]])
                fc0 = cin // 128
                dst = modFM[:, mi, fc0:fc0 + nfc, :]
                src = PB[a][:, 0:nfc * 3].rearrange("p (f s) -> p f s", s=3)
                ci = col0 // 128
                bsrc = b48[:, ci:ci + nfc].unsqueeze(2).to_broadcast([128, nfc, 3])
                tt("dve", dst, src, bsrc, ALU.add, [r_b48], [r_mod[mi], r_PB[a]])
                if which in (1, 4):
                    tsc("dve", dst, dst, 1.0, 32.0, ALU.add, ALU.mult, [r_mod[mi]], [r_mod[mi]])
            else:
                v = bankV()
                o = PB[v][0:3, 0:WAP]
                for k in range(KC):
                    mm(o, scb[:, k, :], wbuf[:, k, :], k == 0, k == KC - 1, r_wb + [r_cT], [r_PB[v]])
                act(grow[:, cin:cin + WAP], o, AF.Copy, [], [r_grow, r_PB[v]])
            if pi >= 8 and ada_loaded[0] < NPIECE:
                load_ada(ada_loaded[0])
                ada_loaded[0] += 1

        def bcast_g(dst, r_dst):
            for t, sel in ((0, selP), (1, selS)):
                for half in range(2):
                    v = bankV()
                    hs = slice(half * 512, (half + 1) * 512)
                    mm(PB[v][:, :], sel[:, :], grow[:, hs], True, True, [r_grow, r_const], [r_PB[v]])
                    tt("dve", dst[t][:, hs], PB[v][:, :], dst[t][:, hs], ALU.add, [], [r_dst[t], r_PB[v]])

        PPW = D // WAP
        for pi in range(0, 2 * PPW):
            ada_piece(pi)

        for i in (1, 0, 2):
            dma("pool", wi[:, :, i * 1024:(i + 1) * 1024], w_in_v[:, :, i * 1024:(i + 1) * 1024], [], [r_wi[2 * i], r_wi[2 * i + 1]],
                sem="wi%d" % i)
        dma("pool", wo[:, :, :], w_out_v[:, :, :], [], [r_wo[0], r_wo[1]], sem="wo0")

        load_ada(8)
        load_ada(9)
        ada_loaded[0] = 10

        PT = PB7
        r_T = r_PB[T_BANK]

        hnT2 = arenaB[:, 44544:44544 + KC * GN].rearrange("p (k n) -> p k n", k=KC)
        r_hnT2 = [Res("hnT2_%d" % k) for k in range(KC)]

        def norm_and_transpose(gi, xt, r_xt, mi_sh, mi_sc, stat0, hnT=hnT, r_hnT=r_hnT, after=()):
            G = groups[gi]
            N = G["N"]
            nb = N // 128
            for b in range(nb):
                ss = stat[:, stat0 + b:stat0 + b + 1]
                rs = stat[:, stat0 + 2 + b:stat0 + 3 + b]
                rst = r_stat[stat0 // 2 + b]
                act(xn[:, b, :], xt[:, b, :], AF.Square, [r_xt], [r_xn[b], rst], accum=ss)
                tsc("pool", rs, ss, 1024.0 * EPS, 0.0, ALU.add, ALU.add, [], [rst])
                tt("pool", rs, rs, nhalf[:], ALU.pow, [r_const], [rst])
                tsc("dve", xn[:, b, :], xt[:, b, :], rs, None, ALU.mult, None, [r_xt, rst], [r_xn[b]])
            for c4 in range(2):
                for cc in range(4):
                    c = c4 * 4 + cc
                    for b in range(nb):
                        tr(PT[:, cc * 256 + b * 128:cc * 256 + (b + 1) * 128], xn[:, b, c * 128:(c + 1) * 128], [r_xn[b]], [r_T])
                for cc in range(4):
                    c = c4 * 4 + cc
                    for (s, c0, L) in G["segs"]:
                        act(hnT[:, c, c0:c0 + L], PT[:, cc * 256 + c0:cc * 256 + c0 + L], AF.Identity,
                            [r_mod[mi_sh], r_mod[mi_sc]], [r_hnT[c], r_T],
                            scale=modFM[:, mi_sc, c, s:s + 1], bias=modFM[:, mi_sh, c, s:s + 1], after=after)

        def proj_pair(cols, N, w, r_w_of_col, hnT=hnT, r_hnT=r_hnT):
            a = bankA()
            for i, col0 in enumerate(cols):
                for k in range(KC):
                    mm(PB[a][:, i * 256:i * 256 + N], w[:, k, col0:col0 + 128], hnT[:, k, 0:N], k == 0, k == KC - 1,
                       [r_w_of_col(col0), r_hnT[k]], [r_PB[a]])
            return a

        def phase1(gi):
            par = gi % 2
            TQ, r_TQ = FAM["TQ"][par]
            TF, r_TF = FAM["TF"][par]
            LF, r_LF = FAM["LF"][par]
            BB, r_BB = FAM["BB"][par]
            TO, r_TO = FAM["TO"][par]
            RA, r_RA = FAM["RA"][par]
            RB, r_RB = FAM["RB"][par]
            RC, r_RC = FAM["RC"][par]
            RD, r_RD = FAM["RD"][par]
            RE, r_RE = FAM["RE"][par]
            QT, r_QT = FAM["QT"][par]
            KT, r_KT = FAM["KT"][par]
            OSQ, r_OSQ = FAM["OSQ"][par]
            XCB, r_XCB = FAM["XCB"][par]
            vTM, r_vTM = FAM["vTM"][par]
            kTT, r_kTT = FAM["kTT"][par]
            mixT, r_mix = FAM["mixT"][par]
            S0m, r_S0m = FAM["S0m"][par]
            scT, r_scT = FAM["scT"][par]
            G = groups[gi]
            N = G["N"]
            nb = N // 128
            nch = N // 64
            segs = G["segs"]
            buf = gi % NXT
            xt = XT[buf]
            r_xt = r_XT[buf]
            chunk_seq = []
            for (s, c0, L) in segs:
                chunk_seq += [s] * (L // 64)
            if gi + 1 < NGRP:
                load_x(gi + 1, xs)
            if G["tok0"] == SEQ:
                for si, (s, c0, L) in enumerate(segs):
                    base = 0 if si == 0 else 128
                    dma("sp", XB[:, :, base:base + 3], cc0_d[:, s - 1, :, :], [], r_XB, sem="ld_cc")

            S.stage = "g%d:A" % gi
            norm_and_transpose(gi, xt, r_xt, 0, 1, 0)
            rwi = lambda col: r_wi[col // 512]

            S.stage = "g%d:%s" % (gi, "iv")
            for b in range(nb):
                v = bankV()
                for k in range(KC):
                    mm(PB[v][:, :], hnT[:, k, b * 128:(b + 1) * 128], wi[:, k, 1024:1536], k == 0, k == KC - 1,
                       [r_wi[2], r_hnT[k]], [r_PB[v]])
                act(vTM[:, b, :], PB[v][:, :], AF.Copy, [], [r_vTM[b], r_PB[v]])
            S.stage = "g%d:%s" % (gi, "fq")
            FULL = (N == GN)

            def pairv(buf, hp):
                return buf[:, 2 * hp:2 * hp + 2, :].rearrange("p i n -> p (i n)")

            for hp in range(2):
                a = proj_pair([512 + (2 * hp + i) * 128 for i in range(2)], N, wi, rwi)
                if FULL:
                    act(pairv(TF, hp), PB[a][:, :], AF.Tanh, [], r_TF[2 * hp:2 * hp + 2] + [r_PB[a]], scale=0.5)
                    continue
                for i in range(2):
                    h = 2 * hp + i
                    act(TF[:, h, 0:N], PB[a][:, i * 256:i * 256 + N], AF.Tanh, [], [r_TF[h], r_PB[a]], scale=0.5)
            for hp in range(2):
                a = proj_pair([(2 * hp + i) * 128 for i in range(2)], N, wi, rwi)
                if FULL:
                    rr = r_TQ[2 * hp:2 * hp + 2] + [r_PB[a]]
                    act(pairv(TQ, hp), PB[a][:, :], AF.Tanh, [], rr, scale=0.5)
                    stt(pairv(TQ, hp), pairv(TQ, hp), 1.0, PB[a][:, :], ALU.add, ALU.mult, [], rr)
                    continue
                for i in range(2):
                    h = 2 * hp + i
                    ps = PB[a][:, i * 256:i * 256 + N]
                    act(TQ[:, h, 0:N], ps, AF.Tanh, [], [r_TQ[h], r_PB[a]], scale=0.5)
                    stt(TQ[:, h, 0:N], TQ[:, h, 0:N], 1.0, ps, ALU.add, ALU.mult, [], [r_TQ[h], r_PB[a]])
            S.stage = "g%d:%s" % (gi, "xr")
            for cp_ in range(2):
                a = proj_pair([2048 + (2 * cp_ + i) * 128 for i in range(2)], N, wi, rwi)
                if FULL:
                    act(XB[:, 2 * cp_:2 * cp_ + 2, 3:3 + N], PB[a][:, :].rearrange("p (i n) -> p i n", i=2), AF.Copy, [],
                        r_XB[2 * cp_:2 * cp_ + 2] + [r_PB[a]])
                for i in range(2):
                    if FULL:
                        break
                    c = 2 * cp_ + i
                    for si, (s, c0, L) in enumerate(segs):
                        base = 0 if si == 0 else 128
                        act(XB[:, c, base + 3:base + 3 + L], PB[a][:, i * 256 + c0:i * 256 + c0 + L], AF.Copy, [], [r_XB[c], r_PB[a]])
                for i in range(2):
                    c = 2 * cp_ + i
                    for si, (s, c0, L) in enumerate(segs):
                        base = 0 if si == 0 else 128
                        xc = RA[:, c, c0:c0 + L]
                        tsc("dve", xc, XB[:, c, base:base + L], pp[:, PP_CW + c * 4:PP_CW + c * 4 + 1], pp[:, PP_CB + c:PP_CB + c + 1],
                            ALU.mult, ALU.add, [r_XB[c], r_pp], [r_RA[c]])
                        for j in range(1, 4):
                            stt(xc, XB[:, c, base + j:base + j + L], pp[:, PP_CW + c * 4 + j:PP_CW + c * 4 + j + 1], xc,
                                ALU.mult, ALU.add, [r_XB[c], r_pp], [r_RA[c]])
                        if G["last"]:
                            cp("pool", cbo[:, s, c, :], XB[:, c, base + L:base + L + 3], [r_XB[c]], [r_cbo])
                        else:
                            cp("pool", XB[:, c, 0:3], XB[:, c, L:L + 3], [], [r_XB[c]])
                    cp("pool", XCB[:, c, 0:N], RA[:, c, 0:N], [r_RA[c]], [r_XCB[c]])
            S.stage = "g%d:%s" % (gi, "gates")
            for c in range(4):
                a = bankA()
                mm(PB[a][:, 0:N], wbd[:, 0, c, :], XCB[:, c, 0:N], True, True, [r_wbd, r_XCB[c]], [r_PB[a]])
                mm(PB[a][:, 256:256 + N], wbd[:, 1, c, :], XCB[:, c, 0:N], True, True, [r_wbd, r_XCB[c]], [r_PB[a]])
                act(RB[:, c, 0:N], PB[a][:, 0:N], AF.Tanh, [r_prm], [r_RB[c], r_PB[a]], scale=0.5, bias=prm[:, P_HBA + c:P_HBA + c + 1])
                act(RC[:, c, 0:N], PB[a][:, 256:256 + N], AF.Tanh, [r_prm], [r_RC[c], r_PB[a]], scale=0.5, bias=prm[:, P_HBX + c:P_HBX + c + 1])
                act(RD[:, c, 0:N], RB[:, c, 0:N], AF.Exp, [r_RB[c], r_prm], [r_RD[c]],
                    scale=prm[:, P_M4 + c:P_M4 + c + 1], bias=prm[:, P_M4 + c:P_M4 + c + 1])
                act(RB[:, c, 0:N], RB[:, c, 0:N], AF.Exp, [r_prm], [r_RB[c]],
                    scale=prm[:, P_M8 + c:P_M8 + c + 1], bias=prm[:, P_M8 + c:P_M8 + c + 1])
                stt(RC[:, c, 0:N], RC[:, c, 0:N], 1.0, RA[:, c, 0:N], ALU.add, ALU.mult, [r_RA[c]], [r_RC[c]])
            S.stage = "g%d:%s" % (gi, "gr")
            for cp_ in range(2):
                a = proj_pair([2560 + (2 * cp_ + i) * 128 for i in range(2)], N, wi, rwi)
                if FULL:
                    act(pairv(RE, cp_), PB[a][:, :], AF.Copy, [], r_RE[2 * cp_:2 * cp_ + 2] + [r_PB[a]])
                    act(pairv(RA, cp_), PB[a][:, :], AF.Square, [], r_RA[2 * cp_:2 * cp_ + 2] + [r_PB[a]])
                    if cp_ == 1:
                        raf = RA[:].rearrange("p c n -> p (c n)")
                        ref = RE[:].rearrange("p c n -> p (c n)")
                        stt(raf, raf, 0.044715, ref, ALU.mult, ALU.mult, r_RE, r_RA)
                        tt("pool", raf, raf, ref, ALU.add, r_RE, r_RA)
                        act(raf, raf, AF.Tanh, [], r_RA, scale=0.7978845608)
                        stt(ref, raf, 1.0, ref, ALU.add, ALU.mult, r_RA, r_RE)
                    continue
                for i in range(2):
                    c = 2 * cp_ + i
                    ps = PB[a][:, i * 256:i * 256 + N]
                    act(RE[:, c, 0:N], ps, AF.Copy, [], [r_RE[c], r_PB[a]])
                    act(RA[:, c, 0:N], ps, AF.Square, [], [r_RA[c], r_PB[a]])
                for i in range(2):
                    c = 2 * cp_ + i
                    stt(RA[:, c, 0:N], RA[:, c, 0:N], 0.044715, RE[:, c, 0:N], ALU.mult, ALU.mult, [r_RE[c]], [r_RA[c]])
                    tt("dve", RA[:, c, 0:N], RA[:, c, 0:N], RE[:, c, 0:N], ALU.add, [r_RE[c]], [r_RA[c]])
                    act(RA[:, c, 0:N], RA[:, c, 0:N], AF.Tanh, [], [r_RA[c]], scale=0.7978845608)
                    stt(RE[:, c, 0:N], RA[:, c, 0:N], 1.0, RE[:, c, 0:N], ALU.add, ALU.mult, [r_RA[c]], [r_RE[c]])
            S.stage = "g%d:%s" % (gi, "og")
            for hp in range(2):
                a = proj_pair([1536 + (2 * hp + i) * 128 for i in range(2)], N, wi, rwi)
                if FULL:
                    rr = r_TO[2 * hp:2 * hp + 2] + [r_PB[a]]
                    act(pairv(TO, hp), PB[a][:, :], AF.Tanh, [], rr, scale=0.5)
                    stt(pairv(TO, hp), pairv(TO, hp), 1.0, PB[a][:, :], ALU.add, ALU.mult, [], rr)
                    continue
                for i in range(2):
                    h = 2 * hp + i
                    ps = PB[a][:, i * 256:i * 256 + N]
                    act(TO[:, h, 0:N], ps, AF.Tanh, [], [r_TO[h], r_PB[a]], scale=0.5)
                    stt(TO[:, h, 0:N], TO[:, h, 0:N], 1.0, ps, ALU.add, ALU.mult, [], [r_TO[h], r_PB[a]])

            S.stage = "g%d:%s" % (gi, "ln")
            for h in range(4):
                act(LF[:, h, 0:N], TF[:, h, 0:N], AF.Ln, [r_TF[h], r_prm], [r_LF[h]],
                    scale=prm[:, P_B + h:P_B + h + 1], bias=prm[:, P_A + h:P_A + h + 1])
                tsc("dve", TF[:, h, 0:N], TF[:, h, 0:N], prm[:, P_NB + h:P_NB + h + 1], prm[:, P_B + h:P_B + h + 1],
                    ALU.mult, ALU.add, [r_prm], [r_TF[h]])
            if N == GN:
                scan(BB[:].rearrange("p h n -> p (h n)"), RM[:], LF[:].rearrange("p h n -> p (h n)"), 0.0,
                     r_LF + [r_const], r_BB)
            else:
                for h in range(4):
                    scan(BB[:, h, 0:N], RM[:, 0:N], LF[:, h, 0:N], 0.0, [r_LF[h], r_const], [r_BB[h]])
            bv = BB[:, :, 0:N].rearrange("p h (c t) -> p h c t", t=64)
            lv = LF[:, :, 0:N].rearrange("p h (c t) -> p h c t", t=64)
            smv = sm[:].rearrange("p a (h c) -> p a h c", h=4)
            if N == GN:
                bw = BB[:].rearrange("p h (c t) -> p (h c) t", t=64)
                lw = LF[:].rearrange("p h (c t) -> p (h c) t", t=64)
                tt("dve", lw, bw, bw[:, :, 31:32].to_broadcast([128, 4 * nch, 64]), ALU.subtract, r_BB, r_LF)
                act(sm[:, 0, :], bw[:, :, 31], AF.Exp, r_BB, [r_sm])
                act(sm[:, 1, :], bw[:, :, 63], AF.Exp, r_BB, [r_sm])
                act(sm[:, 2, :], lw[:, :, 63], AF.Exp, r_LF, [r_sm])
                bf = BB[:].rearrange("p h n -> p (h n)")
                lf = LF[:].rearrange("p h n -> p (h n)")
                act(bf, lf, AF.Exp, r_LF, r_BB)
                act(lf, lf, AF.Exp, [], r_LF, scale=-1.0)
                tt("dve", QT[:].rearrange("p h n -> p (h n)"), TQ[:].rearrange("p h n -> p (h n)"), bf, ALU.mult, r_TQ + r_BB, r_QT)
                tt("dve", KT[:].rearrange("p h n -> p (h n)"), TF[:].rearrange("p h n -> p (h n)"), lf, ALU.mult, r_TF + r_LF, r_KT)
            else:
                for h in range(4):
                    tt("dve", lv[:, h], bv[:, h], bv[:, h, :, 31:32].to_broadcast([128, nch, 64]), ALU.subtract,
                       [r_BB[h]], [r_LF[h]])
                for h in range(4):
                    act(smv[:, 0, h, 0:nch], bv[:, h, :, 31], AF.Exp, [r_BB[h]], [r_sm])
                    act(smv[:, 1, h, 0:nch], bv[:, h, :, 63], AF.Exp, [r_BB[h]], [r_sm])
                    act(smv[:, 2, h, 0:nch], lv[:, h, :, 63], AF.Exp, [r_LF[h]], [r_sm])
                for h in range(4):
                    act(BB[:, h, 0:N], LF[:, h, 0:N], AF.Exp, [r_LF[h]], [r_BB[h]])
                    act(LF[:, h, 0:N], LF[:, h, 0:N], AF.Exp, [], [r_LF[h]], scale=-1.0)
                    tt("dve", QT[:, h, 0:N], TQ[:, h, 0:N], BB[:, h, 0:N], ALU.mult, [r_TQ[h], r_BB[h]], [r_QT[h]])
                    tt("dve", KT[:, h, 0:N], TF[:, h, 0:N], LF[:, h, 0:N], ALU.mult, [r_TF[h], r_LF[h]], [r_KT[h]])
            S.stage = "g%d:%s" % (gi, "rgln")
            if FULL:
                rbf = RB[:].rearrange("p c n -> p (c n)")
                rcf = RC[:].rearrange("p c n -> p (c n)")
                act(rbf, rbf, AF.Ln, [], r_RB, scale=-1.0, bias=1.0)
                act(rbf, rbf, AF.Exp, [], r_RB, scale=0.5)
                stt(rcf, rbf, 0.5, rcf, ALU.mult, ALU.mult, r_RB, r_RC)
            for c in range(4):
                if not FULL:
                    act(RB[:, c, 0:N], RB[:, c, 0:N], AF.Ln, [], [r_RB[c]], scale=-1.0, bias=1.0)
                    act(RB[:, c, 0:N], RB[:, c, 0:N], AF.Exp, [], [r_RB[c]], scale=0.5)
                    stt(RC[:, c, 0:N], RB[:, c, 0:N], 0.5, RC[:, c, 0:N], ALU.mult, ALU.mult, [r_RB[c]], [r_RC[c]])
                for (s, c0, L) in segs:
                    scan(RB[:, c, c0:c0 + L], RD[:, c, c0:c0 + L], RC[:, c, c0:c0 + L], hc[:, s, c:c + 1],
                         [r_RD[c], r_RC[c], r_hc[c]], [r_RB[c]])
                    cp("pool", hc[:, s, c:c + 1], RB[:, c, c0 + L - 1:c0 + L], [r_RB[c]], [r_hc[c]])
                if not FULL:
                    stt(mixT[:, 4 + c, 0:N], RB[:, c, 0:N], 0.5, RE[:, c, 0:N], ALU.mult, ALU.mult, [r_RB[c], r_RE[c]], [r_mix[4 + c]])
            if FULL:
                stt(mixT[:, 4:8, :].rearrange("p c n -> p (c n)"), RB[:].rearrange("p c n -> p (c n)"), 0.5,
                    RE[:].rearrange("p c n -> p (c n)"), ALU.mult, ALU.mult, r_RB + r_RE, r_mix[4:8])

            S.stage = "g%d:%s" % (gi, "kT")
            for h in range(4):
                for b in range(nb):
                    tr(PT[:, (h * nb + b) * 128:(h * nb + b + 1) * 128], KT[:, h, b * 128:(b + 1) * 128], [r_KT[h]], [r_T])
            act(kTT[:, :, 0:nb, :], PT[:, 0:4 * nb * 128].rearrange("p (h b n) -> p h b n", h=4, b=nb), AF.Copy, [], r_kTT + [r_T])
            S.stage = "g%d:%s" % (gi, "state")
            rK = r_PB[K_BANK]
            for j in range(nch):
                s = chunk_seq[j]
                b = j // 2
                p0 = (j % 2) * 64
                for h in range(4):
                    mm(PB[K_BANK][:, h * 128:(h + 1) * 128], kTT[p0:p0 + 64, h, b, :], vTM[p0:p0 + 64, b, h * 128:(h + 1) * 128],
                       True, True, [r_kTT[h], r_vTM[b]], [rK])
                for h in range(4):
                    tsc("pool", S0m[:, j, h, :], Sst[:, s, h, :], smv[:, 0, h, j:j + 1], 0.0, ALU.mult, ALU.add,
                        [r_S[s][h], r_sm], [r_S0m[j][h]])
                    tsc("pool", Sst[:, s, h, :], Sst[:, s, h, :], smv[:, 1, h, j:j + 1], 0.0, ALU.mult, ALU.add,
                        [r_sm], [r_S[s][h]])
                    stt(Sst[:, s, h, :], PB[K_BANK][:, h * 128:(h + 1) * 128], smv[:, 2, h, j:j + 1], Sst[:, s, h, :],
                        ALU.mult, ALU.add, [r_sm], [r_S[s][h], rK])
            S.stage = "g%d:%s" % (gi, "scores")
            rS = r_PB[S_BANK]
            for b in range(nb):
                cs = slice(b * 128, (b + 1) * 128)
                for h in range(4):
                    mm(PB[S_BANK][:, h * 128:(h + 1) * 128], KT[:, h, cs], QT[:, h, cs], True, True, [r_KT[h], r_QT[h]], [rS])
                tt("dve", scT[b][:, :, :], PB[S_BANK][:, :].rearrange("p (h t) -> p h t", h=4),
                   mask2[:].unsqueeze(1).to_broadcast([128, 4, 128]), ALU.mult, [r_const], [r_scT[b], rS])
            S.stage = "g%d:%s" % (gi, "o")
            for hp in range(2):
                O_BANK = O_BANKS[hp % len(O_BANKS)]
                rO = r_PB[O_BANK]
                for i in range(2):
                    h = 2 * hp + i
                    for b in range(nb):
                        cs = slice(i * 256 + b * 128, i * 256 + (b + 1) * 128)
                        mm(PB[O_BANK][:, cs], vTM[:, b, h * 128:(h + 1) * 128], scT[b][:, h, :], True, True,
                           [r_vTM[b], r_scT[b]], [rO])
                        for jj in range(2):
                            j = b * 2 + jj
                            mm(PB[O_BANK][:, i * 256 + j * 64:i * 256 + (j + 1) * 64], S0m[:, j, h, :], QT[:, h, j * 64:(j + 1) * 64],
                               False, True, [r_S0m[j][h], r_QT[h]], [rO], skip=True)
                if FULL:
                    act(pairv(BB, hp), PB[O_BANK][:, :], AF.Copy, [], r_BB[2 * hp:2 * hp + 2] + [rO])
                    act(pairv(OSQ, hp), PB[O_BANK][:, :], AF.Square, [], r_OSQ[2 * hp:2 * hp + 2] + [rO])
                for i in range(2):
                    if FULL:
                        break
                    h = 2 * hp + i
                    ps = PB[O_BANK][:, i * 256:i * 256 + N]
                    act(BB[:, h, 0:N], ps, AF.Copy, [], [r_BB[h], rO])
                    act(OSQ[:, h, 0:N], ps, AF.Square, [], [r_OSQ[h], rO])
                a = bankA()
                for i in range(2):
                    h = 2 * hp + i
                    mm(PB[a][:, i * 256:i * 256 + N], ones_bf[:], OSQ[:, h, 0:N], True, True, [r_const, r_OSQ[h]], [r_PB[a]])
                if FULL:
                    rl = r_LF[2 * hp:2 * hp + 2]
                    act(pairv(LF, hp), PB[a][:, :], AF.Ln, [], rl + [r_PB[a]], scale=1.0 / 128.0, bias=4.0 * EPS)
                    act(pairv(LF, hp), pairv(LF, hp), AF.Exp, [], rl, scale=-0.5)
                    tt("dve", pairv(BB, hp), pairv(BB, hp), pairv(LF, hp), ALU.mult, rl, r_BB[2 * hp:2 * hp + 2])
                for i in range(2):
                    h = 2 * hp + i
                    if not FULL:
                        act(LF[:, h, 0:N], PB[a][:, i * 256:i * 256 + N], AF.Ln, [], [r_LF[h], r_PB[a]], scale=1.0 / 128.0, bias=4.0 * EPS)
                        act(LF[:, h, 0:N], LF[:, h, 0:N], AF.Exp, [], [r_LF[h]], scale=-0.5)
                        tt("dve", BB[:, h, 0:N], BB[:, h, 0:N], LF[:, h, 0:N], ALU.mult, [r_LF[h]], [r_BB[h]])
                    stt(mixT[:, h, 0:N], BB[:, h, 0:N], prm[:, P_GC + h:P_GC + h + 1], TO[:, h, 0:N], ALU.mult, ALU.mult,
                        [r_BB[h], r_prm, r_TO[h]], [r_mix[h]])
            if G["last"]:
                for s in sorted(set(chunk_seq)):
                    final_tokens.append(dma("sp", S_out_d[s].rearrange("h d v -> d h v"), Sst[:, s, :, :], r_S[s], [], sem="st_S%d" % s))

            S.stage = "g%d:%s" % (gi, "outproj")
            gb = g1bc[0] if G["tok0"] < SEQ else g1bc[1]
            r_gb = r_g1bc[0] if G["tok0"] < SEQ else r_g1bc[1]
            for b in range(nb):
                for half in range(2):
                    v = bankV()
                    hs = slice(half * 512, (half + 1) * 512)
                    for k in range(KC):
                        mm(PB[v][:, :], mixT[:, k, b * 128:(b + 1) * 128], wo[:, k, hs], k == 0, k == KC - 1,
                           [r_mix[k], r_wo[half]], [r_PB[v]])
                    tt("dve", tmp1[:, :], PB[v][:, :], gb[:, hs], ALU.mult, [r_gb], [r_tmp1, r_PB[v]])
                    tt(_os.environ.get("K_RESENG", "pool"), xt[:, b, hs], xt[:, b, hs], tmp1[:, :], ALU.add, [r_tmp1], [r_xt])
            dma("sp", x1_d[G["tok0"]:G["tok0"] + N, :].rearrange("(b p) d -> p b d", p=128), xt[:, 0:nb, :],
                [r_xt], [r_x1d[gi]], sem="x1s%d" % buf)

        r_x1d = [Res("x1d%d" % g) for g in range(NGRP)]

        for pi in range(2 * PPW, 3 * PPW):
            ada_piece(pi)
        bcast_g(g1bc, r_g1bc)

        for gi in range(NGRP):
            phase1(gi)
            if gi == 0:
                for pi in range(3 * PPW, 6 * PPW):
                    ada_piece(pi)
                bcast_g(g2bc, r_g2bc)
                dma("sp", fgb[:], fgain_d.partition_broadcast(128), [], [r_fgb])
                tsc("dve", fgb[:], fgb[:], 32.0, None, ALU.mult, None, [r_fgb], [r_fgb])

        final_tokens.append(dma("sp", h_out_d, hc[:], r_hc, [], sem="st_h"))
        final_tokens.append(dma("sp", cb_out_d, cbo[:], [r_cbo], [], sem="st_cb"))

        r_wu = [Res("wu%d" % i) for i in range(8)]
        r_wd = [Res("wd%d" % i) for i in range(8)]
        r_hT = [Res("hT%d" % f) for f in range(32)]
        r_RL = [Res("RL%d" % i) for i in range(3)]
        r_tmp2 = [Res("tmp2_%d" % i) for i in range(2)]
        w_up_v = w_up_d.rearrange("(k p) n -> p k n", p=128)
        w_down_v = w_down_d.rearrange("(f p) n -> p f n", p=128)
        for i in range(4):
            dma("pool", wu[:, :, i * 1024:(i + 1) * 1024], w_up_v[:, :, i * 1024:(i + 1) * 1024], [], [r_wu[2 * i], r_wu[2 * i + 1]],
                sem="wu%d" % i, after=resB)
        for i in range(8):
            dma("pool", wd[:, i * 4:(i + 1) * 4, :], w_down_v[:, i * 4:(i + 1) * 4, :], [], [r_wd[i]], sem="wd%d" % i,
                after=(r_wi if i < 6 else r_wo))

        def load_x1(gi):
            G = groups[gi]
            nb = G["N"] // 128
            buf = (NGRP + gi) % NXT
            dma("sp", XT[buf][:, 0:nb, :], x1_d[G["tok0"]:G["tok0"] + G["N"], :].rearrange("(b p) d -> p b d", p=128),
                [r_x1d[gi]], [r_XT[buf]], sem=xsems[buf])

        load_x1(0)

        def phase2(gi):
            G = groups[gi]
            N = G["N"]
            nb = N // 128
            buf = (NGRP + gi) % NXT
            xt = XT[buf]
            r_xt = r_XT[buf]
            if gi + 1 < NGRP:
                load_x1(gi + 1)
            hb, r_hb = (hnT, r_hnT) if gi % 2 == 0 else (hnT2, r_hnT2)
            norm_and_transpose(gi, xt, r_xt, 2, 3, 4, hb, r_hb, after=(resB if gi == 1 else ()))
            rwu = lambda col: r_wu[col // 512]
            for fp in range(16):
                a = proj_pair([(2 * fp + i) * 128 for i in range(2)], N, wu, rwu, hb, r_hb)
                rl = nxt("RL", 3)
                if N == GN:
                    act(RL[rl][:, :], PB[a][:, :], AF.Relu, [], [r_RL[rl], r_PB[a]])
                    tt("pool", hT[:, 2 * fp:2 * fp + 2, :], RL[rl][:, :].rearrange("p (f n) -> p f n", f=2),
                       RL[rl][:, :].rearrange("p (f n) -> p f n", f=2), ALU.mult, [r_RL[rl]], [r_hT[2 * fp], r_hT[2 * fp + 1]])
                else:
                    for i in range(2):
                        act(RL[rl][:, i * 256:i * 256 + N], PB[a][:, i * 256:i * 256 + N], AF.Relu, [], [r_RL[rl], r_PB[a]])
                        tt("pool", hT[:, 2 * fp + i, 0:N], RL[rl][:, i * 256:i * 256 + N], RL[rl][:, i * 256:i * 256 + N],
                           ALU.mult, [r_RL[rl]], [r_hT[2 * fp + i]])
            gb = g2bc[0] if G["tok0"] < SEQ else g2bc[1]
            r_gb = r_g2bc[0] if G["tok0"] < SEQ else r_g2bc[1]
            for b in range(nb):
                for half in range(2):
                    v = bankV()
                    hs = slice(half * 512, (half + 1) * 512)
                    for fc in range(32):
                        mm(PB[v][:, :], hT[:, fc, b * 128:(b + 1) * 128], wd[:, fc, hs], fc == 0, fc == 31,
                           [r_hT[fc], r_wd[fc // 4]], [r_PB[v]])
                    t = nxt("tmp", 2)
                    tt("dve", tmp2[t][:, :], PB[v][:, :], gb[:, hs], ALU.mult, [r_gb], [r_tmp2[t], r_PB[v]])
                    tt("pool", xt[:, b, hs], xt[:, b, hs], tmp2[t][:, :], ALU.add, [r_tmp2[t]], [r_xt])
                ss = stat[:, 8 + b:9 + b]
                rs = stat[:, 10 + b:11 + b]
                jk = nxt("tmp", 2)
                act(tmp2[jk][:, :].bitcast(BF16), xt[:, b, :], AF.Square, [r_xt], [r_tmp2[jk], r_stat[4 + b]], accum=ss)
                tsc("pool", rs, ss, 1024.0 * EPS, 0.0, ALU.add, ALU.add, [], [r_stat[4 + b]])
                tt("pool", rs, rs, nhalf[:], ALU.pow, [r_const], [r_stat[4 + b]])
                stt(xt[:, b, :], xt[:, b, :], rs, fgb[:], ALU.mult, ALU.mult, [r_stat[4 + b], r_fgb], [r_xt])
            tok = dma("sp", y_d[G["tok0"]:G["tok0"] + N, :].rearrange("(b p) d -> p b d", p=128), xt[:, 0:nb, :],
                      [r_xt], [], sem="ys%d" % buf)
            final_tokens.append(tok)

        for gi in range(NGRP):
            phase2(gi)

        if _os.environ.get("K_FILL", "1") == "1":
            t_end_p1 = 0.0
            fsrc = RM[:, 0:256]
            fdst = PB[5 if OSK else 2][:, 0:256]
            S.filler = (lambda e: e.matmul(fdst, lhsT=ident[:], rhs=fsrc, start=True, stop=True), r_const.w, 0.22,
                        float(_os.environ.get("K_FILL_LO", "150")), float(_os.environ.get("K_FILL_HI", "1250")),
                        float(_os.environ.get("K_FILL_GMIN", "1.0")))
        S.schedule()
        build_nc.last_sched = S
        S.emit(st, final_tokens=final_tokens)
    return nc


_NC_CACHE = {}


def _get_nc():
    if "nc" not in _NC_CACHE:
        _NC_CACHE["nc"] = build_nc()
    return _NC_CACHE["nc"]


def kernel(x_prompt, x_sample, c_prompt, c_sample, state_hgrn, state_rglru, cache_conv,
           hg_lb_logits, w_ada, b_ada, w_in, hg_norm_gain, conv_w, conv_b,
           rg_wa, rg_ba, rg_wx, rg_bx, rg_lambda, w_out, w_up, w_down, final_gain):
    f = lambda a: np.ascontiguousarray(np.asarray(a, dtype=np.float32))
    x_prompt, x_sample, c_prompt, c_sample = f(x_prompt), f(x_sample), f(c_prompt), f(c_sample)
    state_hgrn, state_rglru, cache_conv = f(state_hgrn), f(state_rglru), f(cache_conv)
    n = 8

    def chan(v, nchunk):
        return np.asarray(v, np.float32).reshape(nchunk, 128).T

    pp = np.zeros((128, NPP), np.float32)
    pp[:, 0:4] = chan(hg_norm_gain[0], 4)
    cw = np.asarray(conv_w[0], np.float32)
    for c in range(4):
        for j in range(4):
            pp[:, 4 + c * 4 + j] = cw[j, c * 128:(c + 1) * 128]
    pp[:, 20:24] = chan(conv_b[0], 4)
    pp[:, 24:28] = chan(rg_ba[0], 4)
    pp[:, 28:32] = chan(rg_bx[0], 4)
    pp[:, 32:36] = chan(rg_lambda[0], 4)
    lbl = f(np.asarray(hg_lb_logits, np.float32).reshape(2, 4, 128).transpose(2, 0, 1))
    shared = {
        "lbl": lbl, "w_ada": f(w_ada[0]), "b_ada": f(np.asarray(b_ada[0]).reshape(-1)), "b48": f(np.asarray(b_ada[0], np.float32).reshape(48, 128).T), "w_in": f(w_in[0]),
        "pp": pp, "rg_wa": f(rg_wa[0]), "rg_wx": f(rg_wx[0]), "w_out": f(w_out[0]), "w_up": f(w_up[0]),
        "w_down": f(w_down[0]), "fgain": f(final_gain),
    }
    in_maps = []
    for i in range(n):
        xs_i = np.concatenate([x_prompt[i], x_sample[2 * i], x_sample[2 * i + 1]], axis=0)
        cs = np.stack([c_prompt[i], c_sample[2 * i], c_sample[2 * i + 1]], axis=0)
        cT = f(cs.reshape(3, KC, 128).transpose(2, 1, 0))
        s_hg = f(state_hgrn[0, 2 * i:2 * i + 2])
        h0 = f(state_rglru[0, 2 * i:2 * i + 2].reshape(2, 4, 128).transpose(2, 0, 1))
        cc0 = f(cache_conv[0, 2 * i:2 * i + 2].reshape(2, 3, 4, 128).transpose(3, 0, 2, 1))
        m = dict(shared)
        m.update({"xs": f(xs_i), "cT": cT, "s_hg": s_hg, "h0": h0, "cc0": cc0})
        in_maps.append(m)
    nc = _get_nc()
    res = run_bass_kernel_spmd(nc, in_maps, core_ids=list(range(n)))
    R = res.results
    y_prompt = np.stack([R[i]["y"][0:SEQ] for i in range(n)], axis=0)
    y_sample = np.stack([R[i]["y"][SEQ + 64 * j:SEQ + 64 * (j + 1)] for i in range(n) for j in range(2)], axis=0)
    S_p = np.stack([R[i]["S_out"][0] for i in range(n)], axis=0)[None]
    S_s = np.stack([R[i]["S_out"][1 + j] for i in range(n) for j in range(2)], axis=0)[None]

    def hvec(a):
        return a.T.reshape(512)

    h_p = np.stack([hvec(R[i]["h_out"][:, 0, :]) for i in range(n)], axis=0)[None]
    h_s = np.stack([hvec(R[i]["h_out"][:, 1 + j, :]) for i in range(n) for j in range(2)], axis=0)[None]

    def cbm(a):
        return a.transpose(2, 1, 0).reshape(3, 512)

    cb_p = np.stack([cbm(R[i]["cb_out"][:, 0]) for i in range(n)], axis=0)[None]
    cb_s = np.stack([cbm(R[i]["cb_out"][:, 1 + j]) for i in range(n) for j in range(2)], axis=0)[None]
    outs = (y_prompt, y_sample, S_p, h_p, cb_p, S_s, h_s, cb_s)
    return tuple(np.ascontiguousarray(o, dtype=np.float32) for o in outs)
```

```python
import numpy as np
from contextlib import ExitStack
import concourse.bass as bass
import concourse.mybir as mybir
from concourse.bass_utils import run_bass_kernel_spmd

F32 = mybir.dt.float32
BF16 = mybir.dt.bfloat16
ALU = mybir.AluOpType
AF = mybir.ActivationFunctionType

ENGS = ("pe", "act", "dve", "pool", "sp")


class Res:
    __slots__ = ("name", "w", "r")

    def __init__(self, name=""):
        self.name = name
        self.w = None
        self.r = []


class _Op:
    __slots__ = ("eng", "fn", "deps", "signal", "sigval", "dsem", "dval", "cost", "tset", "idx", "order_deps",
                 "npred", "succ", "dr", "fin", "start", "dcost", "desc", "bind", "nbytes", "tail")

    def __init__(self, eng, fn):
        self.eng = eng
        self.fn = fn
        self.deps = set()
        self.signal = False
        self.sigval = 0
        self.dsem = None
        self.dval = 0
        self.cost = 0.3
        self.tset = None
        self.idx = 0
        self.order_deps = []
        self.dcost = 0.0


import os as _os0
ATTACH_WAIT = _os0.environ.get("K_ATTACH", "1") == "1"
SEM_LAT = float(_os0.environ.get("K_SEMLAT", "0.15"))
TSWITCH = float(_os0.environ.get("K_TSW", "2.0"))


class Sched:
    def __init__(self, nc):
        self.nc = nc
        self.ops = {e: [] for e in ENGS}
        self.all_ops = []
        self.dma_tot = {}
        self.dma_last = {}
        self.dma_sems = {}
        self.stage = ""
        self.filler = None

    def _collect(self, op, reads, writes, after=()):
        eng = op.eng
        skip_self = eng in ("pe", "sp")

        def add(t):
            if t is None:
                return
            if skip_self and t[0] == "op" and t[1].eng == eng:
                op.order_deps.append(t[1])
                return
            op.deps.add(t)

        for r in reads:
            add(r.w)
        for w in after:
            add(w.w)
            for t in w.r:
                add(t)
        for w in writes:
            add(w.w)
            for t in w.r:
                add(t)

    def op(self, eng, fn, reads=(), writes=(), cost=0.3, tset=None, after=()):
        o = _Op(eng, fn)
        o.cost = cost
        o.tset = tset
        self._collect(o, reads, writes, after)
        o.idx = len(self.all_ops)
        o.desc = self.stage
        self.all_ops.append(o)
        self.ops[eng].append(o)
        tok = ("op", o)
        for r in reads:
            r.r.append(tok)
        for w in writes:
            w.w = tok
            w.r = []
        return tok

    def dma(self, eng, fn, sem, reads=(), writes=(), after=(), cost=3.0, nbytes=0.0, after_tok=()):
        o = _Op(eng, fn)
        o.cost = 1.06 if eng == "pool" else 0.06
        o.dcost = cost
        o.nbytes = nbytes
        self._collect(o, reads, writes, after)
        for t in after_tok:
            o.deps.add(t)
        o.idx = len(self.all_ops)
        o.desc = self.stage + ":dma:" + sem
        self.all_ops.append(o)
        self.ops[eng].append(o)
        tot = self.dma_tot[sem] + 16
        self.dma_tot[sem] = tot
        prev = self.dma_last.get(sem)
        if prev is not None:
            assert prev.eng == eng
            o.order_deps.append(prev)
        self.dma_last[sem] = o
        o.dsem = sem
        o.dval = tot
        tok = ("dma", o, sem, tot)
        for r in reads:
            r.r.append(tok)
        for w in writes:
            w.w = tok
            w.r = []
        return tok

    def dsem(self, name):
        if name not in self.dma_tot:
            self.dma_tot[name] = 0
        return name

    def schedule(self):
        import heapq
        ops = self.all_ops
        for o in ops:
            o.succ = []
            o.fin = None
            o.start = None
        for o in ops:
            preds = set(t[1] for t in o.deps) | set(o.order_deps)
            o.npred = len(preds)
            for p in preds:
                p.succ.append(o)
        PRIO = _os0.environ.get("K_PRIO", "1") == "1"
        for o in reversed(ops):
            t = 0.0
            for sc in o.succ:
                v = sc.tail + SEM_LAT
                if v > t:
                    t = v
            o.tail = t + o.cost + (o.dcost if o.dsem is not None else 0.0)
        fut = {e: [] for e in ENGS}
        avail = {e: [] for e in ENGS}

        def data_ready(o):
            dr = 0.0
            for t in o.deps:
                p = t[1]
                f = p.fin + SEM_LAT
                if f > dr:
                    dr = f
            for p in o.order_deps:
                if p.start > dr:
                    dr = p.start
            return dr

        def prio(o):
            return -o.tail if PRIO else o.idx

        for o in ops:
            if o.npred == 0:
                o.dr = 0.0
                heapq.heappush(fut[o.eng], (0.0, o.idx, o))
        free = {e: 0.0 for e in ENGS}
        dma_free = [0.0]
        DMA_BW = 230e3
        cur_set = [None]
        new_order = {e: [] for e in ENGS}
        n_done = 0
        total = len(ops)
        EPS = float(_os0.environ.get("K_EPS", "0.02"))
        while n_done < total:
            best_e = None
            best_t = None
            for e in ENGS:
                if avail[e]:
                    t_e = free[e]
                    if fut[e] and fut[e][0][0] < t_e:
                        pass
                elif fut[e]:
                    t_e = max(free[e], fut[e][0][0])
                else:
                    continue
                if best_t is None or t_e < best_t:
                    best_t = t_e
                    best_e = e
            e = best_e
            t_e = best_t
            f = fut[e]
            while f and f[0][0] <= t_e + EPS:
                dr, idx, o = heapq.heappop(f)
                heapq.heappush(avail[e], (prio(o), idx, o))
            a = avail[e]
            if e == "act" and cur_set[0] is not None:
                cands = heapq.nsmallest(8, a)
                pick = cands[0]
                if pick[2].tset is not None and pick[2].tset != cur_set[0]:
                    for c in cands[1:]:
                        if (c[2].tset is None or c[2].tset == cur_set[0]) and c[0] - pick[0] < float(_os0.environ.get("K_SWTH", "10.0")):
                            pick = c
                            break
                if pick is a[0]:
                    heapq.heappop(a)
                else:
                    a.remove(pick)
                    heapq.heapify(a)
                o = pick[2]
            else:
                o = heapq.heappop(a)[2]
            est = max(free[e], o.dr)
            if e == "act" and o.tset is not None:
                if cur_set[0] is not None and o.tset != cur_set[0]:
                    est += TSWITCH
                cur_set[0] = o.tset
            o.start = est
            o.bind = None
            free[e] = est + o.cost
            if o.dsem is not None:
                xs_ = max(est + o.cost + 0.8, dma_free[0])
                dma_free[0] = xs_ + o.nbytes / DMA_BW
                o.fin = dma_free[0] + 1.2
            else:
                o.fin = est + o.cost
            new_order[e].append(o)
            n_done += 1
            for sc in o.succ:
                sc.npred -= 1
                if sc.npred == 0:
                    sc.dr = data_ready(sc)
                    heapq.heappush(fut[sc.eng], (sc.dr, sc.idx, sc))
        if self.filler is not None:
            fn, ftok, fcost, t_lo, t_hi, gmin = self.filler
            pe = []
            prev_end = 0.0
            nfill = 0
            for o in new_order["pe"]:
                gap = o.start - prev_end
                if t_lo < o.start < t_hi and gap >= gmin:
                    n = int((gap - 0.35) / fcost)
                    for _ in range(max(0, n)):
                        f = _Op("pe", fn)
                        f.cost = fcost
                        f.deps = set([ftok])
                        f.desc = "filler"
                        f.start = prev_end
                        f.fin = prev_end + fcost
                        pe.append(f)
                        nfill += 1
                pe.append(o)
                prev_end = o.start + o.cost
            new_order["pe"] = pe
            self.nfill = nfill
        self.ops = new_order
        self.est_time = max(o.fin for o in ops)
        self.est_busy = {e: sum(o.cost for o in new_order[e]) for e in ENGS}

    def emit(self, stack, final_tokens=()):
        nc = self.nc
        esem = {e: stack.enter_context(nc.semaphore("es_" + e)) for e in ENGS}
        for name in self.dma_tot:
            self.dma_sems[name] = stack.enter_context(nc.semaphore("ds_" + name))
        for e in ENGS:
            for o in self.ops[e]:
                for t in o.deps:
                    if t[0] == "op":
                        t[1].signal = True
        for t in final_tokens:
            if t[0] == "op":
                t[1].signal = True
        for e in ENGS:
            c = 0
            for o in self.ops[e]:
                if o.signal:
                    c += 1
                    o.sigval = c

        def tokkey(t):
            if t[0] == "op":
                return ("e", t[1].eng), t[1].sigval
            return ("d", t[2]), t[3]

        PRUNE = _os0.environ.get("K_PRUNE", "1") == "1"
        know = {}
        if PRUNE:
            allops = []
            for e in ENGS:
                allops.extend(self.ops[e])
            allops.sort(key=lambda o: o.start)
            last_on = {}
            for o in allops:
                k = dict(last_on.get(o.eng, {}))
                for t in o.deps:
                    key, val = tokkey(t)
                    if val > k.get(key, 0):
                        k[key] = val
                    kd = know.get(id(t[1]))
                    if kd:
                        for kk_, vv_ in kd.items():
                            if vv_ > k.get(kk_, 0):
                                k[kk_] = vv_
                last_on[o.eng] = k
                k2 = k
                if o.dsem is not None:
                    k2 = dict(k)
                    k2[("d", o.dsem)] = max(k2.get(("d", o.dsem), 0), o.dval)
                elif o.signal:
                    k2 = dict(k)
                    k2[("e", o.eng)] = max(k2.get(("e", o.eng), 0), o.sigval)
                know[id(o)] = k2

        def run_engine(e, engobj, extra_final=None):
            waited = {}
            for o in self.ops[e]:
                need = {}
                needop = {}
                for t in o.deps:
                    key, val = tokkey(t)
                    if val > need.get(key, 0):
                        need[key] = val
                        needop[key] = t[1]
                todo = []
                items = sorted(need.items(), key=lambda kv: -needop[kv[0]].start) if PRUNE else list(need.items())
                for key, val in items:
                    if waited.get(key, 0) >= val:
                        continue
                    waited[key] = val
                    if PRUNE:
                        kd = know.get(id(needop[key]))
                        if kd:
                            for kk_, vv_ in kd.items():
                                if vv_ > waited.get(kk_, 0):
                                    waited[kk_] = vv_
                    s = esem[key[1]] if key[0] == "e" else self.dma_sems[key[1]]
                    todo.append((s, val))
                attach = None
                if todo and o.dsem is None and ATTACH_WAIT:
                    attach = todo.pop()
                for s, val in todo:
                    engobj.wait_ge(s, val)
                inst = o.fn(engobj)
                if attach is not None:
                    inst._wait_ge(attach[0], attach[1])
                if o.dsem is not None:
                    inst.then_inc(self.dma_sems[o.dsem], 16)
                elif o.signal:
                    inst.then_inc(esem[e], 1)
            if extra_final:
                need = {}
                for t in extra_final:
                    key, val = tokkey(t)
                    if val > need.get(key, 0):
                        need[key] = val
                for key, val in need.items():
                    s = esem[key[1]] if key[0] == "e" else self.dma_sems[key[1]]
                    engobj.wait_ge(s, val)

        with nc.Block() as block:
            @block.tensor
            def _(eng):
                run_engine("pe", eng)

            @block.scalar
            def _(eng):
                run_engine("act", eng)

            @block.vector
            def _(eng):
                run_engine("dve", eng)

            @block.gpsimd
            def _(eng):
                run_engine("pool", eng)

            @block.sync
            def _(eng):
                run_engine("sp", eng, extra_final=final_tokens)


D = 1024
KC = 8
SEQ = 4096
NTOK = SEQ + 128
GN = 256
NPG = SEQ // GN
EPS = 1e-6
NPP = 36
import os as _os
NXT = int(_os.environ.get("K_NXT", "3"))
DBL = set(x for x in _os.environ.get("K_DBL", "").split(",") if x)
EXPLORE = _os.environ.get("K_EXPLORE", "") == "1"


def build_nc(n_prompt_groups=NPG, with_sample=True):
    nc = bass.Bass("TRN2", target_bir_lowering=False)
    S = Sched(nc)

    def din(name, shape):
        return nc.dram_tensor(name, list(shape), F32, kind="ExternalInput").ap()

    def dout(name, shape):
        return nc.dram_tensor(name, list(shape), F32, kind="ExternalOutput").ap()

    xs = din("xs", [NTOK, D])
    cT_d = din("cT", [128, KC, 3])
    s_hg_d = din("s_hg", [2, 4, 128, 128])
    h0_d = din("h0", [128, 2, 4])
    cc0_d = din("cc0", [128, 2, 4, 3])
    lbl_d = din("lbl", [128, 2, 4])
    w_ada_d = din("w_ada", [D, 6 * D])
    b_ada_d = din("b_ada", [6 * D])
    b48_d = din("b48", [128, 48])
    w_in_d = din("w_in", [D, 3 * D])
    pp_d = din("pp", [128, NPP])
    rg_wa_d = din("rg_wa", [8, 64, 64])
    rg_wx_d = din("rg_wx", [8, 64, 64])
    w_out_d = din("w_out", [D, D])
    w_up_d = din("w_up", [D, 4 * D])
    w_down_d = din("w_down", [4 * D, D])
    fgain_d = din("fgain", [D])

    y_d = dout("y", [NTOK, D])
    S_out_d = dout("S_out", [3, 4, 128, 128])
    h_out_d = dout("h_out", [128, 3, 4])
    cb_out_d = dout("cb_out", [128, 3, 4, 3])
    x1_d = nc.dram_tensor("x1_scratch", [NTOK, D], F32).ap()

    groups = []
    for g in range(n_prompt_groups):
        groups.append(dict(tok0=g * GN, N=GN, segs=[(0, 0, GN)], first=(g == 0), last=(g == n_prompt_groups - 1)))
    if with_sample:
        groups.append(dict(tok0=SEQ, N=128, segs=[(1, 0, 64), (2, 64, 64)], first=True, last=True))
    NGRP = len(groups)

    final_tokens = []
    with ExitStack() as st:
        def sbt(name, shape, dt):
            return st.enter_context(nc.sbuf_tensor("sb_" + name, list(shape), dt))

        def pst(name, shape, dt):
            return st.enter_context(nc.psum_tensor("ps_" + name, list(shape), dt))

        arenaA = sbt("arenaA", [128, 32768], BF16)
        ARB_E = 46592
        arenaB = sbt("arenaB", [128, ARB_E], BF16)
        resA = []
        resB = []
        offB = [0]

        offB2 = [0]

        def allocB(nelem_bf16, name):
            if EXPLORE and "_b" in name:
                o = offB2[0]
                offB2[0] = o + nelem_bf16
                return arenaB[:, o:o + nelem_bf16]
            o = offB[0]
            offB[0] = o + nelem_bf16
            assert offB[0] <= ARB_E, (name, offB[0])
            return arenaB[:, o:o + nelem_bf16]

        def resB_new(name):
            r = Res(name)
            resB.append(r)
            return r

        wi = arenaA[:, 0:KC * 3072].rearrange("p (k n) -> p k n", k=KC)
        wo = arenaA[:, KC * 3072:KC * 4096].rearrange("p (k n) -> p k n", k=KC)
        wd = arenaA[:, 0:32 * 1024].rearrange("p (f n) -> p f n", f=32)
        r_wi = [Res("wi%d" % i) for i in range(6)]
        r_wo = [Res("wo%d" % i) for i in range(2)]
        resA.extend(r_wi + r_wo)

        wu = arenaB[:, 0:KC * 4096].rearrange("p (k n) -> p k n", k=KC)
        hT = arenaB[:, 32768:32768 + 32 * GN].rearrange("p (f n) -> p f n", f=32)
        RL = [arenaB[:, 40960 + i * 512:40960 + (i + 1) * 512] for i in range(3)]
        tmp2 = [arenaB[:, 42496 + i * 1024:42496 + (i + 1) * 1024].bitcast(F32) for i in range(2)]

        def f32v(n, name):
            return allocB(2 * n, name).bitcast(F32)

        def _h4f(nm):
            return lambda tag: (f32v(4 * GN, nm + tag).rearrange("p (h n) -> p h n", h=4), [resB_new("%s%s%d" % (nm, tag, h)) for h in range(4)])

        def _h4b(nm):
            return lambda tag: (allocB(4 * GN, nm + tag).rearrange("p (h n) -> p h n", h=4), [resB_new("%s%s%d" % (nm, tag, h)) for h in range(4)])

        CTOR = {}
        for nm in ("TQ", "TF", "LF", "BB", "RA", "RB", "RC", "RD", "RE", "TO"):
            CTOR[nm] = _h4f(nm)
        for nm in ("QT", "KT", "OSQ", "XCB"):
            CTOR[nm] = _h4b(nm)
        CTOR["vTM"] = lambda tag: (allocB(2 * 512, "vTM" + tag).rearrange("p (b n) -> p b n", b=2), [resB_new("vTM%s%d" % (tag, b)) for b in range(2)])
        CTOR["kTT"] = lambda tag: (allocB(4 * 2 * 128, "kTT" + tag).rearrange("p (h b n) -> p h b n", h=4, b=2), [resB_new("kTT%s%d" % (tag, h)) for h in range(4)])
        CTOR["mixT"] = lambda tag: (allocB(8 * GN, "mixT" + tag).rearrange("p (k n) -> p k n", k=8), [resB_new("mix%s%d" % (tag, k)) for k in range(8)])
        CTOR["S0m"] = lambda tag: (allocB(4 * 4 * 128, "S0m" + tag).rearrange("p (j h n) -> p j h n", j=4, h=4),
                                   [[resB_new("S0m%s%d_%d" % (tag, j, h)) for h in range(4)] for j in range(4)])
        CTOR["scT"] = lambda tag: ([allocB(512, "scT%s%d" % (tag, i)).rearrange("p (h t) -> p h t", h=4) for i in range(2)],
                                   [resB_new("scT%s%d" % (tag, i)) for i in range(2)])
        FAM = {}
        for nm, ct in CTOR.items():
            a0 = ct("")
            FAM[nm] = [a0, ct("_b") if nm in DBL else a0]
        XBW = 264
        XB = f32v(4 * XBW, "XB").rearrange("p (c n) -> p c n", c=4)
        Sst = f32v(3 * 4 * 128, "Sst").rearrange("p (s h n) -> p s h n", s=3, h=4)
        WAP = 256
        wada = [allocB(KC * WAP, "wada%d" % i).rearrange("p (k n) -> p k n", k=KC) for i in range(2)]
        g1bc = [f32v(1024, "g1bc%d" % i) for i in range(2)]
        tmp1 = f32v(512, "tmp1")

        r_XB = [resB_new("XB%d" % h) for h in range(4)]
        r_S = [[resB_new("S%d_%d" % (s, h)) for h in range(4)] for s in range(3)]
        r_wada = [resB_new("wada%d" % i) for i in range(2)]
        r_g1bc = [resB_new("g1bc%d" % i) for i in range(2)]
        r_tmp1 = resB_new("tmp1")

        XT = [sbt("xt%d" % i, [128, 2, D], F32) for i in range(min(NXT, 2 if EXPLORE else NXT))]
        while len(XT) < NXT:
            XT.append(XT[0])
        r_XT = [Res("xt%d" % i) for i in range(NXT)]
        xn = sbt("xn", [128, 2, D], BF16)
        r_xn = [Res("xn0"), Res("xn1")]
        hnT = sbt("hnT", [128, KC, GN], BF16)
        r_hnT = [Res("hnT%d" % k) for k in range(KC)]
        g2bc = [sbt("g2bc%d" % i, [128, D], F32) for i in range(2)]
        r_g2bc = [Res("g2bc0"), Res("g2bc1")]
        fgb = sbt("fgb", [128, D], F32)
        r_fgb = Res("fgb")
        ident = sbt("ident", [128, 128], BF16)
        mask2 = sbt("mask2", [128, 128], BF16)
        ones_bf = sbt("ones_bf", [128, 128], BF16)
        RM = sbt("RM", [128, 4 * GN], BF16)
        r_const = Res("const")
        wbd = sbt("wbd", [128, 2, 4, 128], BF16)
        r_wbd = Res("wbd")
        pp = sbt("pp", [128, NPP], F32)
        r_pp = Res("pp")
        prm = sbt("prm", [128, 64], F32)
        r_prm = Res("prm")
        cT = sbt("cTs", [128, KC, 3], F32)
        cTt = sbt("cTt", [128, KC, 3], F32)
        scb = sbt("scb", [128, KC, 3], BF16)
        r_cT = Res("cT")
        modFM = sbt("modFM", [128, 4, KC, 3], F32)
        r_mod = [Res("mod%d" % i) for i in range(4)]
        b48 = sbt("b48", [128, 48], F32)
        r_b48 = Res("b48")
        grow = fgb[0:3, :]
        r_grow = r_fgb
        selP = sbt("selP", [3, 128], F32)
        selS = sbt("selS", [3, 128], F32)
        lbt = sbt("lbt", [128, 2, 4], F32)
        r_lbt = Res("lbt")
        stat = sbt("stat", [128, 16], F32)
        r_stat = [Res("stat%d" % i) for i in range(8)]
        nhalf = sbt("nhalf", [128, 1], F32)
        sm = sbt("sm", [128, 3, 16], F32)
        r_sm = Res("sm")
        hc = sbt("hc", [128, 3, 4], F32)
        r_hc = [Res("hc%d" % c) for c in range(4)]
        cbo = sbt("cbo", [128, 3, 4, 3], F32)
        r_cbo = Res("cbo")

        PB = [pst("pb%d" % i, [128, 512], F32) for i in range(7)]
        PB7 = pst("pb7", [128, 1024], BF16)
        r_PB = [Res("PB%d" % i) for i in range(8)]
        OSK = _os.environ.get("K_OSK", "1") == "1"
        A_BANKS = (0, 1, 2) if (_os.environ.get("K_FILL", "1") != "1" or OSK) else (0, 1)
        V_BANKS = (3, 4)
        BANKCFG = _os.environ.get("K_BANKS", "base")
        O_BANKS = (6,) if OSK else (5,)
        S_BANK, K_BANK, T_BANK = 6, 6, 7
        if BANKCFG == "o2":
            O_BANKS = (5, 6)
            S_BANK, K_BANK = 7, 7
            PB.append(PB7[:, :].bitcast(F32))
        rot = {"A": 0, "V": 0, "tmp": 0, "RL": 0}

        def nxt(k, n):
            v = rot[k]
            rot[k] = (v + 1) % n
            return v

        def bankA():
            return A_BANKS[nxt("A", len(A_BANKS))]

        def bankV():
            return V_BANKS[nxt("V", 2)]

        def nel(ap):
            n = 1
            for d in ap.shape[1:]:
                n *= d
            return n

        def mm(out, lhsT, rhs, start, stop, reads, writes, skip=False):
            if _os.environ.get("K_MMWARM", "1") == "1":
                c = max(0.1, nel(rhs) * 0.00042 + 0.003)
            else:
                c = max(0.035, nel(rhs) * 0.00052 + 0.012)
            if rhs.dtype == F32:
                c *= 4
            if skip:
                S.op("pe", lambda e: e.matmul(out, lhsT=lhsT, rhs=rhs, start=start, stop=stop, skip_group_check=True), reads, writes, cost=c)
            else:
                S.op("pe", lambda e: e.matmul(out, lhsT=lhsT, rhs=rhs, start=start, stop=stop), reads, writes, cost=c)

        def tr(out, in_, reads, writes):
            S.op("pe", lambda e: e.transpose(out, in_, ident[:]), list(reads) + [r_const], writes, cost=0.08)

        TSET = {AF.Tanh: "A", AF.Ln: "L"}

        def act(out, in_, func, reads, writes, scale=1.0, bias=0.0, accum=None, after=()):
            c = 0.25 + nel(out) / 1200.0
            ts_ = TSET.get(func)
            if accum is None:
                S.op("act", lambda e: e.activation(out=out, in_=in_, func=func, bias=bias, scale=scale), reads, writes, cost=c, tset=ts_, after=after)
            else:
                S.op("act", lambda e: e.activation(out=out, in_=in_, func=func, bias=bias, scale=scale, accum_out=accum), reads, writes, cost=c + 0.1, tset=ts_)

        def ecost(eng, n, kind):
            if eng == "pool":
                return 0.12 + n * (0.00105 if kind == "ts" else 0.0023)
            return 0.09 + n / 960.0

        def tsc(eng, out, in0, s1, s2, op0, op1, reads, writes):
            c = ecost(eng, nel(out), "ts")
            if s2 is None:
                S.op(eng, lambda e: e.tensor_scalar(out=out, in0=in0, scalar1=s1, scalar2=None, op0=op0), reads, writes, cost=c)
            else:
                S.op(eng, lambda e: e.tensor_scalar(out=out, in0=in0, scalar1=s1, scalar2=s2, op0=op0, op1=op1), reads, writes, cost=c)

        def stt(out, in0, scalar, in1, op0, op1, reads, writes):
            S.op("dve", lambda e: e.scalar_tensor_tensor(out=out, in0=in0, scalar=scalar, in1=in1, op0=op0, op1=op1), reads, writes,
                 cost=ecost("dve", nel(out), "tt"))

        def tt(eng, out, in0, in1, op, reads, writes):
            c = ecost(eng, nel(out), "tt")
            if op == ALU.pow:
                c = 0.3 + nel(out) * 0.166
            S.op(eng, lambda e: e.tensor_tensor(out=out, in0=in0, in1=in1, op=op), reads, writes, cost=c)

        def cp(eng, out, in_, reads, writes):
            S.op(eng, lambda e: e.tensor_copy(out=out, in_=in_), reads, writes, cost=ecost(eng, nel(out), "ts"))

        def mset(eng, ap, val, writes):
            S.op(eng, lambda e: e.memset(ap, val), (), writes, cost=0.1 + nel(ap) * 0.001)

        def scan(out, d0, d1, init, reads, writes):
            S.op("dve", lambda e: e.tensor_tensor_scan(out=out, data0=d0, data1=d1, initial=init, op0=ALU.mult, op1=ALU.add), reads, writes,
                 cost=0.1 + nel(out) / 960.0)

        dcount = [0]

        wq = []
        WQ_WIN = int(_os.environ.get("K_WQ", "5"))

        def dma(eng, out, in_, reads, writes, sem=None, slow=False, after=()):
            if sem is None:
                sem = "d%d" % dcount[0]
                dcount[0] += 1
            S.dsem(sem)
            nbytes = 4.0 * out.shape[0] * nel(out)
            c = 2.0 + nbytes / 150e3
            atok = ()
            big = (eng == "pool" and nbytes >= 500e3)
            if big and len(wq) >= WQ_WIN:
                atok = (wq[-WQ_WIN],)
            if slow:
                tok = S.dma(eng, lambda e: e.dma_start(out=out, in_=in_, allow_slow_non_contiguous=True), sem, reads, writes, after, cost=c,
                            nbytes=nbytes, after_tok=atok)
            else:
                tok = S.dma(eng, lambda e: e.dma_start(out=out, in_=in_), sem, reads, writes, after, cost=c, nbytes=nbytes, after_tok=atok)
            if big:
                wq.append(tok)
            return tok

        dma("sp", pp[:], pp_d, [], [r_pp])
        dma("sp", cT[:], cT_d, [], [r_cT])
        dma("sp", lbt[:], lbl_d, [], [r_lbt])
        dma("sp", b48[:], b48_d, [], [r_b48])
        xsems = ["xt%d" % i for i in range(NXT)]

        def load_x(gi, src):
            G = groups[gi]
            nb = G["N"] // 128
            buf = gi % NXT
            dma("sp", XT[buf][:, 0:nb, :], src[G["tok0"]:G["tok0"] + G["N"], :].rearrange("(b p) d -> p b d", p=128),
                [], [r_XT[buf]], sem=xsems[buf])

        load_x(0, xs)
        w_in_v = w_in_d.rearrange("(k p) n -> p k n", p=128)
        w_ada_v = w_ada_d.rearrange("(k p) n -> p k n", p=128)
        w_out_v = w_out_d.rearrange("(k p) n -> p k n", p=128)
        NPIECE = 6 * D // WAP
        ada_loaded = [0]

        STG = ["TQ", "TF", "LF", "BB", "RA", "RB", "RC", "RD"]

        def ada_buf(pi):
            if pi < 8:
                v, rl = FAM[STG[pi]][0]
                vb = v.rearrange("p h n -> p (h n)").bitcast(BF16).rearrange("p (k n) -> p k n", k=KC)
                return vb, list(rl), "wadas%d" % pi
            return wada[pi % 2], [r_wada[pi % 2]], "wada%d" % (pi % 2)

        def load_ada(pi):
            vb, rl, sem = ada_buf(pi)
            dma("pool", vb[:, :, :], w_ada_v[:, :, pi * WAP:(pi + 1) * WAP], [], rl, sem=sem)

        for pi in range(8):
            load_ada(pi)
        ada_loaded[0] = 8

        mset("pool", ident[:], 1.0, [r_const])
        S.op("pool", lambda e: e.affine_select(out=ident[:], in_=ident[:], pattern=[[-1, 128]], compare_op=ALU.is_equal,
                                               fill=0.0, base=0, channel_multiplier=1), [r_const], [r_const])
        mset("pool", mask2[:], 1.0, [r_const])
        S.op("pool", lambda e: e.affine_select(out=mask2[:], in_=mask2[:], pattern=[[1, 128]], compare_op=ALU.is_ge,
                                               fill=0.0, base=0, channel_multiplier=-1), [r_const], [r_const])
        mset("pool", mask2[0:64, 64:128], 0.0, [r_const])
        mset("pool", ones_bf[:], 1.0, [r_const])
        mset("pool", RM[:], 1.0, [r_const])
        mset("pool", RM[:].rearrange("p (c t) -> p c t", t=64)[:, :, 0:1], 0.0, [r_const])
        mset("pool", nhalf[:], -0.5, [r_const])
        mset("pool", selP[:], 1.0, [r_const])
        S.op("pool", lambda e: e.affine_select(out=selP[:], in_=selP[:], pattern=[[0, 128]], compare_op=ALU.is_ge,
                                               fill=0.0, base=0, channel_multiplier=-1), [r_const], [r_const])
        mset("pool", selS[:], 1.0, [r_const])
        S.op("pool", lambda e: e.affine_select(out=selS[:], in_=selS[:], pattern=[[1, 128]], compare_op=ALU.is_ge,
                                               fill=0.0, base=64, channel_multiplier=-64), [r_const], [r_const])
        S.op("pool", lambda e: e.affine_select(out=selS[:], in_=selS[:], pattern=[[-1, 128]], compare_op=ALU.is_ge,
                                               fill=0.0, base=-1, channel_multiplier=64), [r_const], [r_const])
        mset("pool", hc[:], 0.0, r_hc)
        mset("pool", XB[:], 0.0, r_XB)
        mset("pool", Sst[:, 0, :, :], 0.0, r_S[0])
        mset("pool", wbd[:], 0.0, [r_wbd])
        for gi_, src in enumerate((rg_wa_d, rg_wx_d)):
            v = src.rearrange("(c two) i o -> two i c o", two=2)
            dma("pool", wbd[0:64, gi_, :, 0:64], v[0], [], [r_wbd], sem="ld_wbd")
            dma("pool", wbd[64:128, gi_, :, 64:128], v[1], [], [r_wbd], sem="ld_wbd")
        if with_sample:
            dma("sp", hc[:, 1:3, :], h0_d, [], r_hc)
            dma("sp", Sst[:, 1:3, :, :], s_hg_d.rearrange("s h d v -> d s h v"), [], r_S[1] + r_S[2])
        for t in range(2):
            dma("sp", g1bc[t][:, :], b_ada_d[2 * D:3 * D].partition_broadcast(128), [], [r_g1bc[t]])
            dma("sp", g2bc[t][:, :], b_ada_d[5 * D:6 * D].partition_broadcast(128), [], [r_g2bc[t]])

        P_A, P_B, P_NB, P_M4, P_M8, P_HBA, P_HBX, P_GC, P_T = 0, 4, 8, 12, 16, 20, 24, 28, 32
        PP_GAIN, PP_CW, PP_CB, PP_BA, PP_BX, PP_LAM = 0, 4, 20, 24, 28, 32
        tt("dve", prm[:, P_T:P_T + 4], lbt[:, 0, :], lbt[:, 1, :], ALU.subtract, [r_lbt], [r_prm])
        act(prm[:, P_T:P_T + 4], prm[:, P_T:P_T + 4], AF.Tanh, [r_prm], [r_prm], scale=0.5)
        tsc("dve", prm[:, P_A:P_A + 4], prm[:, P_T:P_T + 4], 0.25, 0.75, ALU.mult, ALU.add, [r_prm], [r_prm])
        tsc("dve", prm[:, P_B:P_B + 4], prm[:, P_T:P_T + 4], -0.25, 0.25, ALU.mult, ALU.add, [r_prm], [r_prm])
        tsc("dve", prm[:, P_NB:P_NB + 4], prm[:, P_T:P_T + 4], 0.25, -0.25, ALU.mult, ALU.add, [r_prm], [r_prm])
        act(prm[:, P_T + 4:P_T + 8], pp[:, PP_LAM:PP_LAM + 4], AF.Exp, [r_pp, r_prm], [r_prm], scale=-1.0)
        act(prm[:, P_T + 4:P_T + 8], prm[:, P_T + 4:P_T + 8], AF.Ln, [r_prm], [r_prm], bias=1.0)
        tsc("dve", prm[:, P_M4:P_M4 + 4], prm[:, P_T + 4:P_T + 8], -4.0, None, ALU.mult, None, [r_prm], [r_prm])
        tsc("dve", prm[:, P_M8:P_M8 + 4], prm[:, P_T + 4:P_T + 8], -8.0, None, ALU.mult, None, [r_prm], [r_prm])
        tsc("dve", prm[:, P_HBA:P_HBA + 4], pp[:, PP_BA:PP_BA + 4], 0.5, None, ALU.mult, None, [r_pp, r_prm], [r_prm])
        tsc("dve", prm[:, P_HBX:P_HBX + 4], pp[:, PP_BX:PP_BX + 4], 0.5, None, ALU.mult, None, [r_pp, r_prm], [r_prm])
        tsc("dve", prm[:, P_GC:P_GC + 4], pp[:, PP_GAIN:PP_GAIN + 4], 0.5, None, ALU.mult, None, [r_pp, r_prm], [r_prm])
        act(cTt[:], cT[:], AF.Tanh, [r_cT], [r_cT], scale=0.5)
        stt(cTt[:].rearrange("p k s -> p (k s)"), cTt[:].rearrange("p k s -> p (k s)"), 1.0,
            cT[:].rearrange("p k s -> p (k s)"), ALU.add, ALU.mult, [r_cT], [r_cT])
        tsc("dve", scb[:], cTt[:], 0.5, None, ALU.mult, None, [r_cT], [r_cT])

        def ada_piece(pi):
            wbuf, r_wb, _ = ada_buf(pi)
            col0 = pi * WAP
            which = col0 // D
            cin = col0 % D
            if which in (0, 1, 3, 4):
                mi = {0: 0, 1: 1, 3: 2, 4: 3}[which]
                a = bankA()
                nfc = WAP // 128
                for fc in range(nfc):
                    o = PB[a][:, fc * 3:fc * 3 + 3]
                    for k in range(KC):
                        mm(o, wbuf[:, k, fc * 128:(fc + 1) * 128], scb[:, k, :], k == 0, k == KC - 1,
                           r_wb + [r_cT], [r_PB[a]])
                fc0 = cin // 128
                dst = modFM[:, mi, fc0:fc0 + nfc, :]
                src = PB[a][:, 0:nfc * 3].rearrange("p (f s) -> p f s", s=3)
                ci = col0 // 128
                bsrc = b48[:, ci:ci + nfc].unsqueeze(2).to_broadcast([128, nfc, 3])
                tt("dve", dst, src, bsrc, ALU.add, [r_b48], [r_mod[mi], r_PB[a]])
                if which in (1, 4):
                    tsc("dve", dst, dst, 1.0, 32.0, ALU.add, ALU.mult, [r_mod[mi]], [r_mod[mi]])
            else:
                v = bankV()
                o = PB[v][0:3, 0:WAP]
                for k in range(KC):
                    mm(o, scb[:, k, :], wbuf[:, k, :], k == 0, k == KC - 1, r_wb + [r_cT], [r_PB[v]])
                act(grow[:, cin:cin + WAP], o, AF.Copy, [], [r_grow, r_PB[v]])
            if pi >= 8 and ada_loaded[0] < NPIECE:
                load_ada(ada_loaded[0])
                ada_loaded[0] += 1

        def bcast_g(dst, r_dst):
            for t, sel in ((0, selP), (1, selS)):
                for half in range(2):
                    v = bankV()
                    hs = slice(half * 512, (half + 1) * 512)
                    mm(PB[v][:, :], sel[:, :], grow[:, hs], True, True, [r_grow, r_const], [r_PB[v]])
                    tt("dve", dst[t][:, hs], PB[v][:, :], dst[t][:, hs], ALU.add, [], [r_dst[t], r_PB[v]])

        PPW = D // WAP
        for pi in range(0, 2 * PPW):
            ada_piece(pi)

        for i in (1, 0, 2):
            dma("pool", wi[:, :, i * 1024:(i + 1) * 1024], w_in_v[:, :, i * 1024:(i + 1) * 1024], [], [r_wi[2 * i], r_wi[2 * i + 1]],
                sem="wi%d" % i)
        dma("pool", wo[:, :, :], w_out_v[:, :, :], [], [r_wo[0], r_wo[1]], sem="wo0")

        load_ada(8)
        load_ada(9)
        ada_loaded[0] = 10

        PT = PB7
        r_T = r_PB[T_BANK]

        hnT2 = arenaB[:, 44544:44544 + KC * GN].rearrange("p (k n) -> p k n", k=KC)
        r_hnT2 = [Res("hnT2_%d" % k) for k in range(KC)]

        def norm_and_transpose(gi, xt, r_xt, mi_sh, mi_sc, stat0, hnT=hnT, r_hnT=r_hnT, after=()):
            G = groups[gi]
            N = G["N"]
            nb = N // 128
            for b in range(nb):
                ss = stat[:, stat0 + b:stat0 + b + 1]
                rs = stat[:, stat0 + 2 + b:stat0 + 3 + b]
                rst = r_stat[stat0 // 2 + b]
                act(xn[:, b, :], xt[:, b, :], AF.Square, [r_xt], [r_xn[b], rst], accum=ss)
                tsc("pool", rs, ss, 1024.0 * EPS, 0.0, ALU.add, ALU.add, [], [rst])
                tt("pool", rs, rs, nhalf[:], ALU.pow, [r_const], [rst])
                tsc("dve", xn[:, b, :], xt[:, b, :], rs, None, ALU.mult, None, [r_xt, rst], [r_xn[b]])
            for c4 in range(2):
                for cc in range(4):
                    c = c4 * 4 + cc
                    for b in range(nb):
                        tr(PT[:, cc * 256 + b * 128:cc * 256 + (b + 1) * 128], xn[:, b, c * 128:(c + 1) * 128], [r_xn[b]], [r_T])
                for cc in range(4):
                    c = c4 * 4 + cc
                    for (s, c0, L) in G["segs"]:
                        act(hnT[:, c, c0:c0 + L], PT[:, cc * 256 + c0:cc * 256 + c0 + L], AF.Identity,
                            [r_mod[mi_sh], r_mod[mi_sc]], [r_hnT[c], r_T],
                            scale=modFM[:, mi_sc, c, s:s + 1], bias=modFM[:, mi_sh, c, s:s + 1], after=after)

        def proj_pair(cols, N, w, r_w_of_col, hnT=hnT, r_hnT=r_hnT):
            a = bankA()
            for i, col0 in enumerate(cols):
                for k in range(KC):
                    mm(PB[a][:, i * 256:i * 256 + N], w[:, k, col0:col0 + 128], hnT[:, k, 0:N], k == 0, k == KC - 1,
                       [r_w_of_col(col0), r_hnT[k]], [r_PB[a]])
            return a

        def phase1(gi):
            par = gi % 2
            TQ, r_TQ = FAM["TQ"][par]
            TF, r_TF = FAM["TF"][par]
            LF, r_LF = FAM["LF"][par]
            BB, r_BB = FAM["BB"][par]
            TO, r_TO = FAM["TO"][par]
            RA, r_RA = FAM["RA"][par]
            RB, r_RB = FAM["RB"][par]
            RC, r_RC = FAM["RC"][par]
            RD, r_RD = FAM["RD"][par]
            RE, r_RE = FAM["RE"][par]
            QT, r_QT = FAM["QT"][par]
            KT, r_KT = FAM["KT"][par]
            OSQ, r_OSQ = FAM["OSQ"][par]
            XCB, r_XCB = FAM["XCB"][par]
            vTM, r_vTM = FAM["vTM"][par]
            kTT, r_kTT = FAM["kTT"][par]
            mixT, r_mix = FAM["mixT"][par]
            S0m, r_S0m = FAM["S0m"][par]
            scT, r_scT = FAM["scT"][par]
            G = groups[gi]
            N = G["N"]
            nb = N // 128
            nch = N // 64
            segs = G["segs"]
            buf = gi % NXT
            xt = XT[buf]
            r_xt = r_XT[buf]
            chunk_seq = []
            for (s, c0, L) in segs:
                chunk_seq += [s] * (L // 64)
            if gi + 1 < NGRP:
                load_x(gi + 1, xs)
            if G["tok0"] == SEQ:
                for si, (s, c0, L) in enumerate(segs):
                    base = 0 if si == 0 else 128
                    dma("sp", XB[:, :, base:base + 3], cc0_d[:, s - 1, :, :], [], r_XB, sem="ld_cc")

            S.stage = "g%d:A" % gi
            norm_and_transpose(gi, xt, r_xt, 0, 1, 0)
            rwi = lambda col: r_wi[col // 512]

            S.stage = "g%d:%s" % (gi, "iv")
            for b in range(nb):
                v = bankV()
                for k in range(KC):
                    mm(PB[v][:, :], hnT[:, k, b * 128:(b + 1) * 128], wi[:, k, 1024:1536], k == 0, k == KC - 1,
                       [r_wi[2], r_hnT[k]], [r_PB[v]])
                act(vTM[:, b, :], PB[v][:, :], AF.Copy, [], [r_vTM[b], r_PB[v]])
            S.stage = "g%d:%s" % (gi, "fq")
            FULL = (N == GN)

            def pairv(buf, hp):
                return buf[:, 2 * hp:2 * hp + 2, :].rearrange("p i n -> p (i n)")

            for hp in range(2):
                a = proj_pair([512 + (2 * hp + i) * 128 for i in range(2)], N, wi, rwi)
                if FULL:
                    act(pairv(TF, hp), PB[a][:, :], AF.Tanh, [], r_TF[2 * hp:2 * hp + 2] + [r_PB[a]], scale=0.5)
                    continue
                for i in range(2):
                    h = 2 * hp + i
                    act(TF[:, h, 0:N], PB[a][:, i * 256:i * 256 + N], AF.Tanh, [], [r_TF[h], r_PB[a]], scale=0.5)
            for hp in range(2):
                a = proj_pair([(2 * hp + i) * 128 for i in range(2)], N, wi, rwi)
                if FULL:
                    rr = r_TQ[2 * hp:2 * hp + 2] + [r_PB[a]]
                    act(pairv(TQ, hp), PB[a][:, :], AF.Tanh, [], rr, scale=0.5)
                    stt(pairv(TQ, hp), pairv(TQ, hp), 1.0, PB[a][:, :], ALU.add, ALU.mult, [], rr)
                    continue
                for i in range(2):
                    h = 2 * hp + i
                    ps = PB[a][:, i * 256:i * 256 + N]
                    act(TQ[:, h, 0:N], ps, AF.Tanh, [], [r_TQ[h], r_PB[a]], scale=0.5)
                    stt(TQ[:, h, 0:N], TQ[:, h, 0:N], 1.0, ps, ALU.add, ALU.mult, [], [r_TQ[h], r_PB[a]])
            S.stage = "g%d:%s" % (gi, "xr")
            for cp_ in range(2):
                a = proj_pair([2048 + (2 * cp_ + i) * 128 for i in range(2)], N, wi, rwi)
                if FULL:
                    act(XB[:, 2 * cp_:2 * cp_ + 2, 3:3 + N], PB[a][:, :].rearrange("p (i n) -> p i n", i=2), AF.Copy, [],
                        r_XB[2 * cp_:2 * cp_ + 2] + [r_PB[a]])
                for i in range(2):
                    if FULL:
                        break
                    c = 2 * cp_ + i
                    for si, (s, c0, L) in enumerate(segs):
                        base = 0 if si == 0 else 128
                        act(XB[:, c, base + 3:base + 3 + L], PB[a][:, i * 256 + c0:i * 256 + c0 + L], AF.Copy, [], [r_XB[c], r_PB[a]])
                for i in range(2):
                    c = 2 * cp_ + i
                    for si, (s, c0, L) in enumerate(segs):
                        base = 0 if si == 0 else 128
                        xc = RA[:, c, c0:c0 + L]
                        tsc("dve", xc, XB[:, c, base:base + L], pp[:, PP_CW + c * 4:PP_CW + c * 4 + 1], pp[:, PP_CB + c:PP_CB + c + 1],
                            ALU.mult, ALU.add, [r_XB[c], r_pp], [r_RA[c]])
                        for j in range(1, 4):
                            stt(xc, XB[:, c, base + j:base + j + L], pp[:, PP_CW + c * 4 + j:PP_CW + c * 4 + j + 1], xc,
                                ALU.mult, ALU.add, [r_XB[c], r_pp], [r_RA[c]])
                        if G["last"]:
                            cp("pool", cbo[:, s, c, :], XB[:, c, base + L:base + L + 3], [r_XB[c]], [r_cbo])
                        else:
                            cp("pool", XB[:, c, 0:3], XB[:, c, L:L + 3], [], [r_XB[c]])
                    cp("pool", XCB[:, c, 0:N], RA[:, c, 0:N], [r_RA[c]], [r_XCB[c]])
            S.stage = "g%d:%s" % (gi, "gates")
            for c in range(4):
                a = bankA()
                mm(PB[a][:, 0:N], wbd[:, 0, c, :], XCB[:, c, 0:N], True, True, [r_wbd, r_XCB[c]], [r_PB[a]])
                mm(PB[a][:, 256:256 + N], wbd[:, 1, c, :], XCB[:, c, 0:N], True, True, [r_wbd, r_XCB[c]], [r_PB[a]])
                act(RB[:, c, 0:N], PB[a][:, 0:N], AF.Tanh, [r_prm], [r_RB[c], r_PB[a]], scale=0.5, bias=prm[:, P_HBA + c:P_HBA + c + 1])
                act(RC[:, c, 0:N], PB[a][:, 256:256 + N], AF.Tanh, [r_prm], [r_RC[c], r_PB[a]], scale=0.5, bias=prm[:, P_HBX + c:P_HBX + c + 1])
                act(RD[:, c, 0:N], RB[:, c, 0:N], AF.Exp, [r_RB[c], r_prm], [r_RD[c]],
                    scale=prm[:, P_M4 + c:P_M4 + c + 1], bias=prm[:, P_M4 + c:P_M4 + c + 1])
                act(RB[:, c, 0:N], RB[:, c, 0:N], AF.Exp, [r_prm], [r_RB[c]],
                    scale=prm[:, P_M8 + c:P_M8 + c + 1], bias=prm[:, P_M8 + c:P_M8 + c + 1])
                stt(RC[:, c, 0:N], RC[:, c, 0:N], 1.0, RA[:, c, 0:N], ALU.add, ALU.mult, [r_RA[c]], [r_RC[c]])
            S.stage = "g%d:%s" % (gi, "gr")
            for cp_ in range(2):
                a = proj_pair([2560 + (2 * cp_ + i) * 128 for i in range(2)], N, wi, rwi)
                if FULL:
                    act(pairv(RE, cp_), PB[a][:, :], AF.Copy, [], r_RE[2 * cp_:2 * cp_ + 2] + [r_PB[a]])
                    act(pairv(RA, cp_), PB[a][:, :], AF.Square, [], r_RA[2 * cp_:2 * cp_ + 2] + [r_PB[a]])
                    if cp_ == 1:
                        raf = RA[:].rearrange("p c n -> p (c n)")
                        ref = RE[:].rearrange("p c n -> p (c n)")
                        stt(raf, raf, 0.044715, ref, ALU.mult, ALU.mult, r_RE, r_RA)
                        tt("pool", raf, raf, ref, ALU.add, r_RE, r_RA)
                        act(raf, raf, AF.Tanh, [], r_RA, scale=0.7978845608)
                        stt(ref, raf, 1.0, ref, ALU.add, ALU.mult, r_RA, r_RE)
                    continue
                for i in range(2):
                    c = 2 * cp_ + i
                    ps = PB[a][:, i * 256:i * 256 + N]
                    act(RE[:, c, 0:N], ps, AF.Copy, [], [r_RE[c], r_PB[a]])
                    act(RA[:, c, 0:N], ps, AF.Square, [], [r_RA[c], r_PB[a]])
                for i in range(2):
                    c = 2 * cp_ + i
                    stt(RA[:, c, 0:N], RA[:, c, 0:N], 0.044715, RE[:, c, 0:N], ALU.mult, ALU.mult, [r_RE[c]], [r_RA[c]])
                    tt("dve", RA[:, c, 0:N], RA[:, c, 0:N], RE[:, c, 0:N], ALU.add, [r_RE[c]], [r_RA[c]])
                    act(RA[:, c, 0:N], RA[:, c, 0:N], AF.Tanh, [], [r_RA[c]], scale=0.7978845608)
                    stt(RE[:, c, 0:N], RA[:, c, 0:N], 1.0, RE[:, c, 0:N], ALU.add, ALU.mult, [r_RA[c]], [r_RE[c]])
            S.stage = "g%d:%s" % (gi, "og")
            for hp in range(2):
                a = proj_pair([1536 + (2 * hp + i) * 128 for i in range(2)], N, wi, rwi)
                if FULL:
                    rr = r_TO[2 * hp:2 * hp + 2] + [r_PB[a]]
                    act(pairv(TO, hp), PB[a][:, :], AF.Tanh, [], rr, scale=0.5)
                    stt(pairv(TO, hp), pairv(TO, hp), 1.0, PB[a][:, :], ALU.add, ALU.mult, [], rr)
                    continue
                for i in range(2):
                    h = 2 * hp + i
                    ps = PB[a][:, i * 256:i * 256 + N]
                    act(TO[:, h, 0:N], ps, AF.Tanh, [], [r_TO[h], r_PB[a]], scale=0.5)
                    stt(TO[:, h, 0:N], TO[:, h, 0:N], 1.0, ps, ALU.add, ALU.mult, [], [r_TO[h], r_PB[a]])

            S.stage = "g%d:%s" % (gi, "ln")
            for h in range(4):
                act(LF[:, h, 0:N], TF[:, h, 0:N], AF.Ln, [r_TF[h], r_prm], [r_LF[h]],
                    scale=prm[:, P_B + h:P_B + h + 1], bias=prm[:, P_A + h:P_A + h + 1])
                tsc("dve", TF[:, h, 0:N], TF[:, h, 0:N], prm[:, P_NB + h:P_NB + h + 1], prm[:, P_B + h:P_B + h + 1],
                    ALU.mult, ALU.add, [r_prm], [r_TF[h]])
            if N == GN:
                scan(BB[:].rearrange("p h n -> p (h n)"), RM[:], LF[:].rearrange("p h n -> p (h n)"), 0.0,
                     r_LF + [r_const], r_BB)
            else:
                for h in range(4):
                    scan(BB[:, h, 0:N], RM[:, 0:N], LF[:, h, 0:N], 0.0, [r_LF[h], r_const], [r_BB[h]])
            bv = BB[:, :, 0:N].rearrange("p h (c t) -> p h c t", t=64)
            lv = LF[:, :, 0:N].rearrange("p h (c t) -> p h c t", t=64)
            smv = sm[:].rearrange("p a (h c) -> p a h c", h=4)
            if N == GN:
                bw = BB[:].rearrange("p h (c t) -> p (h c) t", t=64)
                lw = LF[:].rearrange("p h (c t) -> p (h c) t", t=64)
                tt("dve", lw, bw, bw[:, :, 31:32].to_broadcast([128, 4 * nch, 64]), ALU.subtract, r_BB, r_LF)
                act(sm[:, 0, :], bw[:, :, 31], AF.Exp, r_BB, [r_sm])
                act(sm[:, 1, :], bw[:, :, 63], AF.Exp, r_BB, [r_sm])
                act(sm[:, 2, :], lw[:, :, 63], AF.Exp, r_LF, [r_sm])
                bf = BB[:].rearrange("p h n -> p (h n)")
                lf = LF[:].rearrange("p h n -> p (h n)")
                act(bf, lf, AF.Exp, r_LF, r_BB)
                act(lf, lf, AF.Exp, [], r_LF, scale=-1.0)
                tt("dve", QT[:].rearrange("p h n -> p (h n)"), TQ[:].rearrange("p h n -> p (h n)"), bf, ALU.mult, r_TQ + r_BB, r_QT)
                tt("dve", KT[:].rearrange("p h n -> p (h n)"), TF[:].rearrange("p h n -> p (h n)"), lf, ALU.mult, r_TF + r_LF, r_KT)
            else:
                for h in range(4):
                    tt("dve", lv[:, h], bv[:, h], bv[:, h, :, 31:32].to_broadcast([128, nch, 64]), ALU.subtract,
                       [r_BB[h]], [r_LF[h]])
                for h in range(4):
                    act(smv[:, 0, h, 0:nch], bv[:, h, :, 31], AF.Exp, [r_BB[h]], [r_sm])
                    act(smv[:, 1, h, 0:nch], bv[:, h, :, 63], AF.Exp, [r_BB[h]], [r_sm])
                    act(smv[:, 2, h, 0:nch], lv[:, h, :, 63], AF.Exp, [r_LF[h]], [r_sm])
                for h in range(4):
                    act(BB[:, h, 0:N], LF[:, h, 0:N], AF.Exp, [r_LF[h]], [r_BB[h]])
                    act(LF[:, h, 0:N], LF[:, h, 0:N], AF.Exp, [], [r_LF[h]], scale=-1.0)
                    tt("dve", QT[:, h, 0:N], TQ[:, h, 0:N], BB[:, h, 0:N], ALU.mult, [r_TQ[h], r_BB[h]], [r_QT[h]])
                    tt("dve", KT[:, h, 0:N], TF[:, h, 0:N], LF[:, h, 0:N], ALU.mult, [r_TF[h], r_LF[h]], [r_KT[h]])
            S.stage = "g%d:%s" % (gi, "rgln")
            if FULL:
                rbf = RB[:].rearrange("p c n -> p (c n)")
                rcf = RC[:].rearrange("p c n -> p (c n)")
                act(rbf, rbf, AF.Ln, [], r_RB, scale=-1.0, bias=1.0)
                act(rbf, rbf, AF.Exp, [], r_RB, scale=0.5)
                stt(rcf, rbf, 0.5, rcf, ALU.mult, ALU.mult, r_RB, r_RC)
            for c in range(4):
                if not FULL:
                    act(RB[:, c, 0:N], RB[:, c, 0:N], AF.Ln, [], [r_RB[c]], scale=-1.0, bias=1.0)
                    act(RB[:, c, 0:N], RB[:, c, 0:N], AF.Exp, [], [r_RB[c]], scale=0.5)
                    stt(RC[:, c, 0:N], RB[:, c, 0:N], 0.5, RC[:, c, 0:N], ALU.mult, ALU.mult, [r_RB[c]], [r_RC[c]])
                for (s, c0, L) in segs:
                    scan(RB[:, c, c0:c0 + L], RD[:, c, c0:c0 + L], RC[:, c, c0:c0 + L], hc[:, s, c:c + 1],
                         [r_RD[c], r_RC[c], r_hc[c]], [r_RB[c]])
                    cp("pool", hc[:, s, c:c + 1], RB[:, c, c0 + L - 1:c0 + L], [r_RB[c]], [r_hc[c]])
                if not FULL:
                    stt(mixT[:, 4 + c, 0:N], RB[:, c, 0:N], 0.5, RE[:, c, 0:N], ALU.mult, ALU.mult, [r_RB[c], r_RE[c]], [r_mix[4 + c]])
            if FULL:
                stt(mixT[:, 4:8, :].rearrange("p c n -> p (c n)"), RB[:].rearrange("p c n -> p (c n)"), 0.5,
                    RE[:].rearrange("p c n -> p (c n)"), ALU.mult, ALU.mult, r_RB + r_RE, r_mix[4:8])

            S.stage = "g%d:%s" % (gi, "kT")
            for h in range(4):
                for b in range(nb):
                    tr(PT[:, (h * nb + b) * 128:(h * nb + b + 1) * 128], KT[:, h, b * 128:(b + 1) * 128], [r_KT[h]], [r_T])
            act(kTT[:, :, 0:nb, :], PT[:, 0:4 * nb * 128].rearrange("p (h b n) -> p h b n", h=4, b=nb), AF.Copy, [], r_kTT + [r_T])
            S.stage = "g%d:%s" % (gi, "state")
            rK = r_PB[K_BANK]
            for j in range(nch):
                s = chunk_seq[j]
                b = j // 2
                p0 = (j % 2) * 64
                for h in range(4):
                    mm(PB[K_BANK][:, h * 128:(h + 1) * 128], kTT[p0:p0 + 64, h, b, :], vTM[p0:p0 + 64, b, h * 128:(h + 1) * 128],
                       True, True, [r_kTT[h], r_vTM[b]], [rK])
                for h in range(4):
                    tsc("pool", S0m[:, j, h, :], Sst[:, s, h, :], smv[:, 0, h, j:j + 1], 0.0, ALU.mult, ALU.add,
                        [r_S[s][h], r_sm], [r_S0m[j][h]])
                    tsc("pool", Sst[:, s, h, :], Sst[:, s, h, :], smv[:, 1, h, j:j + 1], 0.0, ALU.mult, ALU.add,
                        [r_sm], [r_S[s][h]])
                    stt(Sst[:, s, h, :], PB[K_BANK][:, h * 128:(h + 1) * 128], smv[:, 2, h, j:j + 1], Sst[:, s, h, :],
                        ALU.mult, ALU.add, [r_sm], [r_S[s][h], rK])
            S.stage = "g%d:%s" % (gi, "scores")
            rS = r_PB[S_BANK]
            for b in range(nb):
                cs = slice(b * 128, (b + 1) * 128)
                for h in range(4):
                    mm(PB[S_BANK][:, h * 128:(h + 1) * 128], KT[:, h, cs], QT[:, h, cs], True, True, [r_KT[h], r_QT[h]], [rS])
                tt("dve", scT[b][:, :, :], PB[S_BANK][:, :].rearrange("p (h t) -> p h t", h=4),
                   mask2[:].unsqueeze(1).to_broadcast([128, 4, 128]), ALU.mult, [r_const], [r_scT[b], rS])
            S.stage = "g%d:%s" % (gi, "o")
            for hp in range(2):
                O_BANK = O_BANKS[hp % len(O_BANKS)]
                rO = r_PB[O_BANK]
                for i in range(2):
                    h = 2 * hp + i
                    for b in range(nb):
                        cs = slice(i * 256 + b * 128, i * 256 + (b + 1) * 128)
                        mm(PB[O_BANK][:, cs], vTM[:, b, h * 128:(h + 1) * 128], scT[b][:, h, :], True, True,
                           [r_vTM[b], r_scT[b]], [rO])
                        for jj in range(2):
                            j = b * 2 + jj
                            mm(PB[O_BANK][:, i * 256 + j * 64:i * 256 + (j + 1) * 64], S0m[:, j, h, :], QT[:, h, j * 64:(j + 1) * 64],
                               False, True, [r_S0m[j][h], r_QT[h]], [rO], skip=True)
                if FULL:
                    act(pairv(BB, hp), PB[O_BANK][:, :], AF.Copy, [], r_BB[2 * hp:2 * hp + 2] + [rO])
                    act(pairv(OSQ, hp), PB[O_BANK][:, :], AF.Square, [], r_OSQ[2 * hp:2 * hp + 2] + [rO])
                for i in range(2):
                    if FULL:
                        break
                    h = 2 * hp + i
                    ps = PB[O_BANK][:, i * 256:i * 256 + N]
                    act(BB[:, h, 0:N], ps, AF.Copy, [], [r_BB[h], rO])
                    act(OSQ[:, h, 0:N], ps, AF.Square, [], [r_OSQ[h], rO])
                a = bankA()
                for i in range(2):
                    h = 2 * hp + i
                    mm(PB[a][:, i * 256:i * 256 + N], ones_bf[:], OSQ[:, h, 0:N], True, True, [r_const, r_OSQ[h]], [r_PB[a]])
                if FULL:
                    rl = r_LF[2 * hp:2 * hp + 2]
                    act(pairv(LF, hp), PB[a][:, :], AF.Ln, [], rl + [r_PB[a]], scale=1.0 / 128.0, bias=4.0 * EPS)
                    act(pairv(LF, hp), pairv(LF, hp), AF.Exp, [], rl, scale=-0.5)
                    tt("dve", pairv(BB, hp), pairv(BB, hp), pairv(LF, hp), ALU.mult, rl, r_BB[2 * hp:2 * hp + 2])
                for i in range(2):
                    h = 2 * hp + i
                    if not FULL:
                        act(LF[:, h, 0:N], PB[a][:, i * 256:i * 256 + N], AF.Ln, [], [r_LF[h], r_PB[a]], scale=1.0 / 128.0, bias=4.0 * EPS)
                        act(LF[:, h, 0:N], LF[:, h, 0:N], AF.Exp, [], [r_LF[h]], scale=-0.5)
                        tt("dve", BB[:, h, 0:N], BB[:, h, 0:N], LF[:, h, 0:N], ALU.mult, [r_LF[h]], [r_BB[h]])
                    stt(mixT[:, h, 0:N], BB[:, h, 0:N], prm[:, P_GC + h:P_GC + h + 1], TO[:, h, 0:N], ALU.mult, ALU.mult,
                        [r_BB[h], r_prm, r_TO[h]], [r_mix[h]])
            if G["last"]:
                for s in sorted(set(chunk_seq)):
                    final_tokens.append(dma("sp", S_out_d[s].rearrange("h d v -> d h v"), Sst[:, s, :, :], r_S[s], [], sem="st_S%d" % s))

            S.stage = "g%d:%s" % (gi, "outproj")
            gb = g1bc[0] if G["tok0"] < SEQ else g1bc[1]
            r_gb = r_g1bc[0] if G["tok0"] < SEQ else r_g1bc[1]
            for b in range(nb):
                for half in range(2):
                    v = bankV()
                    hs = slice(half * 512, (half + 1) * 512)
                    for k in range(KC):
                        mm(PB[v][:, :], mixT[:, k, b * 128:(b + 1) * 128], wo[:, k, hs], k == 0, k == KC - 1,
                           [r_mix[k], r_wo[half]], [r_PB[v]])
                    tt("dve", tmp1[:, :], PB[v][:, :], gb[:, hs], ALU.mult, [r_gb], [r_tmp1, r_PB[v]])
                    tt(_os.environ.get("K_RESENG", "pool"), xt[:, b, hs], xt[:, b, hs], tmp1[:, :], ALU.add, [r_tmp1], [r_xt])
            dma("sp", x1_d[G["tok0"]:G["tok0"] + N, :].rearrange("(b p) d -> p b d", p=128), xt[:, 0:nb, :],
                [r_xt], [r_x1d[gi]], sem="x1s%d" % buf)

        r_x1d = [Res("x1d%d" % g) for g in range(NGRP)]

        for pi in range(2 * PPW, 3 * PPW):
            ada_piece(pi)
        bcast_g(g1bc, r_g1bc)

        for gi in range(NGRP):
            phase1(gi)
            if gi == 0:
                for pi in range(3 * PPW, 6 * PPW):
                    ada_piece(pi)
                bcast_g(g2bc, r_g2bc)
                dma("sp", fgb[:], fgain_d.partition_broadcast(128), [], [r_fgb])
                tsc("dve", fgb[:], fgb[:], 32.0, None, ALU.mult, None, [r_fgb], [r_fgb])

        final_tokens.append(dma("sp", h_out_d, hc[:], r_hc, [], sem="st_h"))
        final_tokens.append(dma("sp", cb_out_d, cbo[:], [r_cbo], [], sem="st_cb"))

        r_wu = [Res("wu%d" % i) for i in range(8)]
        r_wd = [Res("wd%d" % i) for i in range(8)]
        r_hT = [Res("hT%d" % f) for f in range(32)]
        r_RL = [Res("RL%d" % i) for i in range(3)]
        r_tmp2 = [Res("tmp2_%d" % i) for i in range(2)]
        w_up_v = w_up_d.rearrange("(k p) n -> p k n", p=128)
        w_down_v = w_down_d.rearrange("(f p) n -> p f n", p=128)
        for i in range(4):
            dma("pool", wu[:, :, i * 1024:(i + 1) * 1024], w_up_v[:, :, i * 1024:(i + 1) * 1024], [], [r_wu[2 * i], r_wu[2 * i + 1]],
                sem="wu%d" % i, after=resB)
        for i in range(8):
            dma("pool", wd[:, i * 4:(i + 1) * 4, :], w_down_v[:, i * 4:(i + 1) * 4, :], [], [r_wd[i]], sem="wd%d" % i,
                after=(r_wi if i < 6 else r_wo))

        def load_x1(gi):
            G = groups[gi]
            nb = G["N"] // 128
            buf = (NGRP + gi) % NXT
            dma("sp", XT[buf][:, 0:nb, :], x1_d[G["tok0"]:G["tok0"] + G["N"], :].rearrange("(b p) d -> p b d", p=128),
                [r_x1d[gi]], [r_XT[buf]], sem=xsems[buf])

        load_x1(0)

        def phase2(gi):
            G = groups[gi]
            N = G["N"]
            nb = N // 128
            buf = (NGRP + gi) % NXT
            xt = XT[buf]
            r_xt = r_XT[buf]
            if gi + 1 < NGRP:
                load_x1(gi + 1)
            hb, r_hb = (hnT, r_hnT) if gi % 2 == 0 else (hnT2, r_hnT2)
            norm_and_transpose(gi, xt, r_xt, 2, 3, 4, hb, r_hb, after=(resB if gi == 1 else ()))
            rwu = lambda col: r_wu[col // 512]
            for fp in range(16):
                a = proj_pair([(2 * fp + i) * 128 for i in range(2)], N, wu, rwu, hb, r_hb)
                rl = nxt("RL", 3)
                if N == GN:
                    act(RL[rl][:, :], PB[a][:, :], AF.Relu, [], [r_RL[rl], r_PB[a]])
                    tt("pool", hT[:, 2 * fp:2 * fp + 2, :], RL[rl][:, :].rearrange("p (f n) -> p f n", f=2),
                       RL[rl][:, :].rearrange("p (f n) -> p f n", f=2), ALU.mult, [r_RL[rl]], [r_hT[2 * fp], r_hT[2 * fp + 1]])
                else:
                    for i in range(2):
                        act(RL[rl][:, i * 256:i * 256 + N], PB[a][:, i * 256:i * 256 + N], AF.Relu, [], [r_RL[rl], r_PB[a]])
                        tt("pool", hT[:, 2 * fp + i, 0:N], RL[rl][:, i * 256:i * 256 + N], RL[rl][:, i * 256:i * 256 + N],
                           ALU.mult, [r_RL[rl]], [r_hT[2 * fp + i]])
            gb = g2bc[0] if G["tok0"] < SEQ else g2bc[1]
            r_gb = r_g2bc[0] if G["tok0"] < SEQ else r_g2bc[1]
            for b in range(nb):
                for half in range(2):
                    v = bankV()
                    hs = slice(half * 512, (half + 1) * 512)
                    for fc in range(32):
                        mm(PB[v][:, :], hT[:, fc, b * 128:(b + 1) * 128], wd[:, fc, hs], fc == 0, fc == 31,
                           [r_hT[fc], r_wd[fc // 4]], [r_PB[v]])
                    t = nxt("tmp", 2)
                    tt("dve", tmp2[t][:, :], PB[v][:, :], gb[:, hs], ALU.mult, [r_gb], [r_tmp2[t], r_PB[v]])
                    tt("pool", xt[:, b, hs], xt[:, b, hs], tmp2[t][:, :], ALU.add, [r_tmp2[t]], [r_xt])
                ss = stat[:, 8 + b:9 + b]
                rs = stat[:, 10 + b:11 + b]
                jk = nxt("tmp", 2)
                act(tmp2[jk][:, :].bitcast(BF16), xt[:, b, :], AF.Square, [r_xt], [r_tmp2[jk], r_stat[4 + b]], accum=ss)
                tsc("pool", rs, ss, 1024.0 * EPS, 0.0, ALU.add, ALU.add, [], [r_stat[4 + b]])
                tt("pool", rs, rs, nhalf[:], ALU.pow, [r_const], [r_stat[4 + b]])
                stt(xt[:, b, :], xt[:, b, :], rs, fgb[:], ALU.mult, ALU.mult, [r_stat[4 + b], r_fgb], [r_xt])
            tok = dma("sp", y_d[G["tok0"]:G["tok0"] + N, :].rearrange("(b p) d -> p b d", p=128), xt[:, 0:nb, :],
                      [r_xt], [], sem="ys%d" % buf)
            final_tokens.append(tok)

        for gi in range(NGRP):
            phase2(gi)

        if _os.environ.get("K_FILL", "1") == "1":
            t_end_p1 = 0.0
            fsrc = RM[:, 0:256]
            fdst = PB[5 if OSK else 2][:, 0:256]
            S.filler = (lambda e: e.matmul(fdst, lhsT=ident[:], rhs=fsrc, start=True, stop=True), r_const.w, 0.22,
                        float(_os.environ.get("K_FILL_LO", "150")), float(_os.environ.get("K_FILL_HI", "1250")),
                        float(_os.environ.get("K_FILL_GMIN", "1.0")))
        S.schedule()
        build_nc.last_sched = S
        S.emit(st, final_tokens=final_tokens)
    return nc


_NC_CACHE = {}


def _get_nc():
    if "nc" not in _NC_CACHE:
        _NC_CACHE["nc"] = build_nc()
    return _NC_CACHE["nc"]


def kernel(x_prompt, x_sample, c_prompt, c_sample, state_hgrn, state_rglru, cache_conv,
           hg_lb_logits, w_ada, b_ada, w_in, hg_norm_gain, conv_w, conv_b,
           rg_wa, rg_ba, rg_wx, rg_bx, rg_lambda, w_out, w_up, w_down, final_gain):
    f = lambda a: np.ascontiguousarray(np.asarray(a, dtype=np.float32))
    x_prompt, x_sample, c_prompt, c_sample = f(x_prompt), f(x_sample), f(c_prompt), f(c_sample)
    state_hgrn, state_rglru, cache_conv = f(state_hgrn), f(state_rglru), f(cache_conv)
    n = 8

    def chan(v, nchunk):
        return np.asarray(v, np.float32).reshape(nchunk, 128).T

    pp = np.zeros((128, NPP), np.float32)
    pp[:, 0:4] = chan(hg_norm_gain[0], 4)
    cw = np.asarray(conv_w[0], np.float32)
    for c in range(4):
        for j in range(4):
            pp[:, 4 + c * 4 + j] = cw[j, c * 128:(c + 1) * 128]
    pp[:, 20:24] = chan(conv_b[0], 4)
    pp[:, 24:28] = chan(rg_ba[0], 4)
    pp[:, 28:32] = chan(rg_bx[0], 4)
    pp[:, 32:36] = chan(rg_lambda[0], 4)
    lbl = f(np.asarray(hg_lb_logits, np.float32).reshape(2, 4, 128).transpose(2, 0, 1))
    shared = {
        "lbl": lbl, "w_ada": f(w_ada[0]), "b_ada": f(np.asarray(b_ada[0]).reshape(-1)), "b48": f(np.asarray(b_ada[0], np.float32).reshape(48, 128).T), "w_in": f(w_in[0]),
        "pp": pp, "rg_wa": f(rg_wa[0]), "rg_wx": f(rg_wx[0]), "w_out": f(w_out[0]), "w_up": f(w_up[0]),
        "w_down": f(w_down[0]), "fgain": f(final_gain),
    }
    in_maps = []
    for i in range(n):
        xs_i = np.concatenate([x_prompt[i], x_sample[2 * i], x_sample[2 * i + 1]], axis=0)
        cs = np.stack([c_prompt[i], c_sample[2 * i], c_sample[2 * i + 1]], axis=0)
        cT = f(cs.reshape(3, KC, 128).transpose(2, 1, 0))
        s_hg = f(state_hgrn[0, 2 * i:2 * i + 2])
        h0 = f(state_rglru[0, 2 * i:2 * i + 2].reshape(2, 4, 128).transpose(2, 0, 1))
        cc0 = f(cache_conv[0, 2 * i:2 * i + 2].reshape(2, 3, 4, 128).transpose(3, 0, 2, 1))
        m = dict(shared)
        m.update({"xs": f(xs_i), "cT": cT, "s_hg": s_hg, "h0": h0, "cc0": cc0})
        in_maps.append(m)
    nc = _get_nc()
    res = run_bass_kernel_spmd(nc, in_maps, core_ids=list(range(n)))
    R = res.results
    y_prompt = np.stack([R[i]["y"][0:SEQ] for i in range(n)], axis=0)
    y_sample = np.stack([R[i]["y"][SEQ + 64 * j:SEQ + 64 * (j + 1)] for i in range(n) for j in range(2)], axis=0)
    S_p = np.stack([R[i]["S_out"][0] for i in range(n)], axis=0)[None]
    S_s = np.stack([R[i]["S_out"][1 + j] for i in range(n) for j in range(2)], axis=0)[None]

    def hvec(a):
        return a.T.reshape(512)

    h_p = np.stack([hvec(R[i]["h_out"][:, 0, :]) for i in range(n)], axis=0)[None]
    h_s = np.stack([hvec(R[i]["h_out"][:, 1 + j, :]) for i in range(n) for j in range(2)], axis=0)[None]

    def cbm(a):
        return a.transpose(2, 1, 0).reshape(3, 512)

    cb_p = np.stack([cbm(R[i]["cb_out"][:, 0]) for i in range(n)], axis=0)[None]
    cb_s = np.stack([cbm(R[i]["cb_out"][:, 1 + j]) for i in range(n) for j in range(2)], axis=0)[None]
    outs = (y_prompt, y_sample, S_p, h_p, cb_p, S_s, h_s, cb_s)
    return tuple(np.ascontiguousarray(o, dtype=np.float32) for o in outs)
```
